# Optimizing a Trainium2 kernel written in Bass

```python
import jax, jax.numpy as jnp
from jax import lax
import numpy as np

D_MODEL = 1024
BATCH = 4
SEQ = 4096
DEPTH = 4

N_Q_A = 8
N_KV_A = 2
HEAD_DIM_A = 64
W_A = N_Q_A * HEAD_DIM_A
W_KV_A = N_KV_A * HEAD_DIM_A
WINDOW = 128
BLOCK = 128
N_HEADS_B = 4
HEAD_DIM_B = 128
W_B = N_HEADS_B * HEAD_DIM_B
N_HEADS_C = 4
DK_C = 128
DV_C = 256
WK_C = N_HEADS_C * DK_C
WV_C = N_HEADS_C * DV_C
GATE_RANK = 16
GATE_TEMP = 16.0
N_MEM = 256
N_HEADS_M = 4
HEAD_DIM_M = 128
W_M = N_HEADS_M * HEAD_DIM_M

CHUNK = 16
EPS = 1e-6
MASK_VALUE = -1e30
MIN_GATE = 1e-30
N_EVEN = (DEPTH + 1) // 2
N_ODD = DEPTH // 2
EVEN_SIZES = (W_A, W_KV_A, W_KV_A, W_A, W_B, W_B, W_B, W_B, W_B, W_M, W_M)
ODD_SIZES = (WK_C, WK_C, WV_C, WV_C, GATE_RANK, GATE_RANK, W_M, W_M)
EVEN_IN = sum(EVEN_SIZES)
ODD_IN = sum(ODD_SIZES)
MIX_EVEN = W_A + W_B + W_M
MIX_ODD = WV_C + W_M

kernel_name = "hybrid_bidir_swa_hgrn2_gla_mem"


def rmsnorm(x, g):
    xf = x.astype(jnp.float32)
    y = xf * lax.rsqrt(jnp.mean(xf * xf, axis=-1, keepdims=True) + EPS)
    return (y * g.astype(jnp.float32)).astype(x.dtype)


def split_cols(t, sizes):
    return jnp.split(t, [int(s) for s in np.cumsum(sizes)[:-1]], axis=-1)


def split_heads(t, n_heads):
    B, T, W = t.shape
    return t.reshape(B, T, n_heads, W // n_heads).transpose(0, 2, 1, 3)


def merge_heads(t):
    B, H, T, d = t.shape
    return t.transpose(0, 2, 1, 3).reshape(B, T, H * d)


def group_rmsnorm(o, g, n_heads):
    B, T, W = o.shape
    y = rmsnorm(o.reshape(B, T, n_heads, W // n_heads), g.reshape(n_heads, W // n_heads))
    return y.reshape(B, T, W)


def alibi_slopes(n):
    return 2.0 ** (-8.0 * jnp.arange(1, n + 1, dtype=jnp.float32) / n)


def window_attention(q, k, v, sink):
    f32 = jnp.float32
    B, Hq, T, d = q.shape
    Hkv = k.shape[1]
    G = Hq // Hkv
    nb = T // BLOCK

    def key_blocks(t):
        tp = jnp.pad(t.astype(f32), ((0, 0), (0, 0), (BLOCK, BLOCK), (0, 0)))
        tp = tp.reshape(B, Hkv, nb + 2, BLOCK, d)
        return jnp.concatenate([tp[:, :, :-2], tp[:, :, 1:-1], tp[:, :, 2:]], axis=3)

    kb, vb = key_blocks(k), key_blocks(v)
    qb = q.astype(f32).reshape(B, Hkv, G, nb, BLOCK, d)
    s = jnp.einsum('bngcid,bncjd->bngcij', qb, kb) * (d ** -0.5)
    i = jnp.arange(BLOCK)[:, None]
    j = jnp.arange(3 * BLOCK)[None, :]
    dist = jnp.abs(i - j + BLOCK).astype(f32)
    kpos = (jnp.arange(nb)[:, None, None] - 1) * BLOCK + j[None]
    valid = (dist <= WINDOW)[None] & (kpos >= 0) & (kpos < T)
    slopes = alibi_slopes(Hq).reshape(Hkv, G, 1, 1, 1)
    s = jnp.where(valid, s - slopes * dist, MASK_VALUE)
    sk = sink.astype(f32).reshape(Hkv, G, 1, 1, 1)
    m = jnp.maximum(jnp.max(s, axis=-1, keepdims=True), sk)
    p = jnp.where(valid, jnp.exp(s - m), 0.0)
    denom = jnp.sum(p, axis=-1, keepdims=True) + jnp.exp(sk - m)
    o = jnp.einsum('bngcij,bncjd->bngcid', p, vb) / denom
    return o.reshape(B, Hq, T, d)


def chunked_gated_scan(q, k, v, log_f):
    f32 = jnp.float32
    B, H, T, dk = q.shape
    dv = v.shape[-1]
    n = T // CHUNK

    def to_chunks(t):
        return t.astype(f32).reshape(B, H, n, CHUNK, t.shape[-1]).transpose(2, 0, 1, 3, 4)

    qc, kc, vc, gc = to_chunks(q), to_chunks(k), to_chunks(v), to_chunks(log_f)
    lower = jnp.tril(jnp.ones((CHUNK, CHUNK), dtype=bool))[:, :, None]

    def step(S, inp):
        qi, ki, vi, gi = inp
        b = jnp.cumsum(gi, axis=-2)
        b_last = b[:, :, -1:, :]
        o_inter = jnp.einsum('bhtk,bhkv->bhtv', qi * jnp.exp(b), S)
        diff = b[:, :, :, None, :] - b[:, :, None, :, :]
        decay = jnp.where(lower, jnp.exp(jnp.where(lower, diff, 0.0)), 0.0)
        A = jnp.einsum('bhtk,bhtsk,bhsk->bhts', qi, decay, ki)
        o_intra = jnp.einsum('bhts,bhsv->bhtv', A, vi)
        S_new = jnp.exp(b_last)[:, :, 0, :, None] * S + jnp.einsum(
            'bhsk,bhsv->bhkv', ki * jnp.exp(b_last - b), vi)
        return S_new, o_inter + o_intra

    S0 = jnp.zeros((B, H, dk, dv), f32)
    _, oc = lax.scan(step, S0, (qc, kc, vc, gc))
    return oc.transpose(1, 2, 0, 3, 4).reshape(B, H, T, dv)


def bidir_scan(q, k_fwd, k_bwd, v, lf_fwd, lf_bwd):
    flip = lambda t: jnp.flip(t, axis=2)
    fwd = chunked_gated_scan(q, k_fwd, v, lf_fwd)
    bwd = flip(chunked_gated_scan(flip(q), flip(k_bwd), flip(v), flip(lf_bwd)))
    return fwd + bwd


def hgrn_forget(z, lb):
    zf = z.astype(jnp.float32)
    f = lb + (1.0 - lb) * jax.nn.sigmoid(zf)
    log_f = jnp.log(jnp.maximum(f, MIN_GATE))
    k = (1.0 - lb) * jax.nn.sigmoid(-zf)
    return log_f, k


def memory_attention(q, mem_n, w_kv):
    f32 = jnp.float32
    k, v = jnp.split(mem_n @ w_kv, 2, axis=-1)
    qh = split_heads(q, N_HEADS_M).astype(f32)
    kh = split_heads(k, N_HEADS_M).astype(f32)
    vh = split_heads(v, N_HEADS_M).astype(f32)
    p = jax.nn.softmax(jnp.einsum('bhtd,bhsd->bhts', qh, kh) * (HEAD_DIM_M ** -0.5), axis=-1)
    return merge_heads(jnp.einsum('bhts,bhsd->bhtd', p, vh)).astype(q.dtype)


def even_layer(x, g_norm, w_in, sink, lb, hgrn_g, w_out, mem_n, w_kv):
    h = rmsnorm(x, g_norm)
    qA, kA, vA, gA, qB, zBf, zBb, iB, gB, qM, gM = split_cols(h @ w_in, EVEN_SIZES)
    a = window_attention(split_heads(qA, N_Q_A), split_heads(kA, N_KV_A),
                         split_heads(vA, N_KV_A), sink)
    a = merge_heads(a).astype(x.dtype) * jax.nn.silu(gA)
    lf_f, k_f = hgrn_forget(zBf, lb[0])
    lf_b, k_b = hgrn_forget(zBb, lb[1])
    sh = lambda t: split_heads(t, N_HEADS_B)
    o = bidir_scan(sh(jax.nn.silu(qB)), sh(k_f), sh(k_b), sh(iB), sh(lf_f), sh(lf_b))
    o = group_rmsnorm(merge_heads(o).astype(x.dtype), hgrn_g, N_HEADS_B) * jax.nn.silu(gB)
    mo = memory_attention(qM, mem_n, w_kv) * jax.nn.silu(gM)
    return jnp.concatenate([a, o, mo], axis=-1) @ w_out


def odd_layer(x, g_norm, w_in, w_up, b_gate, gla_g, w_out, mem_n, w_kv):
    h = rmsnorm(x, g_norm)
    qC, kC, vC, gC, rf, rb, qM, gM = split_cols(h @ w_in, ODD_SIZES)
    lf_f = jax.nn.log_sigmoid((rf @ w_up[0] + b_gate[0]).astype(jnp.float32)) / GATE_TEMP
    lf_b = jax.nn.log_sigmoid((rb @ w_up[1] + b_gate[1]).astype(jnp.float32)) / GATE_TEMP
    sh = lambda t: split_heads(t, N_HEADS_C)
    kh = sh(kC)
    o = bidir_scan(sh(qC * (DK_C ** -0.5)), kh, kh, sh(vC), sh(lf_f), sh(lf_b))
    o = group_rmsnorm(merge_heads(o).astype(x.dtype), gla_g, N_HEADS_C) * jax.nn.silu(gC)
    mo = memory_attention(qM, mem_n, w_kv) * jax.nn.silu(gM)
    return jnp.concatenate([o, mo], axis=-1) @ w_out


def setup_inputs(seed: int = 0) -> dict:
    key = jax.random.key(seed)
    ks = jax.random.split(key, 18)
    nrm = lambda k, shape, scale: jax.random.normal(k, shape, jnp.float32) * scale
    return {
        "x": nrm(ks[0], (BATCH, SEQ, D_MODEL), 1.0),
        "mem": nrm(ks[1], (BATCH, N_MEM, D_MODEL), 1.0),
        "norm_even": 1.0 + nrm(ks[2], (N_EVEN, D_MODEL), 0.02),
        "w_in_even": nrm(ks[3], (N_EVEN, D_MODEL, EVEN_IN), D_MODEL ** -0.5),
        "sink": nrm(ks[4], (N_EVEN, N_Q_A), 0.5),
        "lb_param": nrm(ks[5], (N_EVEN, 2, W_B), 0.5),
        "hgrn_norm": 1.0 + nrm(ks[6], (N_EVEN, W_B), 0.02),
        "w_out_even": nrm(ks[7], (N_EVEN, MIX_EVEN, D_MODEL), MIX_EVEN ** -0.5),
        "norm_odd": 1.0 + nrm(ks[8], (N_ODD, D_MODEL), 0.02),
        "w_in_odd": nrm(ks[9], (N_ODD, D_MODEL, ODD_IN), D_MODEL ** -0.5),
        "w_gate_up": nrm(ks[10], (N_ODD, 2, GATE_RANK, WK_C), GATE_RANK ** -0.5),
        "b_gate": nrm(ks[11], (N_ODD, 2, WK_C), 0.1),
        "gla_norm": 1.0 + nrm(ks[12], (N_ODD, WV_C), 0.02),
        "w_out_odd": nrm(ks[13], (N_ODD, MIX_ODD, D_MODEL), MIX_ODD ** -0.5),
        "mem_norm": 1.0 + nrm(ks[14], (D_MODEL,), 0.02),
        "w_mem_kv": nrm(ks[15], (DEPTH, D_MODEL, 2 * W_M), D_MODEL ** -0.5),
        "final_norm": 1.0 + nrm(ks[16], (D_MODEL,), 0.02),
    }


def reference(x, mem, norm_even, w_in_even, sink, lb_param, hgrn_norm, w_out_even,
              norm_odd, w_in_odd, w_gate_up, b_gate, gla_norm, w_out_odd,
              mem_norm, w_mem_kv, final_norm):
    mem_n = rmsnorm(mem, mem_norm)
    lbs = jax.nn.softmax(lb_param.astype(jnp.float32), axis=0)
    lower = jnp.cumsum(lbs, axis=0) - lbs[0]
    for l in range(DEPTH):
        i = l // 2
        if l % 2 == 0:
            x = x + even_layer(x, norm_even[i], w_in_even[i], sink[i], lower[i],
                               hgrn_norm[i], w_out_even[i], mem_n, w_mem_kv[l])
        else:
            x = x + odd_layer(x, norm_odd[i], w_in_odd[i], w_gate_up[i], b_gate[i],
                              gla_norm[i], w_out_odd[i], mem_n, w_mem_kv[l])
    return rmsnorm(x, final_norm)
```

```python
import contextlib
import numpy as np
import concourse.bass as bass
import concourse.mybir as mybir
from concourse.bass_utils import run_bass_kernel_spmd

F32 = mybir.dt.float32
BF16 = mybir.dt.bfloat16
ALU = mybir.AluOpType
AF = mybir.ActivationFunctionType
DT_SIZE = {F32: 4, BF16: 2}

T = 4096
D = 1024
NBLK = T // 512
NTILE = T // 128
EPS = 1e-6
N_CORES = 8


class Buf:
    def __init__(self, name):
        self.name = name
        self.lw = None
        self.rd = {}
        self.dsem = None
        self.tt = {}

    def __getattr__(self, k):
        t = self.__dict__.get('tt', {})
        if k in t:
            return t[k]
        raise AttributeError(k)


class Prog:
    COMPUTE = ('pe', 'act', 'dve', 'pool')
    ENG = ('pe', 'act', 'dve', 'pool', 'sp')

    def __init__(self, nc, es, n_dsem=88, sb_limit=229000, sb_start=16640):
        self.nc = nc
        self.semh = {}
        for e in self.COMPUTE:
            self.semh[e] = es.enter_context(nc.semaphore('c_' + e))
        self.free_dsem = []
        for i in range(n_dsem):
            k = ('d', i)
            self.semh[k] = es.enter_context(nc.semaphore('d%d' % i))
            self.free_dsem.append(k)
        self.cnt = {k: 0 for k in self.semh}
        self.streams = {e: [] for e in self.ENG}
        self.waited = {e: {} for e in self.ENG}
        self.sb_off = sb_start
        self.sb_base = sb_start
        self.sb_limit = sb_limit
        self.uid = 0
        self.phase_dsems = []
        self.nops = 0

    def sbuf(self, shape, dtype, name='t'):
        self.uid += 1
        per_part = int(np.prod(shape[1:])) * DT_SIZE[dtype]
        off = (self.sb_off + 63) // 64 * 64
        assert off + per_part <= self.sb_limit, ('SBUF overflow', name, off, per_part)
        h = self.nc.alloc_sbuf_tensor_at('%s_%d' % (name, self.uid), list(shape), dtype, offset=off)
        self.sb_off = off + per_part
        return h

    def buf(self, name, **tensors):
        b = Buf(name)
        for k, (shape, dtype) in tensors.items():
            b.tt[k] = self.sbuf(shape, dtype, name + '_' + k)
        return b

    def ring(self, n, name, **tensors):
        return Ring([self.buf('%s%d' % (name, i), **tensors) for i in range(n)])

    def wrap(self, name, **handles):
        b = Buf(name)
        b.tt.update(handles)
        return b

    def need_dsem(self, b):
        if b.dsem is None:
            b.dsem = self.free_dsem.pop()
            self.phase_dsems.append(b.dsem)
        return b.dsem

    def persist(self):
        self.sb_base = self.sb_off
        self.phase_dsems = []

    def _deps(self, eng, reads, writes, is_dma):
        deps = {}

        def need(tok):
            if tok is None:
                return
            k, v = tok
            if deps.get(k, 0) < v:
                deps[k] = v
        for b in reads:
            need(b.lw)
        for b in writes:
            if b.lw is not None and (is_dma or b.lw[0] != eng):
                need(b.lw)
            for k, v in b.rd.items():
                if is_dma or k != eng:
                    need((k, v))
        w = self.waited[eng]
        out = []
        for k, v in deps.items():
            if w.get(k, 0) < v:
                w[k] = v
                out.append((k, v))
        return out

    def _commit(self, tok, reads, writes):
        for b in writes:
            b.lw = tok
            b.rd = {}
        k, v = tok
        for b in reads:
            if b in writes:
                continue
            if b.rd.get(k, 0) < v:
                b.rd[k] = v

    def op(self, eng, fn, reads=(), writes=()):
        waits = self._deps(eng, reads, writes, False)
        self.cnt[eng] += 1
        tok = (eng, self.cnt[eng])
        self.streams[eng].append((waits, fn, eng, 1))
        self._commit(tok, reads, writes)
        self.nops += 1

    def dma(self, queue, pairs, reads=(), writes=()):
        onchip = list(writes) + list(reads)
        key = self.need_dsem(onchip[0])
        for b in onchip[1:]:
            assert b.dsem is None or b.dsem == key
            b.dsem = key
        waits = self._deps(queue, reads, writes, True)
        first = True
        for (o, i) in pairs:
            self.cnt[key] += 16
            self.streams[queue].append((waits if first else [],
                                        (lambda e, o=o, i=i: e.dma_start(out=o, in_=i)), key, 16))
            first = False
            self.nops += 1
        self._commit((key, self.cnt[key]), reads, writes)

    def barrier(self):
        for e in self.ENG:
            waits = []
            w = self.waited[e]
            for k, v in self.cnt.items():
                if v > 0 and w.get(k, 0) < v and k != e:
                    w[k] = v
                    waits.append((k, v))
            if waits:
                self.streams[e].append((waits, None, None, 0))

    def end_phase(self, keep=None):
        self.barrier()
        self.free_dsem.extend(self.phase_dsems)
        self.phase_dsems = []
        self.sb_off = self.sb_base if keep is None else keep

    def emit(self):
        nc = self.nc
        semh = self.semh
        streams = self.streams
        with nc.Block() as block:
            def body(name):
                def f(e):
                    for (waits, fn, key, inc) in streams[name]:
                        for (k, v) in waits:
                            e.wait_ge(semh[k], v)
                        if fn is not None:
                            fn(e).then_inc(semh[key], inc)
                return f
            block.tensor(body('pe'))
            block.scalar(body('act'))
            block.vector(body('dve'))
            block.gpsimd(body('pool'))
            block.sync(body('sp'))


class Ring:
    def __init__(self, bufs):
        self.bufs = bufs
        self.i = 0

    def next(self):
        b = self.bufs[self.i % len(self.bufs)]
        self.i += 1
        return b


def mm(P, ob, o, lb, l, rb, r, start=True, stop=True):
    P.op('pe', lambda e: e.matmul(o, lhsT=l, rhs=r, start=start, stop=stop), reads=[lb, rb], writes=[ob])


def act(P, ob, o, ib, i, func, scale=1.0, bias=0.0, extra=()):
    P.op('act', lambda e: e.activation(out=o, in_=i, func=func, scale=scale, bias=bias),
         reads=[ib] + list(extra), writes=[ob])


def tt(P, eng, ob, o, ab, a, bb, b, op):
    P.op(eng, lambda e: e.tensor_tensor(out=o, in0=a, in1=b, op=op), reads=[ab, bb], writes=[ob])


def ts(P, eng, ob, o, ab, a, s1, s2, op0, op1=None, extra=()):
    if op1 is None:
        P.op(eng, lambda e: e.tensor_scalar(out=o, in0=a, scalar1=s1, scalar2=None, op0=op0),
             reads=[ab] + list(extra), writes=[ob])
    else:
        P.op(eng, lambda e: e.tensor_scalar(out=o, in0=a, scalar1=s1, scalar2=s2, op0=op0, op1=op1),
             reads=[ab] + list(extra), writes=[ob])


def stt(P, eng, ob, o, ab, a, scalar, bb, b, op0, op1, extra=()):
    P.op(eng, lambda e: e.scalar_tensor_tensor(out=o, in0=a, scalar=scalar, in1=b, op0=op0, op1=op1),
         reads=[ab, bb] + list(extra), writes=[ob])


def cp(P, eng, ob, o, ib, i):
    P.op(eng, lambda e: e.tensor_copy(out=o, in_=i), reads=[ib], writes=[ob])


def mset(P, eng, ob, o, val):
    P.op(eng, lambda e: e.memset(o, val), writes=[ob])


class Cfg:
    def __init__(self, nA, nB, nM, nC, n_layers=4, final_norm=True, debug=False, stop_after=None):
        self.nA, self.nB, self.nM, self.nC = nA, nB, nM, nC
        self.debug = debug
        self.stop_after = stop_after
        self.nKV = nA // 4
        self.n_layers = n_layers
        self.final_norm = final_norm
        g = []
        o = 0

        def add(name, n):
            nonlocal o
            g.append((name, o, n))
            o += n
        add('qA', nA * 64); add('kA', self.nKV * 64); add('gA', nA * 64)
        add('qB', nB * 128); add('gB', nB * 128); add('qM', nM * 128); add('gM', nM * 128)
        add('zf', nB * 128); add('zb', nB * 128)
        add('vA', self.nKV * 64); add('iB', nB * 128)
        self.ge = {n: (s, c) for n, s, c in g}
        self.nce = o
        g = []
        o = 0
        add('qC', nC * 128); add('kC', nC * 128); add('gC', nC * 256); add('qM', nM * 128); add('gM', nM * 128)
        add('rf', 16); add('rb', 16); add('vC', nC * 256)
        self.go = {n: (s, c) for n, s, c in g}
        self.nco = o
        self.mix_e = nA * 64 + nB * 128 + nM * 128
        self.mix_o = nC * 256 + nM * 128
        self.mixr = max(self.mix_e, self.mix_o)
        assert self.mix_e == self.mix_o
        c = {}
        o = 0
        for nm, n in (('norm', 32), ('fin', 8), ('mem', 8), ('hg', 2 * nB), ('gl', 2 * nC * 2),
                      ('lbp', 4 * nB), ('bg', 4 * nC), ('sink', 2 * nA)):
            c[nm] = o
            o += n
        self.cc = c
        self.ncc = o


def alibi_table(heads):
    s = np.arange(128)[:, None, None, None]
    j = np.arange(3)[None, :, None, None]
    t = np.arange(128)[None, None, None, :]
    dist = np.abs(t - s - (j - 1) * 128).astype(np.float64)
    slopes = np.array([2.0 ** (-8.0 * (h + 1) / 8) for h in heads])[None, None, :, None]
    e = np.exp(-slopes * dist) * (dist <= 128)
    return e.astype(np.float32)


def scan_masks():
    s = np.arange(128)[:, None]
    t = np.arange(128)[None, :]
    m = []
    for C in (64, 128):
        same = (s // C) == (t // C)
        m.append(((s <= t) & same).astype(np.float32))
        m.append(((s >= t) & same).astype(np.float32))
    m.append(np.eye(128, dtype=np.float32))
    p = np.arange(128)[:, None]
    m.append((p < 64).astype(np.float32))
    m.append((p >= 64).astype(np.float32))
    return np.concatenate(m, axis=1)


def build(cfg):
    nc = bass.Bass("TRN2", target_bir_lowering=False)
    nA, nB, nM, nC, nKV = cfg.nA, cfg.nB, cfg.nM, cfg.nC, cfg.nKV
    MIXR = cfg.mixr
    NKM = MIXR // 128

    def din(name, shape, dt=F32):
        return nc.dram_tensor(name, list(shape), dt, kind="ExternalInput").ap()

    def dscr(name, shape, dt):
        dbg = cfg.debug and name in ('mixT',)
        return nc.dram_tensor(name, list(shape), dt, kind="ExternalOutput" if dbg else "Internal").ap()

    xT_in = din('xT', [D, T])
    memT_in = din('memT', [D, 256])
    consts_in = din('consts', [128, cfg.ncc])
    etab_in = din('etab', [128, 3 * nA * 128])
    masks_in = din('masks', [128, 642])
    w_in_e = din('w_in_e', [2, D, cfg.nce])
    w_in_o = din('w_in_o', [2, D, cfg.nco])
    w_out_in = din('w_out', [4, MIXR, D])
    w_kv_in = din('w_kv', [4, D, 2 * nM * 128])
    w_up_in = din('w_up', [16, 2 * 2 * nC * 128])
    yT = nc.dram_tensor('yT', [D, T], F32, kind="ExternalOutput").ap()

    xs = [dscr('xs0', [D, T], F32), dscr('xs1', [D, T], F32)]
    mixT = dscr('mixT', [MIXR, T], BF16)
    s_qA = dscr('s_qA', [nA * 64, T], BF16)
    s_kA = dscr('s_kA', [nKV * 64, T], BF16)
    s_gA = dscr('s_gA', [nA * 64, T], BF16)
    NSH = max(nB, nC)
    s_q = dscr('s_q', [NSH * 128, T], BF16)
    s_k = dscr('s_k', [2, NSH * 128, T], BF16)
    s_lf = dscr('s_lf', [2, NSH * 128, T], F32)
    s_g = dscr('s_g', [max(nB * 128, nC * 256), T], BF16)
    s_qM = dscr('s_qM', [nM * 128, T], BF16)
    s_gM = dscr('s_gM', [nM * 128, T], BF16)
    s_vA = dscr('s_vA', [T, nKV * 64], BF16)
    s_v = dscr('s_v', [T, max(nB * 128, nC * 256)], BF16)

    with contextlib.ExitStack() as es:
        P = Prog(nc, es)
        psb = [P.wrap('ps%d' % i, p=nc.alloc_psum_tensor('ps%d' % i, [128, 512], F32)) for i in range(7)]
        pst = P.wrap('pst', p=nc.alloc_psum_tensor('pst', [128, 1024], BF16))

        CO = P.buf('CO', c=([128, cfg.ncc], F32), lb=([128, 4 * nB], F32), oml=([128, 4 * nB], F32),
                   noml=([128, 4 * nB], F32), negb=([128, 4 * nC], F32), esink=([128, 2 * nA], F32),
                   tmp=([128, 4 * nB], F32))
        MK = P.buf('MK', f=([128, 642], F32), b=([128, 642], BF16))
        ET = P.buf('ET', b=([128, 3 * nA * 128], BF16))
        ON = P.buf('ON', b=([128, 128], BF16), rst64=([128, 512], F32), rst128=([128, 512], F32))
        MEMN = P.buf('MEMN', b=([128, 8, 256], BF16))
        WUP = P.buf('WUP', b=([16, 2 * 2 * nC * 128], BF16))
        P.persist()

        cc = cfg.cc
        P.dma('sp', [(CO.c[:], consts_in)], writes=[CO])
        P.dma('sp', [(MK.f[:], masks_in)], writes=[MK])
        cp(P, 'pool', MK, MK.b[:], MK, MK.f[:])
        mset(P, 'pool', ON, ON.b[:], 1.0)
        mset(P, 'pool', ON, ON.rst64[:], 1.0)
        mset(P, 'pool', ON, ON.rst64[:].rearrange("p (c t) -> p c t", t=64)[:, :, 0:1], 0.0)
        mset(P, 'pool', ON, ON.rst128[:], 1.0)
        mset(P, 'pool', ON, ON.rst128[:].rearrange("p (c t) -> p c t", t=128)[:, :, 0:1], 0.0)
        nlb = 2 * nB
        lbp = cc['lbp']
        mset(P, 'dve', CO, CO.lb[:, 0:nlb], 0.0)
        tt(P, 'dve', CO, CO.tmp[:, 0:nlb], CO, CO.c[:, lbp:lbp + nlb], CO, CO.c[:, lbp + nlb:lbp + 2 * nlb], ALU.subtract)
        act(P, CO, CO.tmp[:, 0:nlb], CO, CO.tmp[:, 0:nlb], AF.Exp)
        ts(P, 'dve', CO, CO.tmp[:, 0:nlb], CO, CO.tmp[:, 0:nlb], 1.0, None, ALU.add)
        P.op('dve', lambda e: e.reciprocal(out=CO.lb[:, nlb:2 * nlb], in_=CO.tmp[:, 0:nlb]), reads=[CO], writes=[CO])
        ts(P, 'dve', CO, CO.oml[:], CO, CO.lb[:], -1.0, 1.0, ALU.mult, ALU.add)
        ts(P, 'dve', CO, CO.noml[:], CO, CO.oml[:], -1.0, None, ALU.mult)
        ts(P, 'dve', CO, CO.negb[:], CO, CO.c[:, cc['bg']:cc['bg'] + 4 * nC], -1.0, None, ALU.mult)
        act(P, CO, CO.esink[:], CO, CO.c[:, cc['sink']:cc['sink'] + 2 * nA], AF.Exp)

        with_stage = P.buf('PST', f=([128, 3 * nA * 128], F32))
        P.dma('sp', [(with_stage.f[:], etab_in)], writes=[with_stage])
        cp(P, 'pool', ET, ET.b[:], with_stage, with_stage.f[:])
        wu = P.buf('WUS', f=([16, 2 * 2 * nC * 128], F32))
        P.dma('sp', [(wu.f[:], w_up_in)], writes=[wu])
        cp(P, 'pool', WUP, WUP.b[:], wu, wu.f[:])
        mm_ = P.buf('MEMS', f=([128, 8, 256], F32), sq=([128, 8, 256], BF16), r=([128, 256], F32))
        P.dma('sp', [(mm_.f[:], memT_in.rearrange("(k p) t -> p k t", p=128))], writes=[mm_])
        act(P, mm_, mm_.sq[:], mm_, mm_.f[:], AF.Square)
        for k in range(8):
            mm(P, psb[0], psb[0].p[:, 0:256], ON, ON.b[:], mm_, mm_.sq[:, k, :], start=(k == 0), stop=(k == 7))
        act(P, mm_, mm_.r[:], psb[0], psb[0].p[:, 0:256], AF.Ln, scale=1.0 / D, bias=EPS)
        act(P, mm_, mm_.r[:], mm_, mm_.r[:], AF.Exp, scale=-0.5)
        for k in range(8):
            stt(P, 'dve', MEMN, MEMN.b[:, k, :], mm_, mm_.f[:, k, :], CO.c[:, cc['mem'] + k:cc['mem'] + k + 1],
                mm_, mm_.r[:], ALU.mult, ALU.mult, extra=[CO])
        P.end_phase()

        def phase_op1(l, HT):
            last = (l == cfg.n_layers)
            src = xT_in if l <= 1 else xs[(l - 1) % 2]
            dst = xs[l % 2]
            if l > 0:
                WO = P.buf('WO', b=([128, NKM, D], BF16))
                wst = P.ring(2, 'wost', f=([128, NKM, 128], F32))
                wv = w_out_in[l - 1].rearrange("(k p) c -> p k c", p=128)
                for c8 in range(8):
                    st = wst.next()
                    P.dma('sp', [(st.f[:], wv[:, :, c8 * 128:(c8 + 1) * 128])], writes=[st])
                    cp(P, 'pool', WO, WO.b[:, :, c8 * 128:(c8 + 1) * 128], st, st.f[:])
                mixr = P.ring(2, 'mixb', b=([128, NKM, 512], BF16))
            xr = P.ring(2, 'xb', f=([128, 8, 512], F32))
            sqr = P.ring(2, 'sqb', b=([128, 8, 512], BF16))
            rr = P.ring(2, 'rstd', f=([128, 512], F32))
            yr = P.ring(2, 'yb', f=([128, 8, 512], F32)) if last else None
            psi = 0
            for tb in range(NBLK):
                cs = slice(tb * 512, (tb + 1) * 512)
                X = xr.next()
                P.dma('sp', [(X.f[:], src.rearrange("(k p) t -> p k t", p=128)[:, :, cs])], writes=[X])
                if l > 0:
                    MX = mixr.next()
                    P.dma('sp', [(MX.b[:], mixT.rearrange("(k p) t -> p k t", p=128)[:, :, cs])], writes=[MX])
                    for c8 in range(8):
                        ps = psb[psi % 4]
                        psi += 1
                        for k in range(NKM):
                            mm(P, ps, ps.p[:], WO, WO.b[:, k, c8 * 128:(c8 + 1) * 128], MX, MX.b[:, k, :],
                               start=(k == 0), stop=(k == NKM - 1))
                        tt(P, 'dve', X, X.f[:, c8, :], ps, ps.p[:], X, X.f[:, c8, :], ALU.add)
                    if not last:
                        P.dma('pool', [(dst.rearrange("(k p) t -> p k t", p=128)[:, :, cs], X.f[:])], reads=[X])
                if last and not cfg.final_norm:
                    P.dma('pool', [(yT.rearrange("(k p) t -> p k t", p=128)[:, :, cs], X.f[:])], reads=[X])
                    continue
                SQ = sqr.next()
                act(P, SQ, SQ.b[:], X, X.f[:], AF.Square)
                pn = psb[4 + tb % 2]
                for k in range(8):
                    mm(P, pn, pn.p[:], ON, ON.b[:], SQ, SQ.b[:, k, :], start=(k == 0), stop=(k == 7))
                R = rr.next()
                act(P, R, R.f[:], pn, pn.p[:], AF.Ln, scale=1.0 / D, bias=EPS)
                act(P, R, R.f[:], R, R.f[:], AF.Exp, scale=-0.5)
                if not last:
                    gcol = cc['norm'] + l * 8
                    for k in range(8):
                        stt(P, 'dve', HT, HT.b[:, k, cs], X, X.f[:, k, :],
                            CO.c[:, gcol + k:gcol + k + 1], R, R.f[:], ALU.mult, ALU.mult, extra=[CO])
                else:
                    Y = yr.next()
                    gcol = cc['fin']
                    for k in range(8):
                        stt(P, 'dve', Y, Y.f[:, k, :], X, X.f[:, k, :],
                            CO.c[:, gcol + k:gcol + k + 1], R, R.f[:], ALU.mult, ALU.mult, extra=[CO])
                    P.dma('pool', [(yT.rearrange("(k p) t -> p k t", p=128)[:, :, cs], Y.f[:])], reads=[Y])

        def phase_p2(l, HT):
            even = (l % 2 == 0)
            i2 = l // 2
            wsrc = (w_in_e if even else w_in_o)[i2].rearrange("(k p) c -> p k c", p=128)
            G = cfg.ge if even else cfg.go
            wst = P.ring(3, 'wst', f=([128, 8, 128], F32))
            wbr = P.ring(3, 'wb', b=([128, 8, 128], BF16))
            ob = P.ring(4, 'ob', b=([128, 512], BF16))
            of = P.ring(4, 'of', f=([128, 512], F32))
            of2 = P.ring(3, 'of2', f=([128, 512], F32))
            psr = Ring(psb[0:4])
            psu = Ring(psb[4:7])
            rfr = P.ring(2, 'rfb', b=([16, 512], BF16))

            def load_w(c0, ncol):
                st = wst.next()
                P.dma('sp', [(st.f[:, :, 0:ncol], wsrc[:, :, c0:c0 + ncol])], writes=[st])
                wb = wbr.next()
                cp(P, 'pool', wb, wb.b[:, :, 0:ncol], st, st.f[:, :, 0:ncol])
                return wb

            def proj(wb, ncol, tb):
                ps = psr.next()
                for k in range(8):
                    mm(P, ps, ps.p[0:ncol, :], wb, wb.b[:, k, 0:ncol], HT, HT.b[:, k, tb * 512:(tb + 1) * 512],
                       start=(k == 0), stop=(k == 7))
                return ps

            def fgroup(name, kind, dst, scale=1.0, **kw):
                c0, n = G[name]
                r = 0
                while r < n:
                    ncol = min(128, n - r)
                    wb = load_w(c0 + r, ncol)
                    for tb in range(NBLK):
                        cs = slice(tb * 512, (tb + 1) * 512)
                        ps = proj(wb, ncol, tb)
                        if kind in ('copy', 'silu'):
                            O = ob.next()
                            act(P, O, O.b[0:ncol, :], ps, ps.p[0:ncol, :], AF.Copy if kind == 'copy' else AF.Silu,
                                scale=scale)
                            P.dma('pool', [(dst[r:r + ncol, cs], O.b[0:ncol, :])], reads=[O])
                        elif kind == 'hz':
                            d = kw['d']
                            h = r // 128
                            col = i2 * 2 * nB + d * nB + h
                            E = of.next()
                            act(P, E, E.f[:], ps, ps.p[:], AF.Exp, scale=-1.0)
                            ts(P, 'dve', E, E.f[:], E, E.f[:], 1.0, None, ALU.add)
                            P.op('dve', lambda e, E=E: e.reciprocal(out=E.f[:], in_=E.f[:]), reads=[E], writes=[E])
                            L = of2.next()
                            act(P, L, L.f[:], E, E.f[:], AF.Ln, scale=CO.oml[:, col:col + 1], bias=CO.lb[:, col:col + 1],
                                extra=[CO])
                            ts(P, 'pool', L, L.f[:], L, L.f[:], float(np.log(1e-30)), None, ALU.max)
                            P.dma('pool', [(s_lf[d, r:r + 128, cs], L.f[:])], reads=[L])
                            O = ob.next()
                            ts(P, 'dve', O, O.b[:], E, E.f[:], CO.noml[:, col:col + 1], CO.oml[:, col:col + 1],
                               ALU.mult, ALU.add, extra=[CO])
                            P.dma('pool', [(s_k[d, r:r + 128, cs], O.b[:])], reads=[O])
                        elif kind == 'gr':
                            d = kw['d']
                            RF = rfr.next()
                            act(P, RF, RF.b[:], ps, ps.p[0:16, :], AF.Copy)
                            for hc in range(nC):
                                col = i2 * 2 * nC + d * nC + hc
                                wcol = ((i2 * 2 + d) * nC + hc) * 128
                                pu = psu.next()
                                mm(P, pu, pu.p[:], WUP, WUP.b[:, wcol:wcol + 128], RF, RF.b[:])
                                E = of.next()
                                act(P, E, E.f[:], pu, pu.p[:], AF.Exp, scale=-1.0, bias=CO.negb[:, col:col + 1], extra=[CO])
                                act(P, E, E.f[:], E, E.f[:], AF.Ln, scale=1.0, bias=1.0)
                                L = of2.next()
                                ts(P, 'pool', L, L.f[:], E, E.f[:], -1.0 / 16.0, None, ALU.mult)
                                P.dma('pool', [(s_lf[d, hc * 128:(hc + 1) * 128, cs], L.f[:])], reads=[L])
                    r += ncol

            def tgroup(name, dst):
                c0, n = G[name]
                WT = P.buf('WT', b=([128, 8, n], BF16))
                r = 0
                while r < n:
                    st = wst.next()
                    P.dma('sp', [(st.f[:], wsrc[:, :, c0 + r:c0 + r + 128])], writes=[st])
                    cp(P, 'pool', WT, WT.b[:, :, r:r + 128], st, st.f[:])
                    r += 128
                for tti in range(NTILE):
                    r = 0
                    while r < n:
                        ncol = min(512, n - r)
                        ps = psr.next()
                        for k in range(8):
                            mm(P, ps, ps.p[:, 0:ncol], HT, HT.b[:, k, tti * 128:(tti + 1) * 128], WT, WT.b[:, k, r:r + ncol],
                               start=(k == 0), stop=(k == 7))
                        O = ob.next()
                        act(P, O, O.b[:, 0:ncol], ps, ps.p[:, 0:ncol], AF.Copy)
                        P.dma('pool', [(dst[tti * 128:(tti + 1) * 128, r:r + ncol], O.b[:, 0:ncol])], reads=[O])
                        r += ncol

            if even:
                fgroup('qA', 'copy', s_qA, scale=0.125)
                fgroup('kA', 'copy', s_kA)
                fgroup('qM', 'copy', s_qM)
                fgroup('gA', 'silu', s_gA)
                fgroup('qB', 'silu', s_q)
                fgroup('gB', 'silu', s_g)
                fgroup('gM', 'silu', s_gM)
                fgroup('zf', 'hz', None, d=0)
                fgroup('zb', 'hz', None, d=1)
                tgroup('vA', s_vA)
                tgroup('iB', s_v)
            else:
                fgroup('qC', 'copy', s_q, scale=float(128 ** -0.5))
                fgroup('kC', 'copy', s_k[0])
                fgroup('qM', 'copy', s_qM)
                fgroup('gC', 'silu', s_g)
                fgroup('gM', 'silu', s_gM)
                fgroup('rf', 'gr', None, d=0)
                fgroup('rb', 'gr', None, d=1)
                tgroup('vC', s_v)

        def phase_kv(l):
            KV = P.buf('KV', k=([128, nM, 256], BF16), v=([128, 2, nM * 128], BF16))
            WK = P.buf('WK', b=([128, 8, 2 * nM * 128], BF16))
            wst = P.ring(2, 'kvst', f=([128, 8, 128], F32))
            wv = w_kv_in[l].rearrange("(k p) c -> p k c", p=128)
            for c in range(2 * nM):
                st = wst.next()
                P.dma('sp', [(st.f[:], wv[:, :, c * 128:(c + 1) * 128])], writes=[st])
                cp(P, 'pool', WK, WK.b[:, :, c * 128:(c + 1) * 128], st, st.f[:])
            for h in range(nM):
                ps = psb[h % 2]
                for k in range(8):
                    mm(P, ps, ps.p[:, 0:256], WK, WK.b[:, k, h * 128:(h + 1) * 128], MEMN, MEMN.b[:, k, :],
                       start=(k == 0), stop=(k == 7))
                act(P, KV, KV.k[:, h, :], ps, ps.p[:, 0:256], AF.Copy)
            for sc in range(2):
                ps = psb[2 + sc]
                for k in range(8):
                    mm(P, ps, ps.p[:, 0:nM * 128], MEMN, MEMN.b[:, k, sc * 128:(sc + 1) * 128],
                       WK, WK.b[:, k, nM * 128:2 * nM * 128], start=(k == 0), stop=(k == 7))
                act(P, KV, KV.v[:, sc, :], ps, ps.p[:, 0:nM * 128], AF.Copy)
            return KV

        def phase_mem(KV, row0):
            qr = P.ring(3, 'mq', q=([128, 512], BF16), g=([128, 512], BF16))
            pr = P.ring(3, 'mp', b=([128, 2, 512], BF16))
            rr = P.ring(2, 'mr', f=([128, 512], F32), o=([128, 512], F32))
            orr = P.ring(3, 'mo', b=([128, 512], BF16))
            scale = float(128 ** -0.5)
            it = 0
            for tb in range(NBLK):
                cs = slice(tb * 512, (tb + 1) * 512)
                for h in range(nM):
                    Q = qr.next()
                    P.dma('sp', [(Q.q[:], s_qM[h * 128:(h + 1) * 128, cs]), (Q.g[:], s_gM[h * 128:(h + 1) * 128, cs])],
                          writes=[Q])
                    PB = pr.next()
                    for sc in range(2):
                        ps = psb[(it * 2 + sc) % 4]
                        mm(P, ps, ps.p[:], KV, KV.k[:, h, sc * 128:(sc + 1) * 128], Q, Q.q[:])
                        act(P, PB, PB.b[:, sc, :], ps, ps.p[:], AF.Exp, scale=scale)
                    po = psb[4 + it % 2]
                    pd = psb[6]
                    for sc in range(2):
                        mm(P, po, po.p[:], KV, KV.v[:, sc, h * 128:(h + 1) * 128], PB, PB.b[:, sc, :],
                           start=(sc == 0), stop=(sc == 1))
                    for sc in range(2):
                        mm(P, pd, pd.p[:], ON, ON.b[:], PB, PB.b[:, sc, :], start=(sc == 0), stop=(sc == 1))
                    R = rr.next()
                    P.op('dve', lambda e, R=R, pd=pd: e.reciprocal(out=R.f[:], in_=pd.p[:]), reads=[pd], writes=[R])
                    tt(P, 'dve', R, R.o[:], po, po.p[:], R, R.f[:], ALU.mult)
                    O = orr.next()
                    tt(P, 'pool', O, O.b[:], R, R.o[:], Q, Q.g[:], ALU.mult)
                    P.dma('pool', [(mixT[row0 + h * 128:row0 + (h + 1) * 128, cs], O.b[:])], reads=[O])
                    it += 1

        def phase_attn(i2, row0):
            lr = P.ring(2, 'aq', q=([64, 4, 512], BF16), g=([64, 4, 512], BF16), k=([64, 768], BF16),
                        v=([128, 6, 64], BF16))
            pr = P.ring(4, 'ap', b=([128, 512], BF16))
            dr = P.ring(2, 'ad', f=([64, 512], F32), o=([64, 512], F32))
            orr = P.ring(2, 'ao', b=([64, 4, 512], BF16))
            pss = Ring(psb[0:4])
            it = 0
            for tb in range(NBLK):
                cs = slice(tb * 512, (tb + 1) * 512)
                for n in range(nKV):
                    L = lr.next()
                    k0 = max(0, tb * 512 - 128)
                    k1 = min(T, tb * 512 + 640)
                    ko = k0 - (tb * 512 - 128)
                    t0 = max(0, tb * 4 - 1)
                    t1 = min(NTILE, tb * 4 + 5)
                    to = t0 - (tb * 4 - 1)
                    P.dma('sp', [
                        (L.q[:], s_qA[n * 256:(n + 1) * 256, cs].rearrange("(g d) t -> d g t", d=64)),
                        (L.g[:], s_gA[n * 256:(n + 1) * 256, cs].rearrange("(g d) t -> d g t", d=64)),
                        (L.k[:, ko:ko + (k1 - k0)], s_kA[n * 64:(n + 1) * 64, k0:k1]),
                        (L.v[:, to:to + (t1 - t0), :],
                         s_vA[t0 * 128:t1 * 128, n * 64:(n + 1) * 64].rearrange("(j p) c -> p j c", p=128)),
                    ], writes=[L])
                    O = orr.next()
                    for qt in range(4):
                        c = tb * 4 + qt
                        js = [j for j in range(3) if 0 <= c - 1 + j < NTILE]
                        PBs = []
                        for j in js:
                            ps = pss.next()
                            kc = (qt + j) * 128
                            mm(P, ps, ps.p[:].rearrange("p (g t) -> p g t", g=4), L, L.k[:, kc:kc + 128], L, L.q[:, :, qt * 128:(qt + 1) * 128])
                            PB = pr.next()
                            act(P, PB, PB.b[:], ps, ps.p[:], AF.Exp)
                            e0 = (j * nA + n * 4) * 128
                            tt(P, 'pool', PB, PB.b[:], PB, PB.b[:], ET, ET.b[:, e0:e0 + 512], ALU.mult)
                            PBs.append((j, PB))
                        po = psb[4 + it % 2]
                        pd = psb[6]
                        for ii, (j, PB) in enumerate(PBs):
                            mm(P, po, po.p[0:64, :], L, L.v[:, qt + j, :], PB, PB.b[:],
                               start=(ii == 0), stop=(ii == len(PBs) - 1))
                        for ii, (j, PB) in enumerate(PBs):
                            mm(P, pd, pd.p[0:64, :], ON, ON.b[:, 0:64], PB, PB.b[:],
                               start=(ii == 0), stop=(ii == len(PBs) - 1))
                        Dn = dr.next()
                        sk = i2 * nA + n * 4
                        tt(P, 'dve', Dn, Dn.f[:].rearrange("p (g t) -> p g t", g=4),
                           pd, pd.p[0:64, :].rearrange("p (g t) -> p g t", g=4),
                           CO, CO.esink[0:64, sk:sk + 4].unsqueeze(2).to_broadcast([64, 4, 128]), ALU.add)
                        P.op('dve', lambda e, Dn=Dn: e.reciprocal(out=Dn.f[:], in_=Dn.f[:]), reads=[Dn], writes=[Dn])
                        tt(P, 'dve', Dn, Dn.o[:], po, po.p[0:64, :], Dn, Dn.f[:], ALU.mult)
                        tt(P, 'pool', O, O.b[:, :, qt * 128:(qt + 1) * 128],
                           Dn, Dn.o[:].rearrange("p (g t) -> p g t", g=4),
                           L, L.g[:, :, qt * 128:(qt + 1) * 128], ALU.mult)
                        it += 1
                    P.dma('pool', [(mixT[row0 + n * 256:row0 + (n + 1) * 256, cs].rearrange("(g d) t -> d g t", d=64),
                                    O.b[:])], reads=[O])

        def phase_scan(nh, dv, C, kdirs, gcol0, row0):
            nch = T // C
            ncb = 512 // C
            wv = 512 // ncb
            nr = dv // wv
            nvh = dv // 128
            mcol = 0 if C == 64 else 256
            rst = ON.rst64 if C == 64 else ON.rst128
            QT = [P.buf('QT%d' % d, b=([128, T], BF16)) for d in range(2)]
            AT = [P.buf('AT%d' % d, b=([128, NTILE, 128], BF16)) for d in range(2)]
            SIN = [P.buf('SIN%d' % d, b=([128, nch, dv], BF16)) for d in range(2)]
            DS = P.buf('DS', f=([128, dv, nch], F32))
            DA = P.buf('DA', f=([128, nch], F32))
            DR = P.buf('DR', f=([128, 32, nch], F32))
            SF = P.ring(2, 'SF', f=([128, 32, nch], F32))
            ld = P.ring(2, 'sl', lf=([128, 512], F32), k=([128, 512], BF16), q=([128, 512], BF16),
                        v=([128, 4, dv], BF16))
            w1 = P.ring(2, 'sw1', pre=([128, 512], F32), b=([128, 512], F32))
            w2 = P.ring(2, 'sw2', eb=([128, 512], F32), enb=([128, 512], F32), ec=([128, 512], F32))
            w3 = P.ring(2, 'sw3', kt=([128, 512], BF16), kh=([128, 512], BF16), khT=([128, 2, 4, 128], BF16))
            l3 = P.ring(2, 'sl3', v=([128, 4, dv], BF16), g=([128, nvh, 512], BF16))
            w4 = P.ring(2, 'sw4', sq=([128, nvh, 512], BF16), r=([128, 512], F32), m=([128, nvh, 512], F32))
            w5 = P.ring(2, 'sw5', b=([128, nvh, 512], BF16))
            it = 0
            import os
            LIM = int(os.environ.get('SCAN_LIMIT', '99'))
            for h in range(nh):
                hr = slice(h * 128, (h + 1) * 128)
                for d in range(2):
                    kd = d if kdirs else 0
                    for tb in range(NBLK):
                        cs = slice(tb * 512, (tb + 1) * 512)
                        L = ld.next()
                        P.dma('sp', [
                            (L.lf[:], s_lf[d, hr, cs]), (L.k[:], s_k[kd, hr, cs]), (L.q[:], s_q[hr, cs]),
                            (L.v[:], s_v[cs, h * dv:(h + 1) * dv].rearrange("(j p) c -> p j c", p=128)),
                        ], writes=[L])
                        W1 = w1.next()
                        P.op('dve', lambda e, W1=W1, L=L: e.tensor_tensor_scan(
                            out=W1.pre[:], data0=rst[:], data1=L.lf[:], initial=0.0, op0=ALU.mult, op1=ALU.add),
                            reads=[L, ON], writes=[W1])
                        pre3 = W1.pre[:].rearrange("p (c t) -> p c t", t=C)
                        b3 = W1.b[:].rearrange("p (c t) -> p c t", t=C)
                        if d == 0:
                            bsrc = W1.pre
                            edge = pre3[:, :, C - 1:C]
                        else:
                            tt(P, 'pool', W1, W1.b[:], L, L.lf[:], W1, W1.pre[:], ALU.subtract)
                            tt(P, 'dve', W1, b3, W1, b3, W1, pre3[:, :, C - 1:C].to_broadcast([128, ncb, C]), ALU.add)
                            bsrc = W1.b
                            edge = b3[:, :, 0:1]
                        W2 = w2.next()
                        act(P, W2, W2.eb[:], W1, bsrc[:], AF.Exp)
                        ts(P, 'pool', W2, W2.ec[:], W1, bsrc[:], -80.0, None, ALU.max)
                        act(P, W2, W2.enb[:], W2, W2.ec[:], AF.Exp, scale=-1.0)
                        tt(P, 'dve', W2, W2.ec[:].rearrange("p (c t) -> p c t", t=C), W1,
                           edge.to_broadcast([128, ncb, C]), W1, bsrc[:].rearrange("p (c t) -> p c t", t=C), ALU.subtract)
                        act(P, W2, W2.ec[:], W2, W2.ec[:], AF.Exp)
                        if d == 0:
                            j0 = tb * ncb
                            act(P, DA, DA.f[:, j0:j0 + ncb], W1, edge.rearrange("p c o -> p (c o)"), AF.Exp)
                        else:
                            for cl in range(ncb):
                                j = nch - 1 - (tb * ncb + cl)
                                act(P, DA, DA.f[:, j:j + 1], W1, b3[:, cl, 0:1], AF.Exp)
                        if LIM <= 0:
                            return
                        tt(P, 'dve', QT[d], QT[d].b[:, cs], L, L.q[:], W2, W2.eb[:], ALU.mult)
                        W3 = w3.next()
                        tt(P, 'dve', W3, W3.kt[:], L, L.k[:], W2, W2.enb[:], ALU.mult)
                        tt(P, 'dve', W3, W3.kh[:], L, L.k[:], W2, W2.ec[:], ALU.mult)
                        if LIM <= 1:
                            return
                        pa = psb[it % 2]
                        for i4 in range(4):
                            ss = slice(i4 * 128, (i4 + 1) * 128)
                            mm(P, pa, pa.p[:, ss], W3, W3.kt[:, ss], QT[d], QT[d].b[:, tb * 512 + i4 * 128:tb * 512 + (i4 + 1) * 128])
                        mk = MK.b[:, mcol + d * 128:mcol + (d + 1) * 128]
                        tt(P, 'dve', AT[d], AT[d].b[:, tb * 4:(tb + 1) * 4, :], pa, pa.p[:].rearrange("p (i t) -> p i t", i=4),
                           MK, mk.unsqueeze(1).to_broadcast([128, 4, 128]), ALU.mult)
                        if LIM <= 2:
                            return
                        for i4 in range(4):
                            P.op('pe', lambda e, W3=W3, i4=i4: e.transpose(pst.p[:, i4 * 128:(i4 + 1) * 128],
                                                                       W3.kh[:, i4 * 128:(i4 + 1) * 128], MK.b[:, 512:640]),
                                 reads=[W3, MK], writes=[pst])
                        if C == 64:
                            for half in range(2):
                                act(P, W3, W3.khT[:, half, :, :], pst, pst.p[:, 0:512].rearrange("p (i t) -> p i t", i=4),
                                    AF.Copy, scale=MK.f[:, 640 + half:641 + half], extra=[MK])
                        else:
                            act(P, W3, W3.khT[:, 0, :, :], pst, pst.p[:, 0:512].rearrange("p (i t) -> p i t", i=4), AF.Copy)
                        if LIM <= 3:
                            return
                        for r in range(nr):
                            pd = psb[2 + (it * nr + r) % 2]
                            for cl in range(ncb):
                                i4 = (cl * C) // 128
                                p0 = (cl * C) % 128
                                slot = cl if d == 0 else ncb - 1 - cl
                                mm(P, pd, pd.p[:, slot * wv:(slot + 1) * wv], W3, W3.khT[:, p0 // 64, i4, :],
                                   L, L.v[:, i4, r * wv:(r + 1) * wv])
                            j0 = tb * ncb if d == 0 else nch - (tb + 1) * ncb
                            cp(P, 'dve', DS, DS.f[:, r * wv:(r + 1) * wv, j0:j0 + ncb],
                               pd, pd.p[:].rearrange("p (c v) -> p v c", v=wv))
                        it += 1
                        if LIM <= 4:
                            return
                    act(P, DR, DR.f[:], DA, DA.f[:].unsqueeze(1).to_broadcast([128, 32, nch]), AF.Copy)
                    mset(P, 'pool', DR, DR.f[:, :, 0:1], 0.0)
                    mset(P, 'pool', SIN[d], SIN[d].b[:, 0, :], 0.0)
                    for v0 in range(0, dv, 32):
                        S = SF.next()
                        P.op('dve', lambda e, S=S, v0=v0: e.tensor_tensor_scan(
                            out=S.f[:].rearrange("p v c -> p (v c)"), data0=DR.f[:].rearrange("p v c -> p (v c)"),
                            data1=DS.f[:, v0:v0 + 32, :].rearrange("p v c -> p (v c)"), initial=0.0,
                            op0=ALU.mult, op1=ALU.add), reads=[DR, DS], writes=[S])
                        act(P, SIN[d], SIN[d].b[:, 1:nch, v0:v0 + 32],
                            S, S.f[:, :, 0:nch - 1].rearrange("p v c -> p c v"), AF.Copy)
                if LIM <= 5:
                    return
                for tb in range(NBLK):
                    cs = slice(tb * 512, (tb + 1) * 512)
                    L3 = l3.next()
                    P.dma('sp', [
                        (L3.v[:], s_v[cs, h * dv:(h + 1) * dv].rearrange("(j p) c -> p j c", p=128)),
                        (L3.g[:], s_g[h * dv:(h + 1) * dv, cs].rearrange("(a p) t -> p a t", p=128)),
                    ], writes=[L3])
                    W4 = w4.next()
                    pos = []
                    for vh in range(nvh):
                        po = psb[4 + vh]
                        pos.append(po)
                        vs = slice(vh * 128, (vh + 1) * 128)
                        for i4 in range(4):
                            ti = tb * 4 + i4
                            ts_ = slice(i4 * 128, (i4 + 1) * 128)
                            mm(P, po, po.p[:, ts_], L3, L3.v[:, i4, vs], AT[0], AT[0].b[:, ti, :], start=True, stop=False)
                            mm(P, po, po.p[:, ts_], L3, L3.v[:, i4, vs], AT[1], AT[1].b[:, ti, :], start=False, stop=False)
                            nci = 128 // C
                            for ci in range(nci):
                                cg = ti * nci + ci
                                tcs = slice(i4 * 128 + ci * C, i4 * 128 + (ci + 1) * C)
                                gcs = slice(ti * 128 + ci * C, ti * 128 + (ci + 1) * C)
                                mm(P, po, po.p[:, tcs], SIN[0], SIN[0].b[:, cg, vs], QT[0], QT[0].b[:, gcs],
                                   start=False, stop=False)
                                mm(P, po, po.p[:, tcs], SIN[1], SIN[1].b[:, nch - 1 - cg, vs], QT[1], QT[1].b[:, gcs],
                                   start=False, stop=(ci == nci - 1))
                        act(P, W4, W4.sq[:, vh, :], po, po.p[:], AF.Square)
                    pn = psb[6]
                    for vh in range(nvh):
                        mm(P, pn, pn.p[:], ON, ON.b[:], W4, W4.sq[:, vh, :], start=(vh == 0), stop=(vh == nvh - 1))
                    act(P, W4, W4.r[:], pn, pn.p[:], AF.Ln, scale=1.0 / dv, bias=EPS)
                    act(P, W4, W4.r[:], W4, W4.r[:], AF.Exp, scale=-0.5)
                    O = w5.next()
                    for vh in range(nvh):
                        tt(P, 'dve', W4, W4.m[:, vh, :], pos[vh], pos[vh].p[:], W4, W4.r[:], ALU.mult)
                        gc = gcol0 + h * nvh + vh
                        stt(P, 'dve', O, O.b[:, vh, :], W4, W4.m[:, vh, :], CO.c[:, gc:gc + 1], L3, L3.g[:, vh, :],
                            ALU.mult, ALU.mult, extra=[CO])
                    P.dma('pool', [(mixT[row0 + h * dv:row0 + (h + 1) * dv, cs].rearrange("(a p) t -> p a t", p=128),
                                    O.b[:])], reads=[O])
                    if LIM <= 6:
                        return

        stop = cfg.stop_after
        import os
        if os.environ.get('ONLY_SCAN'):
            phase_scan(nB, 128, 64, True, cc['hg'], nA * 64)
            P.end_phase()
            P.emit()
            return nc
        for l in range(cfg.n_layers):
            lastl = (l == cfg.n_layers - 1)
            HT = P.buf('HT', b=([128, 8, T], BF16))
            mark = P.sb_off
            phase_op1(l, HT)
            P.end_phase(keep=mark)
            if lastl and stop == 'op1':
                break
            phase_p2(l, HT)
            P.end_phase()
            if lastl and stop == 'p2':
                break
            i2 = l // 2
            if not os.environ.get('SKIP_MEM'):
                KV = phase_kv(l)
            if l % 2 == 0:
                if not os.environ.get('SKIP_MEM'):
                    phase_mem(KV, nA * 64 + nB * 128)
                    P.end_phase()
                if lastl and stop == 'mem':
                    break
                phase_scan(nB, 128, 64, True, cc['hg'] + i2 * nB, nA * 64)
                P.end_phase()
                if lastl and stop == 'scan0':
                    break
                if not os.environ.get('SKIP_ATTN'):
                    phase_attn(i2, 0)
                    P.end_phase()
            else:
                phase_mem(KV, nC * 256)
                P.end_phase()
                if lastl and stop == 'mem':
                    break
                phase_scan(nC, 256, 128, False, cc['gl'] + i2 * nC * 2, 0)
                P.end_phase()
        if stop is None:
            phase_op1(cfg.n_layers, None)
            P.end_phase()
        P.emit()
        print('ops recorded:', P.nops, {e: len(s) for e, s in P.streams.items()})
    return nc


def core_inputs(cfg, b, heads, inp):
    f32 = np.float32
    hA, hKV, hB, hM, hC = heads['A'], heads['KV'], heads['B'], heads['M'], heads['C']
    out = {}
    out['xT'] = np.ascontiguousarray(inp['x'][b].T)
    out['memT'] = np.ascontiguousarray(inp['mem'][b].T)

    def cols(base, hs, w):
        return np.concatenate([np.arange(base + h * w, base + (h + 1) * w) for h in hs])
    EV = np.cumsum([0, 512, 128, 128, 512, 512, 512, 512, 512, 512, 512, 512])
    names = ['qA', 'kA', 'vA', 'gA', 'qB', 'zf', 'zb', 'iB', 'gB', 'qM', 'gM']
    eo = dict(zip(names, EV[:-1]))
    sel_e = np.concatenate([
        cols(eo['qA'], hA, 64), cols(eo['kA'], hKV, 64), cols(eo['gA'], hA, 64),
        cols(eo['qB'], hB, 128), cols(eo['gB'], hB, 128), cols(eo['qM'], hM, 128), cols(eo['gM'], hM, 128),
        cols(eo['zf'], hB, 128), cols(eo['zb'], hB, 128), cols(eo['vA'], hKV, 64), cols(eo['iB'], hB, 128)])
    assert len(sel_e) == cfg.nce
    out['w_in_e'] = np.ascontiguousarray(inp['w_in_even'][:, :, sel_e])
    OD = np.cumsum([0, 512, 512, 1024, 1024, 16, 16, 512, 512])
    oo = dict(zip(['qC', 'kC', 'vC', 'gC', 'rf', 'rb', 'qM', 'gM'], OD[:-1]))
    sel_o = np.concatenate([
        cols(oo['qC'], hC, 128), cols(oo['kC'], hC, 128), cols(oo['gC'], hC, 256),
        cols(oo['qM'], hM, 128), cols(oo['gM'], hM, 128),
        np.arange(oo['rf'], oo['rf'] + 16), np.arange(oo['rb'], oo['rb'] + 16), cols(oo['vC'], hC, 256)])
    assert len(sel_o) == cfg.nco
    out['w_in_o'] = np.ascontiguousarray(inp['w_in_odd'][:, :, sel_o])
    rows_e = np.concatenate([cols(0, hA, 64), cols(512, hB, 128), cols(1024, hM, 128)])
    rows_o = np.concatenate([cols(0, hC, 256), cols(1024, hM, 128)])
    wo = np.zeros((4, cfg.mixr, D), f32)
    for l in range(4):
        if l % 2 == 0:
            wo[l] = inp['w_out_even'][l // 2][rows_e]
        else:
            wo[l] = inp['w_out_odd'][l // 2][rows_o]
    out['w_out'] = wo
    kvc = np.concatenate([cols(0, hM, 128), cols(512, hM, 128)])
    out['w_kv'] = np.ascontiguousarray(inp['w_mem_kv'][:, :, kvc])
    wu = inp['w_gate_up'][:, :, :, cols(0, hC, 128)]
    out['w_up'] = np.ascontiguousarray(wu.transpose(2, 0, 1, 3).reshape(16, -1))
    cc = cfg.cc
    C = np.zeros((128, cfg.ncc), f32)
    for l in range(4):
        g = inp['norm_even'][l // 2] if l % 2 == 0 else inp['norm_odd'][l // 2]
        C[:, cc['norm'] + l * 8:cc['norm'] + (l + 1) * 8] = g.reshape(8, 128).T
    C[:, cc['fin']:cc['fin'] + 8] = inp['final_norm'].reshape(8, 128).T
    C[:, cc['mem']:cc['mem'] + 8] = inp['mem_norm'].reshape(8, 128).T
    nB, nC, nA = cfg.nB, cfg.nC, cfg.nA
    for i in range(2):
        for hi, h in enumerate(hB):
            C[:, cc['hg'] + i * nB + hi] = inp['hgrn_norm'][i, h * 128:(h + 1) * 128]
        for hi, h in enumerate(hC):
            for vh in range(2):
                C[:, cc['gl'] + i * nC * 2 + hi * 2 + vh] = inp['gla_norm'][i, h * 256 + vh * 128:h * 256 + (vh + 1) * 128]
        for d in range(2):
            for hi, h in enumerate(hB):
                C[:, cc['lbp'] + i * 2 * nB + d * nB + hi] = inp['lb_param'][i, d, h * 128:(h + 1) * 128]
            for hi, h in enumerate(hC):
                C[:, cc['bg'] + i * 2 * nC + d * nC + hi] = inp['b_gate'][i, d, h * 128:(h + 1) * 128]
        for hi, h in enumerate(hA):
            C[:, cc['sink'] + i * nA + hi] = inp['sink'][i, h]
    out['consts'] = C
    out['etab'] = np.ascontiguousarray(alibi_table(hA).reshape(128, -1))
    out['masks'] = scan_masks()
    return out


_CACHE = {}


def kernel(**inputs):
    inp = {k: np.asarray(v) for k, v in inputs.items()}
    cfg = Cfg(8, 4, 4, 4)
    heads = {'A': list(range(8)), 'KV': [0, 1], 'B': list(range(4)), 'M': list(range(4)), 'C': list(range(4))}
    if 'nc' not in _CACHE:
        _CACHE['nc'] = build(cfg)
    nc = _CACHE['nc']
    in_maps = [core_inputs(cfg, c % 4, heads, inp) for c in range(N_CORES)]
    res = run_bass_kernel_spmd(nc, in_maps, core_ids=list(range(N_CORES)))
    out = np.stack([np.ascontiguousarray(res.results[b]['yT'].T) for b in range(4)], axis=0)
    return out.astype(np.float32)
```

```python
import contextlib
import numpy as np
import concourse.bass as bass
import concourse.mybir as mybir
from concourse.bass_utils import run_bass_kernel_spmd

F32 = mybir.dt.float32
BF16 = mybir.dt.bfloat16
ALU = mybir.AluOpType
AF = mybir.ActivationFunctionType
DT_SIZE = {F32: 4, BF16: 2}

T = 4096
D = 1024
NBLK = T // 512
NTILE = T // 128
EPS = 1e-6
N_CORES = 8


class Buf:
    def __init__(self, name):
        self.name = name
        self.lw = None
        self.rd = {}
        self.dsem = None
        self.tt = {}

    def __getattr__(self, k):
        t = self.__dict__.get('tt', {})
        if k in t:
            return t[k]
        raise AttributeError(k)


class Prog:
    COMPUTE = ('pe', 'act', 'dve', 'pool')
    ENG = ('pe', 'act', 'dve', 'pool', 'sp')

    def __init__(self, nc, es, n_dsem=88, sb_limit=229000, sb_start=16640):
        self.nc = nc
        self.semh = {}
        for e in self.COMPUTE:
            self.semh[e] = es.enter_context(nc.semaphore('c_' + e))
        self.free_dsem = []
        for i in range(n_dsem):
            k = ('d', i)
            self.semh[k] = es.enter_context(nc.semaphore('d%d' % i))
            self.free_dsem.append(k)
        self.cnt = {k: 0 for k in self.semh}
        self.streams = {e: [] for e in self.ENG}
        self.waited = {e: {} for e in self.ENG}
        self.sb_off = sb_start
        self.sb_base = sb_start
        self.sb_limit = sb_limit
        self.uid = 0
        self.phase_dsems = []
        self.nops = 0

    def sbuf(self, shape, dtype, name='t'):
        self.uid += 1
        per_part = int(np.prod(shape[1:])) * DT_SIZE[dtype]
        off = (self.sb_off + 63) // 64 * 64
        assert off + per_part <= self.sb_limit, ('SBUF overflow', name, off, per_part)
        h = self.nc.alloc_sbuf_tensor_at('%s_%d' % (name, self.uid), list(shape), dtype, offset=off)
        self.sb_off = off + per_part
        return h

    def buf(self, name, **tensors):
        b = Buf(name)
        for k, (shape, dtype) in tensors.items():
            b.tt[k] = self.sbuf(shape, dtype, name + '_' + k)
        return b

    def ring(self, n, name, **tensors):
        return Ring([self.buf('%s%d' % (name, i), **tensors) for i in range(n)])

    def wrap(self, name, **handles):
        b = Buf(name)
        b.tt.update(handles)
        return b

    def need_dsem(self, b):
        if b.dsem is None:
            b.dsem = self.free_dsem.pop()
            self.phase_dsems.append(b.dsem)
        return b.dsem

    def persist(self):
        self.sb_base = self.sb_off
        self.phase_dsems = []

    def _deps(self, eng, reads, writes, is_dma):
        deps = {}

        def need(tok):
            if tok is None:
                return
            k, v = tok
            if deps.get(k, 0) < v:
                deps[k] = v
        for b in reads:
            need(b.lw)
        for b in writes:
            if b.lw is not None and (is_dma or b.lw[0] != eng):
                need(b.lw)
            for k, v in b.rd.items():
                if is_dma or k != eng:
                    need((k, v))
        w = self.waited[eng]
        out = []
        for k, v in deps.items():
            if w.get(k, 0) < v:
                w[k] = v
                out.append((k, v))
        return out

    def _commit(self, tok, reads, writes):
        for b in writes:
            b.lw = tok
            b.rd = {}
        k, v = tok
        for b in reads:
            if b in writes:
                continue
            if b.rd.get(k, 0) < v:
                b.rd[k] = v

    def op(self, eng, fn, reads=(), writes=()):
        waits = self._deps(eng, reads, writes, False)
        self.cnt[eng] += 1
        tok = (eng, self.cnt[eng])
        self.streams[eng].append((waits, fn, eng, 1))
        self._commit(tok, reads, writes)
        self.nops += 1

    def dma(self, queue, pairs, reads=(), writes=()):
        onchip = list(writes) + list(reads)
        key = self.need_dsem(onchip[0])
        for b in onchip[1:]:
            assert b.dsem is None or b.dsem == key
            b.dsem = key
        waits = self._deps(queue, reads, writes, True)
        first = True
        for (o, i) in pairs:
            self.cnt[key] += 16
            self.streams[queue].append((waits if first else [],
                                        (lambda e, o=o, i=i: e.dma_start(out=o, in_=i)), key, 16))
            first = False
            self.nops += 1
        self._commit((key, self.cnt[key]), reads, writes)

    def barrier(self):
        for e in self.ENG:
            waits = []
            w = self.waited[e]
            for k, v in self.cnt.items():
                if v > 0 and w.get(k, 0) < v and k != e:
                    w[k] = v
                    waits.append((k, v))
            if waits:
                self.streams[e].append((waits, None, None, 0))

    def end_phase(self, keep=None):
        self.barrier()
        self.free_dsem.extend(self.phase_dsems)
        self.phase_dsems = []
        self.sb_off = self.sb_base if keep is None else keep

    def emit(self):
        nc = self.nc
        semh = self.semh
        streams = self.streams
        with nc.Block() as block:
            def body(name):
                def f(e):
                    for (waits, fn, key, inc) in streams[name]:
                        for (k, v) in waits:
                            e.wait_ge(semh[k], v)
                        if fn is not None:
                            fn(e).then_inc(semh[key], inc)
                return f
            block.tensor(body('pe'))
            block.scalar(body('act'))
            block.vector(body('dve'))
            block.gpsimd(body('pool'))
            block.sync(body('sp'))


class Ring:
    def __init__(self, bufs):
        self.bufs = bufs
        self.i = 0

    def next(self):
        b = self.bufs[self.i % len(self.bufs)]
        self.i += 1
        return b


def mm(P, ob, o, lb, l, rb, r, start=True, stop=True):
    P.op('pe', lambda e: e.matmul(o, lhsT=l, rhs=r, start=start, stop=stop), reads=[lb, rb], writes=[ob])


def act(P, ob, o, ib, i, func, scale=1.0, bias=0.0, extra=()):
    P.op('act', lambda e: e.activation(out=o, in_=i, func=func, scale=scale, bias=bias),
         reads=[ib] + list(extra), writes=[ob])


def tt(P, eng, ob, o, ab, a, bb, b, op):
    P.op(eng, lambda e: e.tensor_tensor(out=o, in0=a, in1=b, op=op), reads=[ab, bb], writes=[ob])


def ts(P, eng, ob, o, ab, a, s1, s2, op0, op1=None, extra=()):
    if op1 is None:
        P.op(eng, lambda e: e.tensor_scalar(out=o, in0=a, scalar1=s1, scalar2=None, op0=op0),
             reads=[ab] + list(extra), writes=[ob])
    else:
        P.op(eng, lambda e: e.tensor_scalar(out=o, in0=a, scalar1=s1, scalar2=s2, op0=op0, op1=op1),
             reads=[ab] + list(extra), writes=[ob])


def stt(P, eng, ob, o, ab, a, scalar, bb, b, op0, op1, extra=()):
    P.op(eng, lambda e: e.scalar_tensor_tensor(out=o, in0=a, scalar=scalar, in1=b, op0=op0, op1=op1),
         reads=[ab, bb] + list(extra), writes=[ob])


def cp(P, eng, ob, o, ib, i):
    P.op(eng, lambda e: e.tensor_copy(out=o, in_=i), reads=[ib], writes=[ob])


def mset(P, eng, ob, o, val):
    P.op(eng, lambda e: e.memset(o, val), writes=[ob])


class Cfg:
    def __init__(self, nA, nB, nM, nC, n_layers=4, final_norm=True, debug=False, stop_after=None):
        self.nA, self.nB, self.nM, self.nC = nA, nB, nM, nC
        self.debug = debug
        self.stop_after = stop_after
        self.nKV = nA // 4
        self.n_layers = n_layers
        self.final_norm = final_norm
        g = []
        o = 0

        def add(name, n):
            nonlocal o
            g.append((name, o, n))
            o += (n + 127) // 128 * 128
        add('qA', nA * 64); add('kA', self.nKV * 64); add('gA', nA * 64)
        add('qB', nB * 128); add('gB', nB * 128); add('qM', nM * 128); add('gM', nM * 128)
        add('zf', nB * 128); add('zb', nB * 128)
        add('vA', self.nKV * 64); add('iB', nB * 128)
        self.ge = {n: (s, c) for n, s, c in g}
        self.nce = o
        g = []
        o = 0
        add('qC', nC * 128); add('kC', nC * 128); add('gC', nC * 256); add('qM', nM * 128); add('gM', nM * 128)
        add('rf', 16); add('rb', 16); add('vC', nC * 256)
        self.go = {n: (s, c) for n, s, c in g}
        self.nco = o
        self.mix_e = nA * 64 + nB * 128 + nM * 128
        self.mix_o = nC * 256 + nM * 128
        self.mixr = max(self.mix_e, self.mix_o)
        assert self.mix_e == self.mix_o
        c = {}
        o = 0
        for nm, n in (('norm', 32), ('fin', 8), ('mem', 8), ('hg', 2 * nB), ('gl', 2 * nC * 2),
                      ('lbp', 4 * nB), ('bg', 4 * nC), ('sink', 2 * nA)):
            c[nm] = o
            o += n
        self.cc = c
        self.ncc = o


def alibi_table(heads):
    s = np.arange(128)[:, None, None, None]
    j = np.arange(3)[None, :, None, None]
    t = np.arange(128)[None, None, None, :]
    dist = np.abs(t - s - (j - 1) * 128).astype(np.float64)
    slopes = np.array([2.0 ** (-8.0 * (h + 1) / 8) for h in heads])[None, None, :, None]
    e = np.exp(-slopes * dist) * (dist <= 128)
    return e.astype(np.float32)


def scan_masks():
    s = np.arange(128)[:, None]
    t = np.arange(128)[None, :]
    m = []
    for C in (64, 128):
        same = (s // C) == (t // C)
        m.append(((s <= t) & same).astype(np.float32))
        m.append(((s >= t) & same).astype(np.float32))
    m.append(np.eye(128, dtype=np.float32))
    p = np.arange(128)[:, None]
    m.append((p < 64).astype(np.float32))
    m.append((p >= 64).astype(np.float32))
    return np.concatenate(m, axis=1)


def build(cfg):
    nc = bass.Bass("TRN2", target_bir_lowering=False)
    nA, nB, nM, nC, nKV = cfg.nA, cfg.nB, cfg.nM, cfg.nC, cfg.nKV
    MIXR = cfg.mixr
    NKM = MIXR // 128

    def din(name, shape, dt=F32):
        return nc.dram_tensor(name, list(shape), dt, kind="ExternalInput").ap()

    def dscr(name, shape, dt):
        dbg = cfg.debug and name in ('mixT',)
        return nc.dram_tensor(name, list(shape), dt, kind="ExternalOutput" if dbg else "Internal").ap()

    xT_in = din('xT', [D, T])
    memT_in = din('memT', [D, 256])
    consts_in = din('consts', [128, cfg.ncc])
    etab_in = din('etab', [128, 3 * nA * 128])
    masks_in = din('masks', [128, 642])
    w_in_e = din('w_in_e', [2, cfg.nce // 128, 128, 1024])
    w_in_o = din('w_in_o', [2, cfg.nco // 128, 128, 1024])
    w_out_in = din('w_out', [4, 8, 128, NKM * 128])
    w_kv_in = din('w_kv', [4, 2 * nM, 128, 1024])
    w_up_in = din('w_up', [16, 2 * 2 * nC * 128])
    yT = nc.dram_tensor('yT', [D, T], F32, kind="ExternalOutput").ap()

    xs = [dscr('xs0', [D, T], F32), dscr('xs1', [D, T], F32)]
    mixT = dscr('mixT', [MIXR, T], BF16)
    s_qA = dscr('s_qA', [nA * 64, T], BF16)
    s_kA = dscr('s_kA', [nKV * 64, T], BF16)
    s_gA = dscr('s_gA', [nA * 64, T], BF16)
    NSH = max(nB, nC)
    s_q = dscr('s_q', [NSH * 128, T], BF16)
    s_k = dscr('s_k', [2, NSH * 128, T], BF16)
    s_lf = dscr('s_lf', [2, NSH * 128, T], F32)
    s_g = dscr('s_g', [max(nB * 128, nC * 256), T], BF16)
    s_qM = dscr('s_qM', [nM * 128, T], BF16)
    s_gM = dscr('s_gM', [nM * 128, T], BF16)
    s_vA = dscr('s_vA', [T, nKV * 64], BF16)
    s_v = dscr('s_v', [T, max(nB * 128, nC * 256)], BF16)

    with contextlib.ExitStack() as es:
        P = Prog(nc, es)
        psb = [P.wrap('ps%d' % i, p=nc.alloc_psum_tensor('ps%d' % i, [128, 512], F32)) for i in range(7)]
        pst = P.wrap('pst', p=nc.alloc_psum_tensor('pst', [128, 1024], BF16))

        CO = P.buf('CO', c=([128, cfg.ncc], F32), lb=([128, 4 * nB], F32), oml=([128, 4 * nB], F32),
                   noml=([128, 4 * nB], F32), negb=([128, 4 * nC], F32), esink=([128, 2 * nA], F32),
                   tmp=([128, 4 * nB], F32))
        MK = P.buf('MK', f=([128, 642], F32), b=([128, 642], BF16))
        ET = P.buf('ET', b=([128, 3 * nA * 128], BF16))
        ON = P.buf('ON', b=([128, 128], BF16), rst64=([128, 512], F32), rst128=([128, 512], F32))
        MEMN = P.buf('MEMN', b=([128, 8, 256], BF16))
        WUP = P.buf('WUP', b=([16, 2 * 2 * nC * 128], BF16))
        P.persist()

        cc = cfg.cc
        P.dma('sp', [(CO.c[:], consts_in)], writes=[CO])
        P.dma('sp', [(MK.f[:], masks_in)], writes=[MK])
        cp(P, 'pool', MK, MK.b[:], MK, MK.f[:])
        mset(P, 'pool', ON, ON.b[:], 1.0)
        mset(P, 'pool', ON, ON.rst64[:], 1.0)
        mset(P, 'pool', ON, ON.rst64[:].rearrange("p (c t) -> p c t", t=64)[:, :, 0:1], 0.0)
        mset(P, 'pool', ON, ON.rst128[:], 1.0)
        mset(P, 'pool', ON, ON.rst128[:].rearrange("p (c t) -> p c t", t=128)[:, :, 0:1], 0.0)
        nlb = 2 * nB
        lbp = cc['lbp']
        mset(P, 'dve', CO, CO.lb[:, 0:nlb], 0.0)
        tt(P, 'dve', CO, CO.tmp[:, 0:nlb], CO, CO.c[:, lbp:lbp + nlb], CO, CO.c[:, lbp + nlb:lbp + 2 * nlb], ALU.subtract)
        act(P, CO, CO.tmp[:, 0:nlb], CO, CO.tmp[:, 0:nlb], AF.Exp)
        ts(P, 'dve', CO, CO.tmp[:, 0:nlb], CO, CO.tmp[:, 0:nlb], 1.0, None, ALU.add)
        P.op('dve', lambda e: e.reciprocal(out=CO.lb[:, nlb:2 * nlb], in_=CO.tmp[:, 0:nlb]), reads=[CO], writes=[CO])
        ts(P, 'dve', CO, CO.oml[:], CO, CO.lb[:], -1.0, 1.0, ALU.mult, ALU.add)
        ts(P, 'dve', CO, CO.noml[:], CO, CO.oml[:], -1.0, None, ALU.mult)
        ts(P, 'dve', CO, CO.negb[:], CO, CO.c[:, cc['bg']:cc['bg'] + 4 * nC], -1.0, None, ALU.mult)
        act(P, CO, CO.esink[:], CO, CO.c[:, cc['sink']:cc['sink'] + 2 * nA], AF.Exp)

        with_stage = P.buf('PST', f=([128, 3 * nA * 128], F32))
        P.dma('sp', [(with_stage.f[:], etab_in)], writes=[with_stage])
        cp(P, 'pool', ET, ET.b[:], with_stage, with_stage.f[:])
        wu = P.buf('WUS', f=([16, 2 * 2 * nC * 128], F32))
        P.dma('sp', [(wu.f[:], w_up_in)], writes=[wu])
        cp(P, 'pool', WUP, WUP.b[:], wu, wu.f[:])
        mm_ = P.buf('MEMS', f=([128, 8, 256], F32), sq=([128, 8, 256], BF16), r=([128, 256], F32))
        P.dma('sp', [(mm_.f[:], memT_in.rearrange("(k p) t -> p k t", p=128))], writes=[mm_])
        act(P, mm_, mm_.sq[:], mm_, mm_.f[:], AF.Square)
        for k in range(8):
            mm(P, psb[0], psb[0].p[:, 0:256], ON, ON.b[:], mm_, mm_.sq[:, k, :], start=(k == 0), stop=(k == 7))
        act(P, mm_, mm_.r[:], psb[0], psb[0].p[:, 0:256], AF.Ln, scale=1.0 / D, bias=EPS)
        act(P, mm_, mm_.r[:], mm_, mm_.r[:], AF.Exp, scale=-0.5)
        for k in range(8):
            stt(P, 'dve', MEMN, MEMN.b[:, k, :], mm_, mm_.f[:, k, :], CO.c[:, cc['mem'] + k:cc['mem'] + k + 1],
                mm_, mm_.r[:], ALU.mult, ALU.mult, extra=[CO])
        P.end_phase()

        def phase_op1(l, HT):
            last = (l == cfg.n_layers)
            src = xT_in if l <= 1 else xs[(l - 1) % 2]
            dst = xs[l % 2]
            if l > 0:
                WO = P.buf('WO', b=([128, NKM, D], BF16))
                wst = P.ring(2, 'wost', f=([128, NKM, 128], F32))
                for c8 in range(8):
                    st = wst.next()
                    P.dma('sp', [(st.f[:].rearrange("p k c -> p (k c)"), w_out_in[l - 1, c8])], writes=[st])
                    cp(P, 'pool', WO, WO.b[:, :, c8 * 128:(c8 + 1) * 128], st, st.f[:])
                mixr = P.ring(2, 'mixb', b=([128, NKM, 512], BF16))
            xr = P.ring(2, 'xb', f=([128, 8, 512], F32))
            sqr = P.ring(2, 'sqb', b=([128, 8, 512], BF16))
            rr = P.ring(2, 'rstd', f=([128, 512], F32))
            yr = P.ring(2, 'yb', f=([128, 8, 512], F32)) if last else None
            psi = 0
            for tb in range(NBLK):
                cs = slice(tb * 512, (tb + 1) * 512)
                X = xr.next()
                P.dma('sp', [(X.f[:], src.rearrange("(k p) t -> p k t", p=128)[:, :, cs])], writes=[X])
                if l > 0:
                    MX = mixr.next()
                    P.dma('sp', [(MX.b[:], mixT.rearrange("(k p) t -> p k t", p=128)[:, :, cs])], writes=[MX])
                    for c8 in range(8):
                        ps = psb[psi % 4]
                        psi += 1
                        for k in range(NKM):
                            mm(P, ps, ps.p[:], WO, WO.b[:, k, c8 * 128:(c8 + 1) * 128], MX, MX.b[:, k, :],
                               start=(k == 0), stop=(k == NKM - 1))
                        tt(P, 'dve', X, X.f[:, c8, :], ps, ps.p[:], X, X.f[:, c8, :], ALU.add)
                    if not last:
                        P.dma('pool', [(dst.rearrange("(k p) t -> p k t", p=128)[:, :, cs], X.f[:])], reads=[X])
                if last and not cfg.final_norm:
                    P.dma('pool', [(yT.rearrange("(k p) t -> p k t", p=128)[:, :, cs], X.f[:])], reads=[X])
                    continue
                SQ = sqr.next()
                act(P, SQ, SQ.b[:], X, X.f[:], AF.Square)
                pn = psb[4 + tb % 2]
                for k in range(8):
                    mm(P, pn, pn.p[:], ON, ON.b[:], SQ, SQ.b[:, k, :], start=(k == 0), stop=(k == 7))
                R = rr.next()
                act(P, R, R.f[:], pn, pn.p[:], AF.Ln, scale=1.0 / D, bias=EPS)
                act(P, R, R.f[:], R, R.f[:], AF.Exp, scale=-0.5)
                if not last:
                    gcol = cc['norm'] + l * 8
                    for k in range(8):
                        stt(P, 'dve', HT, HT.b[:, k, cs], X, X.f[:, k, :],
                            CO.c[:, gcol + k:gcol + k + 1], R, R.f[:], ALU.mult, ALU.mult, extra=[CO])
                else:
                    Y = yr.next()
                    gcol = cc['fin']
                    for k in range(8):
                        stt(P, 'dve', Y, Y.f[:, k, :], X, X.f[:, k, :],
                            CO.c[:, gcol + k:gcol + k + 1], R, R.f[:], ALU.mult, ALU.mult, extra=[CO])
                    P.dma('pool', [(yT.rearrange("(k p) t -> p k t", p=128)[:, :, cs], Y.f[:])], reads=[Y])

        def phase_p2(l, HT):
            even = (l % 2 == 0)
            i2 = l // 2
            wsrc = (w_in_e if even else w_in_o)[i2]
            G = cfg.ge if even else cfg.go
            wst = P.ring(3, 'wst', f=([128, 8, 128], F32))
            wbr = P.ring(3, 'wb', b=([128, 8, 128], BF16))
            ob = P.ring(4, 'ob', b=([128, 512], BF16))
            of = P.ring(4, 'of', f=([128, 512], F32))
            of2 = P.ring(3, 'of2', f=([128, 512], F32))
            psr = Ring(psb[0:4])
            psu = Ring(psb[4:7])
            rfr = P.ring(2, 'rfb', b=([16, 512], BF16))

            def load_w(c0, ncol):
                st = wst.next()
                assert c0 % 128 == 0
                P.dma('sp', [(st.f[:].rearrange("p k c -> p (k c)"), wsrc[c0 // 128])], writes=[st])
                wb = wbr.next()
                cp(P, 'pool', wb, wb.b[:, :, 0:ncol], st, st.f[:, :, 0:ncol])
                return wb

            def proj(wb, ncol, tb):
                ps = psr.next()
                for k in range(8):
                    mm(P, ps, ps.p[0:ncol, :], wb, wb.b[:, k, 0:ncol], HT, HT.b[:, k, tb * 512:(tb + 1) * 512],
                       start=(k == 0), stop=(k == 7))
                return ps

            def fgroup(name, kind, dst, scale=1.0, **kw):
                c0, n = G[name]
                r = 0
                while r < n:
                    ncol = min(128, n - r)
                    wb = load_w(c0 + r, ncol)
                    for tb in range(NBLK):
                        cs = slice(tb * 512, (tb + 1) * 512)
                        ps = proj(wb, ncol, tb)
                        if kind in ('copy', 'silu'):
                            O = ob.next()
                            act(P, O, O.b[0:ncol, :], ps, ps.p[0:ncol, :], AF.Copy if kind == 'copy' else AF.Silu,
                                scale=scale)
                            P.dma('act', [(dst[r:r + ncol, cs], O.b[0:ncol, :])], reads=[O])
                        elif kind == 'hz':
                            d = kw['d']
                            h = r // 128
                            col = i2 * 2 * nB + d * nB + h
                            E = of.next()
                            act(P, E, E.f[:], ps, ps.p[:], AF.Exp, scale=-1.0)
                            act(P, E, E.f[:], E, E.f[:], AF.Ln, scale=1.0, bias=1.0)
                            act(P, E, E.f[:], E, E.f[:], AF.Exp, scale=-1.0)
                            L = of2.next()
                            act(P, L, L.f[:], E, E.f[:], AF.Ln, scale=CO.oml[:, col:col + 1], bias=CO.lb[:, col:col + 1],
                                extra=[CO])
                            ts(P, 'dve', L, L.f[:], L, L.f[:], float(np.log(1e-30)), None, ALU.max)
                            P.dma('sp', [(s_lf[d, r:r + 128, cs], L.f[:])], reads=[L])
                            O = ob.next()
                            ts(P, 'dve', O, O.b[:], E, E.f[:], CO.noml[:, col:col + 1], CO.oml[:, col:col + 1],
                               ALU.mult, ALU.add, extra=[CO])
                            P.dma('sp', [(s_k[d, r:r + 128, cs], O.b[:])], reads=[O])
                        elif kind == 'gr':
                            d = kw['d']
                            RF = rfr.next()
                            act(P, RF, RF.b[:], ps, ps.p[0:16, :], AF.Copy)
                            for hc in range(nC):
                                col = i2 * 2 * nC + d * nC + hc
                                wcol = ((i2 * 2 + d) * nC + hc) * 128
                                pu = psu.next()
                                mm(P, pu, pu.p[:], WUP, WUP.b[:, wcol:wcol + 128], RF, RF.b[:])
                                E = of.next()
                                act(P, E, E.f[:], pu, pu.p[:], AF.Exp, scale=-1.0, bias=CO.negb[:, col:col + 1], extra=[CO])
                                act(P, E, E.f[:], E, E.f[:], AF.Ln, scale=1.0, bias=1.0)
                                L = of2.next()
                                ts(P, 'pool', L, L.f[:], E, E.f[:], -1.0 / 16.0, None, ALU.mult)
                                P.dma('sp', [(s_lf[d, hc * 128:(hc + 1) * 128, cs], L.f[:])], reads=[L])
                    r += ncol

            def tgroup(name, dst):
                c0, n = G[name]
                WT = P.buf('WT', b=([128, 8, n], BF16))
                r = 0
                while r < n:
                    st = wst.next()
                    P.dma('sp', [(st.f[:].rearrange("p k c -> p (k c)"), wsrc[(c0 + r) // 128])], writes=[st])
                    nn = min(128, n - r)
                    cp(P, 'pool', WT, WT.b[:, :, r:r + nn], st, st.f[:, :, 0:nn])
                    r += 128
                for tti in range(NTILE):
                    r = 0
                    while r < n:
                        ncol = min(512, n - r)
                        ps = psr.next()
                        for k in range(8):
                            mm(P, ps, ps.p[:, 0:ncol], HT, HT.b[:, k, tti * 128:(tti + 1) * 128], WT, WT.b[:, k, r:r + ncol],
                               start=(k == 0), stop=(k == 7))
                        O = ob.next()
                        act(P, O, O.b[:, 0:ncol], ps, ps.p[:, 0:ncol], AF.Copy)
                        P.dma('act', [(dst[tti * 128:(tti + 1) * 128, r:r + ncol], O.b[:, 0:ncol])], reads=[O])
                        r += ncol

            if even:
                fgroup('qA', 'copy', s_qA, scale=0.125)
                fgroup('kA', 'copy', s_kA)
                fgroup('qM', 'copy', s_qM)
                fgroup('gA', 'silu', s_gA)
                fgroup('qB', 'silu', s_q)
                fgroup('gB', 'silu', s_g)
                fgroup('gM', 'silu', s_gM)
                fgroup('zf', 'hz', None, d=0)
                fgroup('zb', 'hz', None, d=1)
                tgroup('vA', s_vA)
                tgroup('iB', s_v)
            else:
                fgroup('qC', 'copy', s_q, scale=float(128 ** -0.5))
                fgroup('kC', 'copy', s_k[0])
                fgroup('qM', 'copy', s_qM)
                fgroup('gC', 'silu', s_g)
                fgroup('gM', 'silu', s_gM)
                fgroup('rf', 'gr', None, d=0)
                fgroup('rb', 'gr', None, d=1)
                tgroup('vC', s_v)

        def phase_kv(l):
            KV = P.buf('KV', k=([128, nM, 256], BF16), v=([128, 2, nM * 128], BF16))
            WK = P.buf('WK', b=([128, 8, 2 * nM * 128], BF16))
            wst = P.ring(2, 'kvst', f=([128, 8, 128], F32))
            for c in range(2 * nM):
                st = wst.next()
                P.dma('sp', [(st.f[:].rearrange("p k c -> p (k c)"), w_kv_in[l, c])], writes=[st])
                cp(P, 'pool', WK, WK.b[:, :, c * 128:(c + 1) * 128], st, st.f[:])
            for h in range(nM):
                ps = psb[h % 2]
                for k in range(8):
                    mm(P, ps, ps.p[:, 0:256], WK, WK.b[:, k, h * 128:(h + 1) * 128], MEMN, MEMN.b[:, k, :],
                       start=(k == 0), stop=(k == 7))
                act(P, KV, KV.k[:, h, :], ps, ps.p[:, 0:256], AF.Copy)
            for sc in range(2):
                ps = psb[2 + sc]
                for k in range(8):
                    mm(P, ps, ps.p[:, 0:nM * 128], MEMN, MEMN.b[:, k, sc * 128:(sc + 1) * 128],
                       WK, WK.b[:, k, nM * 128:2 * nM * 128], start=(k == 0), stop=(k == 7))
                act(P, KV, KV.v[:, sc, :], ps, ps.p[:, 0:nM * 128], AF.Copy)
            return KV

        def phase_mem(KV, row0):
            qr = P.ring(3, 'mq', q=([128, 512], BF16), g=([128, 512], BF16))
            pr = P.ring(3, 'mp', b=([128, 2, 512], BF16))
            rr = P.ring(2, 'mr', f=([128, 512], F32), o=([128, 512], F32))
            orr = P.ring(3, 'mo', b=([128, 512], BF16))
            scale = float(128 ** -0.5)
            it = 0
            for tb in range(NBLK):
                cs = slice(tb * 512, (tb + 1) * 512)
                for h in range(nM):
                    Q = qr.next()
                    P.dma('sp', [(Q.q[:], s_qM[h * 128:(h + 1) * 128, cs]), (Q.g[:], s_gM[h * 128:(h + 1) * 128, cs])],
                          writes=[Q])
                    PB = pr.next()
                    for sc in range(2):
                        ps = psb[(it * 2 + sc) % 4]
                        mm(P, ps, ps.p[:], KV, KV.k[:, h, sc * 128:(sc + 1) * 128], Q, Q.q[:])
                        act(P, PB, PB.b[:, sc, :], ps, ps.p[:], AF.Exp, scale=scale)
                    po = psb[4 + it % 2]
                    pd = psb[6]
                    for sc in range(2):
                        mm(P, po, po.p[:], KV, KV.v[:, sc, h * 128:(h + 1) * 128], PB, PB.b[:, sc, :],
                           start=(sc == 0), stop=(sc == 1))
                    for sc in range(2):
                        mm(P, pd, pd.p[:], ON, ON.b[:], PB, PB.b[:, sc, :], start=(sc == 0), stop=(sc == 1))
                    R = rr.next()
                    act(P, R, R.f[:], pd, pd.p[:], AF.Ln)
                    act(P, R, R.f[:], R, R.f[:], AF.Exp, scale=-1.0)
                    tt(P, 'dve', R, R.o[:], po, po.p[:], R, R.f[:], ALU.mult)
                    O = orr.next()
                    tt(P, 'pool', O, O.b[:], R, R.o[:], Q, Q.g[:], ALU.mult)
                    P.dma('pool', [(mixT[row0 + h * 128:row0 + (h + 1) * 128, cs], O.b[:])], reads=[O])
                    it += 1

        def phase_attn(i2, row0):
            lr = P.ring(2, 'aq', q=([64, 4, 512], BF16), g=([64, 4, 512], BF16), k=([64, 768], BF16),
                        v=([128, 6, 64], BF16))
            pr = P.ring(4, 'ap', b=([128, 512], BF16))
            dr = P.ring(2, 'ad', f=([64, 512], F32), o=([64, 512], F32))
            orr = P.ring(2, 'ao', b=([64, 4, 512], BF16))
            pss = Ring(psb[0:4])
            it = 0
            for tb in range(NBLK):
                cs = slice(tb * 512, (tb + 1) * 512)
                for n in range(nKV):
                    L = lr.next()
                    k0 = max(0, tb * 512 - 128)
                    k1 = min(T, tb * 512 + 640)
                    ko = k0 - (tb * 512 - 128)
                    t0 = max(0, tb * 4 - 1)
                    t1 = min(NTILE, tb * 4 + 5)
                    to = t0 - (tb * 4 - 1)
                    P.dma('sp', [
                        (L.q[:], s_qA[n * 256:(n + 1) * 256, cs].rearrange("(g d) t -> d g t", d=64)),
                        (L.g[:], s_gA[n * 256:(n + 1) * 256, cs].rearrange("(g d) t -> d g t", d=64)),
                        (L.k[:, ko:ko + (k1 - k0)], s_kA[n * 64:(n + 1) * 64, k0:k1]),
                        (L.v[:, to:to + (t1 - t0), :],
                         s_vA[t0 * 128:t1 * 128, n * 64:(n + 1) * 64].rearrange("(j p) c -> p j c", p=128)),
                    ], writes=[L])
                    O = orr.next()
                    for qt in range(4):
                        c = tb * 4 + qt
                        js = [j for j in range(3) if 0 <= c - 1 + j < NTILE]
                        PBs = []
                        for j in js:
                            ps = pss.next()
                            kc = (qt + j) * 128
                            mm(P, ps, ps.p[:].rearrange("p (g t) -> p g t", g=4), L, L.k[:, kc:kc + 128], L, L.q[:, :, qt * 128:(qt + 1) * 128])
                            PB = pr.next()
                            act(P, PB, PB.b[:], ps, ps.p[:], AF.Exp)
                            e0 = (j * nA + n * 4) * 128
                            tt(P, 'pool' if j == 1 else 'dve', PB, PB.b[:], PB, PB.b[:], ET, ET.b[:, e0:e0 + 512], ALU.mult)
                            PBs.append((j, PB))
                        po = psb[4 + it % 2]
                        pd = psb[6]
                        for ii, (j, PB) in enumerate(PBs):
                            mm(P, po, po.p[0:64, :], L, L.v[:, qt + j, :], PB, PB.b[:],
                               start=(ii == 0), stop=(ii == len(PBs) - 1))
                        for ii, (j, PB) in enumerate(PBs):
                            mm(P, pd, pd.p[0:64, :], ON, ON.b[:, 0:64], PB, PB.b[:],
                               start=(ii == 0), stop=(ii == len(PBs) - 1))
                        Dn = dr.next()
                        sk = i2 * nA + n * 4
                        tt(P, 'dve', Dn, Dn.f[:].rearrange("p (g t) -> p g t", g=4),
                           pd, pd.p[0:64, :].rearrange("p (g t) -> p g t", g=4),
                           CO, CO.esink[0:64, sk:sk + 4].unsqueeze(2).to_broadcast([64, 4, 128]), ALU.add)
                        act(P, Dn, Dn.f[:], Dn, Dn.f[:], AF.Ln)
                        act(P, Dn, Dn.f[:], Dn, Dn.f[:], AF.Exp, scale=-1.0)
                        tt(P, 'dve', Dn, Dn.o[:], po, po.p[0:64, :], Dn, Dn.f[:], ALU.mult)
                        tt(P, 'pool', O, O.b[:, :, qt * 128:(qt + 1) * 128],
                           Dn, Dn.o[:].rearrange("p (g t) -> p g t", g=4),
                           L, L.g[:, :, qt * 128:(qt + 1) * 128], ALU.mult)
                        it += 1
                    P.dma('pool', [(mixT[row0 + n * 256:row0 + (n + 1) * 256, cs].rearrange("(g d) t -> d g t", d=64),
                                    O.b[:])], reads=[O])

        def phase_scan(nh, dv, C, kdirs, gcol0, row0):
            nch = T // C
            ncb = 512 // C
            wv = 512 // ncb
            nr = dv // wv
            nvh = dv // 128
            mcol = 0 if C == 64 else 256
            rst = ON.rst64 if C == 64 else ON.rst128
            QT = [P.buf('QT%d' % d, b=([128, T], BF16)) for d in range(2)]
            AT = [P.buf('AT%d' % d, b=([128, NTILE, 128], BF16)) for d in range(2)]
            SIN = [P.buf('SIN%d' % d, b=([128, nch, dv], BF16)) for d in range(2)]
            DS = P.buf('DS', f=([128, dv, nch], F32))
            DA = P.buf('DA', f=([128, nch], F32))
            DR = P.buf('DR', f=([128, 32, nch], F32))
            SF = P.ring(2, 'SF', f=([128, 32, nch], F32))
            ld = P.ring(2, 'sl', lf=([128, 512], F32), k=([128, 512], BF16), q=([128, 512], BF16),
                        v=([128, 4, dv], BF16))
            w1 = P.ring(2, 'sw1', pre=([128, 512], F32), b=([128, 512], F32))
            w2 = P.ring(2, 'sw2', eb=([128, 512], F32), enb=([128, 512], F32), ec=([128, 512], F32))
            w3 = P.ring(2, 'sw3', kt=([128, 512], BF16), kh=([128, 512], BF16), khT=([128, 2, 4, 128], BF16))
            l3 = P.ring(2, 'sl3', v=([128, 4, dv], BF16), g=([128, nvh, 512], BF16))
            w4 = P.ring(2, 'sw4', sq=([128, nvh, 512], BF16), r=([128, 512], F32), m=([128, nvh, 512], F32))
            w5 = P.ring(2, 'sw5', b=([128, nvh, 512], BF16))
            it = 0
            import os
            LIM = int(os.environ.get('SCAN_LIMIT', '99'))
            for h in range(nh):
                hr = slice(h * 128, (h + 1) * 128)
                for d in range(2):
                    kd = d if kdirs else 0
                    for tb in range(NBLK):
                        cs = slice(tb * 512, (tb + 1) * 512)
                        L = ld.next()
                        P.dma('sp', [
                            (L.lf[:], s_lf[d, hr, cs]), (L.k[:], s_k[kd, hr, cs]), (L.q[:], s_q[hr, cs]),
                            (L.v[:], s_v[cs, h * dv:(h + 1) * dv].rearrange("(j p) c -> p j c", p=128)),
                        ], writes=[L])
                        W1 = w1.next()
                        P.op('dve', lambda e, W1=W1, L=L: e.tensor_tensor_scan(
                            out=W1.pre[:], data0=rst[:], data1=L.lf[:], initial=0.0, op0=ALU.mult, op1=ALU.add),
                            reads=[L, ON], writes=[W1])
                        pre3 = W1.pre[:].rearrange("p (c t) -> p c t", t=C)
                        b3 = W1.b[:].rearrange("p (c t) -> p c t", t=C)
                        if d == 0:
                            bsrc = W1.pre
                            edge = pre3[:, :, C - 1:C]
                        else:
                            tt(P, 'pool', W1, W1.b[:], L, L.lf[:], W1, W1.pre[:], ALU.subtract)
                            tt(P, 'dve', W1, b3, W1, b3, W1, pre3[:, :, C - 1:C].to_broadcast([128, ncb, C]), ALU.add)
                            bsrc = W1.b
                            edge = b3[:, :, 0:1]
                        W2 = w2.next()
                        act(P, W2, W2.eb[:], W1, bsrc[:], AF.Exp)
                        ts(P, 'dve', W2, W2.ec[:], W1, bsrc[:], -80.0, None, ALU.max)
                        act(P, W2, W2.enb[:], W2, W2.ec[:], AF.Exp, scale=-1.0)
                        tt(P, 'dve', W2, W2.ec[:].rearrange("p (c t) -> p c t", t=C), W1,
                           edge.to_broadcast([128, ncb, C]), W1, bsrc[:].rearrange("p (c t) -> p c t", t=C), ALU.subtract)
                        act(P, W2, W2.ec[:], W2, W2.ec[:], AF.Exp)
                        if d == 0:
                            j0 = tb * ncb
                            act(P, DA, DA.f[:, j0:j0 + ncb], W1, edge.rearrange("p c o -> p (c o)"), AF.Exp)
                        else:
                            for cl in range(ncb):
                                j = nch - 1 - (tb * ncb + cl)
                                act(P, DA, DA.f[:, j:j + 1], W1, b3[:, cl, 0:1], AF.Exp)
                        if LIM <= 0:
                            return
                        tt(P, 'dve', QT[d], QT[d].b[:, cs], L, L.q[:], W2, W2.eb[:], ALU.mult)
                        W3 = w3.next()
                        tt(P, 'dve', W3, W3.kt[:], L, L.k[:], W2, W2.enb[:], ALU.mult)
                        tt(P, 'dve', W3, W3.kh[:], L, L.k[:], W2, W2.ec[:], ALU.mult)
                        if LIM <= 1:
                            return
                        pa = psb[it % 2]
                        for i4 in range(4):
                            ss = slice(i4 * 128, (i4 + 1) * 128)
                            mm(P, pa, pa.p[:, ss], W3, W3.kt[:, ss], QT[d], QT[d].b[:, tb * 512 + i4 * 128:tb * 512 + (i4 + 1) * 128])
                        mk = MK.b[:, mcol + d * 128:mcol + (d + 1) * 128]
                        tt(P, 'dve', AT[d], AT[d].b[:, tb * 4:(tb + 1) * 4, :], pa, pa.p[:].rearrange("p (i t) -> p i t", i=4),
                           MK, mk.unsqueeze(1).to_broadcast([128, 4, 128]), ALU.mult)
                        if LIM <= 2:
                            return
                        for i4 in range(4):
                            P.op('pe', lambda e, W3=W3, i4=i4: e.transpose(pst.p[:, i4 * 128:(i4 + 1) * 128],
                                                                       W3.kh[:, i4 * 128:(i4 + 1) * 128], MK.b[:, 512:640]),
                                 reads=[W3, MK], writes=[pst])
                        if C == 64:
                            for half in range(2):
                                act(P, W3, W3.khT[:, half, :, :], pst, pst.p[:, 0:512].rearrange("p (i t) -> p i t", i=4),
                                    AF.Copy, scale=MK.f[:, 640 + half:641 + half], extra=[MK])
                        else:
                            act(P, W3, W3.khT[:, 0, :, :], pst, pst.p[:, 0:512].rearrange("p (i t) -> p i t", i=4), AF.Copy)
                        if LIM <= 3:
                            return
                        for r in range(nr):
                            pd = psb[2 + (it * nr + r) % 2]
                            for cl in range(ncb):
                                i4 = (cl * C) // 128
                                p0 = (cl * C) % 128
                                slot = cl if d == 0 else ncb - 1 - cl
                                mm(P, pd, pd.p[:, slot * wv:(slot + 1) * wv], W3, W3.khT[:, p0 // 64, i4, :],
                                   L, L.v[:, i4, r * wv:(r + 1) * wv])
                            j0 = tb * ncb if d == 0 else nch - (tb + 1) * ncb
                            cp(P, 'dve', DS, DS.f[:, r * wv:(r + 1) * wv, j0:j0 + ncb],
                               pd, pd.p[:].rearrange("p (c v) -> p v c", v=wv))
                        it += 1
                        if LIM <= 4:
                            return
                    act(P, DR, DR.f[:], DA, DA.f[:].unsqueeze(1).to_broadcast([128, 32, nch]), AF.Copy)
                    mset(P, 'pool', DR, DR.f[:, :, 0:1], 0.0)
                    mset(P, 'pool', SIN[d], SIN[d].b[:, 0, :], 0.0)
                    for v0 in range(0, dv, 32):
                        S = SF.next()
                        P.op('dve', lambda e, S=S, v0=v0: e.tensor_tensor_scan(
                            out=S.f[:].rearrange("p v c -> p (v c)"), data0=DR.f[:].rearrange("p v c -> p (v c)"),
                            data1=DS.f[:, v0:v0 + 32, :].rearrange("p v c -> p (v c)"), initial=0.0,
                            op0=ALU.mult, op1=ALU.add), reads=[DR, DS], writes=[S])
                        act(P, SIN[d], SIN[d].b[:, 1:nch, v0:v0 + 32],
                            S, S.f[:, :, 0:nch - 1].rearrange("p v c -> p c v"), AF.Copy)
                if LIM <= 5:
                    return
                for tb in range(NBLK):
                    cs = slice(tb * 512, (tb + 1) * 512)
                    L3 = l3.next()
                    P.dma('sp', [
                        (L3.v[:], s_v[cs, h * dv:(h + 1) * dv].rearrange("(j p) c -> p j c", p=128)),
                        (L3.g[:], s_g[h * dv:(h + 1) * dv, cs].rearrange("(a p) t -> p a t", p=128)),
                    ], writes=[L3])
                    W4 = w4.next()
                    pos = []
                    for vh in range(nvh):
                        po = psb[4 + vh]
                        pos.append(po)
                        vs = slice(vh * 128, (vh + 1) * 128)
                        for i4 in range(4):
                            ti = tb * 4 + i4
                            ts_ = slice(i4 * 128, (i4 + 1) * 128)
                            mm(P, po, po.p[:, ts_], L3, L3.v[:, i4, vs], AT[0], AT[0].b[:, ti, :], start=True, stop=False)
                            mm(P, po, po.p[:, ts_], L3, L3.v[:, i4, vs], AT[1], AT[1].b[:, ti, :], start=False, stop=False)
                            nci = 128 // C
                            for ci in range(nci):
                                cg = ti * nci + ci
                                tcs = slice(i4 * 128 + ci * C, i4 * 128 + (ci + 1) * C)
                                gcs = slice(ti * 128 + ci * C, ti * 128 + (ci + 1) * C)
                                mm(P, po, po.p[:, tcs], SIN[0], SIN[0].b[:, cg, vs], QT[0], QT[0].b[:, gcs],
                                   start=False, stop=False)
                                mm(P, po, po.p[:, tcs], SIN[1], SIN[1].b[:, nch - 1 - cg, vs], QT[1], QT[1].b[:, gcs],
                                   start=False, stop=(ci == nci - 1))
                        act(P, W4, W4.sq[:, vh, :], po, po.p[:], AF.Square)
                    pn = psb[6]
                    for vh in range(nvh):
                        mm(P, pn, pn.p[:], ON, ON.b[:], W4, W4.sq[:, vh, :], start=(vh == 0), stop=(vh == nvh - 1))
                    act(P, W4, W4.r[:], pn, pn.p[:], AF.Ln, scale=1.0 / dv, bias=EPS)
                    act(P, W4, W4.r[:], W4, W4.r[:], AF.Exp, scale=-0.5)
                    O = w5.next()
                    for vh in range(nvh):
                        tt(P, 'dve', W4, W4.m[:, vh, :], pos[vh], pos[vh].p[:], W4, W4.r[:], ALU.mult)
                        gc = gcol0 + h * nvh + vh
                        stt(P, 'dve', O, O.b[:, vh, :], W4, W4.m[:, vh, :], CO.c[:, gc:gc + 1], L3, L3.g[:, vh, :],
                            ALU.mult, ALU.mult, extra=[CO])
                    P.dma('sp', [(mixT[row0 + h * dv:row0 + (h + 1) * dv, cs].rearrange("(a p) t -> p a t", p=128),
                                    O.b[:])], reads=[O])
                    if LIM <= 6:
                        return

        stop = cfg.stop_after
        import os
        if os.environ.get('ONLY_SCAN'):
            phase_scan(nB, 128, 64, True, cc['hg'], nA * 64)
            P.end_phase()
            P.emit()
            return nc
        for l in range(cfg.n_layers):
            lastl = (l == cfg.n_layers - 1)
            HT = P.buf('HT', b=([128, 8, T], BF16))
            mark = P.sb_off
            phase_op1(l, HT)
            P.end_phase(keep=mark)
            if lastl and stop == 'op1':
                break
            phase_p2(l, HT)
            P.end_phase()
            if lastl and stop == 'p2':
                break
            i2 = l // 2
            if not os.environ.get('SKIP_MEM'):
                KV = phase_kv(l)
            if l % 2 == 0:
                if not os.environ.get('SKIP_MEM'):
                    phase_mem(KV, nA * 64 + nB * 128)
                    P.end_phase()
                if lastl and stop == 'mem':
                    break
                phase_scan(nB, 128, 64, True, cc['hg'] + i2 * nB, nA * 64)
                P.end_phase()
                if lastl and stop == 'scan0':
                    break
                if not os.environ.get('SKIP_ATTN'):
                    phase_attn(i2, 0)
                    P.end_phase()
            else:
                phase_mem(KV, nC * 256)
                P.end_phase()
                if lastl and stop == 'mem':
                    break
                phase_scan(nC, 256, 128, False, cc['gl'] + i2 * nC * 2, 0)
                P.end_phase()
        if stop is None:
            phase_op1(cfg.n_layers, None)
            P.end_phase()
        P.emit()
        print('ops recorded:', P.nops, {e: len(s) for e, s in P.streams.items()})
    return nc


def core_inputs(cfg, b, heads, inp):
    f32 = np.float32
    hA, hKV, hB, hM, hC = heads['A'], heads['KV'], heads['B'], heads['M'], heads['C']
    out = {}
    out['xT'] = np.ascontiguousarray(inp['x'][b].T)
    out['memT'] = np.ascontiguousarray(inp['mem'][b].T)

    def cols(base, hs, w):
        return np.concatenate([np.arange(base + h * w, base + (h + 1) * w) for h in hs])

    def padded(groups):
        sel = []
        for g in groups:
            sel.append(g)
            pad = (-len(g)) % 128
            if pad:
                sel.append(-np.ones(pad, dtype=np.int64))
        return np.concatenate(sel)

    def chunkify(w, sel):
        L, K = w.shape[0], w.shape[1]
        wp = np.zeros((L, K, len(sel)), f32)
        ok = sel >= 0
        wp[:, :, ok] = w[:, :, sel[ok]]
        nk, nch = K // 128, len(sel) // 128
        return np.ascontiguousarray(wp.reshape(L, nk, 128, nch, 128).transpose(0, 3, 2, 1, 4)).reshape(L, nch, 128, nk * 128)
    EV = np.cumsum([0, 512, 128, 128, 512, 512, 512, 512, 512, 512, 512, 512])
    names = ['qA', 'kA', 'vA', 'gA', 'qB', 'zf', 'zb', 'iB', 'gB', 'qM', 'gM']
    eo = dict(zip(names, EV[:-1]))
    sel_e = padded([
        cols(eo['qA'], hA, 64), cols(eo['kA'], hKV, 64), cols(eo['gA'], hA, 64),
        cols(eo['qB'], hB, 128), cols(eo['gB'], hB, 128), cols(eo['qM'], hM, 128), cols(eo['gM'], hM, 128),
        cols(eo['zf'], hB, 128), cols(eo['zb'], hB, 128), cols(eo['vA'], hKV, 64), cols(eo['iB'], hB, 128)])
    assert len(sel_e) == cfg.nce, (len(sel_e), cfg.nce)
    out['w_in_e'] = chunkify(inp['w_in_even'], sel_e)
    OD = np.cumsum([0, 512, 512, 1024, 1024, 16, 16, 512, 512])
    oo = dict(zip(['qC', 'kC', 'vC', 'gC', 'rf', 'rb', 'qM', 'gM'], OD[:-1]))
    sel_o = padded([
        cols(oo['qC'], hC, 128), cols(oo['kC'], hC, 128), cols(oo['gC'], hC, 256),
        cols(oo['qM'], hM, 128), cols(oo['gM'], hM, 128),
        np.arange(oo['rf'], oo['rf'] + 16), np.arange(oo['rb'], oo['rb'] + 16), cols(oo['vC'], hC, 256)])
    assert len(sel_o) == cfg.nco, (len(sel_o), cfg.nco)
    out['w_in_o'] = chunkify(inp['w_in_odd'], sel_o)
    rows_e = np.concatenate([cols(0, hA, 64), cols(512, hB, 128), cols(1024, hM, 128)])
    rows_o = np.concatenate([cols(0, hC, 256), cols(1024, hM, 128)])
    wo = np.zeros((4, cfg.mixr, D), f32)
    for l in range(4):
        if l % 2 == 0:
            wo[l] = inp['w_out_even'][l // 2][rows_e]
        else:
            wo[l] = inp['w_out_odd'][l // 2][rows_o]
    out['w_out'] = chunkify(wo, np.arange(D))
    kvc = np.concatenate([cols(0, hM, 128), cols(512, hM, 128)])
    out['w_kv'] = chunkify(inp['w_mem_kv'], kvc)
    wu = inp['w_gate_up'][:, :, :, cols(0, hC, 128)]
    out['w_up'] = np.ascontiguousarray(wu.transpose(2, 0, 1, 3).reshape(16, -1))
    cc = cfg.cc
    C = np.zeros((128, cfg.ncc), f32)
    for l in range(4):
        g = inp['norm_even'][l // 2] if l % 2 == 0 else inp['norm_odd'][l // 2]
        C[:, cc['norm'] + l * 8:cc['norm'] + (l + 1) * 8] = g.reshape(8, 128).T
    C[:, cc['fin']:cc['fin'] + 8] = inp['final_norm'].reshape(8, 128).T
    C[:, cc['mem']:cc['mem'] + 8] = inp['mem_norm'].reshape(8, 128).T
    nB, nC, nA = cfg.nB, cfg.nC, cfg.nA
    for i in range(2):
        for hi, h in enumerate(hB):
            C[:, cc['hg'] + i * nB + hi] = inp['hgrn_norm'][i, h * 128:(h + 1) * 128]
        for hi, h in enumerate(hC):
            for vh in range(2):
                C[:, cc['gl'] + i * nC * 2 + hi * 2 + vh] = inp['gla_norm'][i, h * 256 + vh * 128:h * 256 + (vh + 1) * 128]
        for d in range(2):
            for hi, h in enumerate(hB):
                C[:, cc['lbp'] + i * 2 * nB + d * nB + hi] = inp['lb_param'][i, d, h * 128:(h + 1) * 128]
            for hi, h in enumerate(hC):
                C[:, cc['bg'] + i * 2 * nC + d * nC + hi] = inp['b_gate'][i, d, h * 128:(h + 1) * 128]
        for hi, h in enumerate(hA):
            C[:, cc['sink'] + i * nA + hi] = inp['sink'][i, h]
    out['consts'] = C
    out['etab'] = np.ascontiguousarray(alibi_table(hA).reshape(128, -1))
    out['masks'] = scan_masks()
    return out


_CACHE = {}


def kernel(**inputs):
    inp = {k: np.asarray(v) for k, v in inputs.items()}
    cfg = Cfg(8, 4, 4, 4)
    heads = {'A': list(range(8)), 'KV': [0, 1], 'B': list(range(4)), 'M': list(range(4)), 'C': list(range(4))}
    if 'nc' not in _CACHE:
        _CACHE['nc'] = build(cfg)
    nc = _CACHE['nc']
    in_maps = [core_inputs(cfg, c % 4, heads, inp) for c in range(N_CORES)]
    res = run_bass_kernel_spmd(nc, in_maps, core_ids=list(range(N_CORES)))
    out = np.stack([np.ascontiguousarray(res.results[b]['yT'].T) for b in range(4)], axis=0)
    return out.astype(np.float32)
```

```python
import contextlib
import numpy as np
import concourse.bass as bass
import concourse.mybir as mybir
from concourse.bass_utils import run_bass_kernel_spmd

F32 = mybir.dt.float32
BF16 = mybir.dt.bfloat16
ALU = mybir.AluOpType
AF = mybir.ActivationFunctionType
DT_SIZE = {F32: 4, BF16: 2}

T = 4096
D = 1024
NBLK = T // 512
NTILE = T // 128
EPS = 1e-6
N_CORES = 8


class Buf:
    def __init__(self, name):
        self.name = name
        self.lw = None
        self.rd = {}
        self.dsem = None
        self.tt = {}

    def __getattr__(self, k):
        t = self.__dict__.get('tt', {})
        if k in t:
            return t[k]
        raise AttributeError(k)


class Prog:
    COMPUTE = ('pe', 'act', 'dve', 'pool')
    ENG = ('pe', 'act', 'dve', 'pool', 'sp')

    def __init__(self, nc, es, n_dsem=88, sb_limit=229000, sb_start=16640):
        self.nc = nc
        self.semh = {}
        for e in self.COMPUTE:
            self.semh[e] = es.enter_context(nc.semaphore('c_' + e))
        self.free_dsem = []
        for i in range(n_dsem):
            k = ('d', i)
            self.semh[k] = es.enter_context(nc.semaphore('d%d' % i))
            self.free_dsem.append(k)
        self.cnt = {k: 0 for k in self.semh}
        self.streams = {e: [] for e in self.ENG}
        self.waited = {e: {} for e in self.ENG}
        self.sb_off = sb_start
        self.sb_base = sb_start
        self.sb_limit = sb_limit
        self.uid = 0
        self.phase_dsems = []
        self.nops = 0

    def sbuf(self, shape, dtype, name='t'):
        self.uid += 1
        per_part = int(np.prod(shape[1:])) * DT_SIZE[dtype]
        off = (self.sb_off + 63) // 64 * 64
        assert off + per_part <= self.sb_limit, ('SBUF overflow', name, off, per_part)
        h = self.nc.alloc_sbuf_tensor_at('%s_%d' % (name, self.uid), list(shape), dtype, offset=off)
        self.sb_off = off + per_part
        return h

    def buf(self, name, **tensors):
        b = Buf(name)
        for k, (shape, dtype) in tensors.items():
            b.tt[k] = self.sbuf(shape, dtype, name + '_' + k)
        return b

    def ring(self, n, name, **tensors):
        return Ring([self.buf('%s%d' % (name, i), **tensors) for i in range(n)])

    def wrap(self, name, **handles):
        b = Buf(name)
        b.tt.update(handles)
        return b

    def need_dsem(self, b):
        if b.dsem is None:
            b.dsem = self.free_dsem.pop()
            self.phase_dsems.append(b.dsem)
        return b.dsem

    def persist(self):
        self.sb_base = self.sb_off
        self.phase_dsems = []

    def _deps(self, eng, reads, writes, is_dma):
        deps = {}

        def need(tok):
            if tok is None:
                return
            k, v = tok
            if deps.get(k, 0) < v:
                deps[k] = v
        for b in reads:
            need(b.lw)
        for b in writes:
            if b.lw is not None and (is_dma or b.lw[0] != eng):
                need(b.lw)
            for k, v in b.rd.items():
                if is_dma or k != eng:
                    need((k, v))
        w = self.waited[eng]
        out = []
        for k, v in deps.items():
            if w.get(k, 0) < v:
                w[k] = v
                out.append((k, v))
        return out

    def _commit(self, tok, reads, writes):
        for b in writes:
            b.lw = tok
            b.rd = {}
        k, v = tok
        for b in reads:
            if b in writes:
                continue
            if b.rd.get(k, 0) < v:
                b.rd[k] = v

    def op(self, eng, fn, reads=(), writes=()):
        waits = self._deps(eng, reads, writes, False)
        self.cnt[eng] += 1
        tok = (eng, self.cnt[eng])
        self.streams[eng].append((waits, fn, eng, 1))
        self._commit(tok, reads, writes)
        self.nops += 1

    def dma(self, queue, pairs, reads=(), writes=()):
        onchip = list(writes) + list(reads)
        key = self.need_dsem(onchip[0])
        for b in onchip[1:]:
            assert b.dsem is None or b.dsem == key
            b.dsem = key
        waits = self._deps(queue, reads, writes, True)
        first = True
        for (o, i) in pairs:
            self.cnt[key] += 16
            self.streams[queue].append((waits if first else [],
                                        (lambda e, o=o, i=i: e.dma_start(out=o, in_=i)), key, 16))
            first = False
            self.nops += 1
        self._commit((key, self.cnt[key]), reads, writes)

    def barrier(self):
        for e in self.ENG:
            waits = []
            w = self.waited[e]
            for k, v in self.cnt.items():
                if v > 0 and w.get(k, 0) < v and k != e:
                    w[k] = v
                    waits.append((k, v))
            if waits:
                self.streams[e].append((waits, None, None, 0))

    def end_phase(self, keep=None):
        self.barrier()
        self.free_dsem.extend(self.phase_dsems)
        self.phase_dsems = []
        self.sb_off = self.sb_base if keep is None else keep

    def emit(self):
        nc = self.nc
        semh = self.semh
        streams = self.streams
        with nc.Block() as block:
            def body(name):
                def f(e):
                    for (waits, fn, key, inc) in streams[name]:
                        for (k, v) in waits:
                            e.wait_ge(semh[k], v)
                        if fn is not None:
                            fn(e).then_inc(semh[key], inc)
                return f
            block.tensor(body('pe'))
            block.scalar(body('act'))
            block.vector(body('dve'))
            block.gpsimd(body('pool'))
            block.sync(body('sp'))


class Ring:
    def __init__(self, bufs):
        self.bufs = bufs
        self.i = 0

    def next(self):
        b = self.bufs[self.i % len(self.bufs)]
        self.i += 1
        return b


def mm(P, ob, o, lb, l, rb, r, start=True, stop=True):
    P.op('pe', lambda e: e.matmul(o, lhsT=l, rhs=r, start=start, stop=stop), reads=[lb, rb], writes=[ob])


def act(P, ob, o, ib, i, func, scale=1.0, bias=0.0, extra=()):
    P.op('act', lambda e: e.activation(out=o, in_=i, func=func, scale=scale, bias=bias),
         reads=[ib] + list(extra), writes=[ob])


def tt(P, eng, ob, o, ab, a, bb, b, op):
    P.op(eng, lambda e: e.tensor_tensor(out=o, in0=a, in1=b, op=op), reads=[ab, bb], writes=[ob])


def ts(P, eng, ob, o, ab, a, s1, s2, op0, op1=None, extra=()):
    if op1 is None:
        P.op(eng, lambda e: e.tensor_scalar(out=o, in0=a, scalar1=s1, scalar2=None, op0=op0),
             reads=[ab] + list(extra), writes=[ob])
    else:
        P.op(eng, lambda e: e.tensor_scalar(out=o, in0=a, scalar1=s1, scalar2=s2, op0=op0, op1=op1),
             reads=[ab] + list(extra), writes=[ob])


def stt(P, eng, ob, o, ab, a, scalar, bb, b, op0, op1, extra=()):
    P.op(eng, lambda e: e.scalar_tensor_tensor(out=o, in0=a, scalar=scalar, in1=b, op0=op0, op1=op1),
         reads=[ab, bb] + list(extra), writes=[ob])


def cp(P, eng, ob, o, ib, i):
    P.op(eng, lambda e: e.tensor_copy(out=o, in_=i), reads=[ib], writes=[ob])


def mset(P, eng, ob, o, val):
    P.op(eng, lambda e: e.memset(o, val), writes=[ob])


class Cfg:
    def __init__(self, nA, nB, nM, nC, n_layers=4, final_norm=True, debug=False, stop_after=None):
        self.nA, self.nB, self.nM, self.nC = nA, nB, nM, nC
        self.debug = debug
        self.stop_after = stop_after
        self.nKV = nA // 4
        self.n_layers = n_layers
        self.final_norm = final_norm
        g = []
        o = 0

        def add(name, n):
            nonlocal o
            g.append((name, o, n))
            o += (n + 127) // 128 * 128
        add('qA', nA * 64); add('kA', self.nKV * 64); add('gA', nA * 64)
        add('qB', nB * 128); add('gB', nB * 128); add('qM', nM * 128); add('gM', nM * 128)
        add('zf', nB * 128); add('zb', nB * 128)
        add('vA', self.nKV * 64); add('iB', nB * 128)
        self.ge = {n: (s, c) for n, s, c in g}
        self.nce = o
        g = []
        o = 0
        add('qC', nC * 128); add('kC', nC * 128); add('gC', nC * 256); add('qM', nM * 128); add('gM', nM * 128)
        add('rf', 16); add('rb', 16); add('vC', nC * 256)
        self.go = {n: (s, c) for n, s, c in g}
        self.nco = o
        self.mix_e = nA * 64 + nB * 128 + nM * 128
        self.mix_o = nC * 256 + nM * 128
        self.mixr = max(self.mix_e, self.mix_o)
        assert self.mix_e == self.mix_o
        c = {}
        o = 0
        for nm, n in (('norm', 32), ('fin', 8), ('mem', 8), ('hg', 2 * nB), ('gl', 2 * nC * 2),
                      ('lbp', 4 * nB), ('bg', 4 * nC), ('sink', 2 * nA)):
            c[nm] = o
            o += n
        self.cc = c
        self.ncc = o


def alibi_table(heads):
    s = np.arange(128)[:, None, None, None]
    j = np.arange(3)[None, :, None, None]
    t = np.arange(128)[None, None, None, :]
    dist = np.abs(t - s - (j - 1) * 128).astype(np.float64)
    slopes = np.array([2.0 ** (-8.0 * (h + 1) / 8) for h in heads])[None, None, :, None]
    e = np.exp(-slopes * dist) * (dist <= 128)
    return e.astype(np.float32)


def scan_masks():
    s = np.arange(128)[:, None]
    t = np.arange(128)[None, :]
    m = []
    for C in (64, 128):
        same = (s // C) == (t // C)
        m.append(((s <= t) & same).astype(np.float32))
        m.append(((s >= t) & same).astype(np.float32))
    m.append(np.eye(128, dtype=np.float32))
    p = np.arange(128)[:, None]
    m.append((p < 64).astype(np.float32))
    m.append((p >= 64).astype(np.float32))
    return np.concatenate(m, axis=1)


def build(cfg):
    nc = bass.Bass("TRN2", target_bir_lowering=False)
    nA, nB, nM, nC, nKV = cfg.nA, cfg.nB, cfg.nM, cfg.nC, cfg.nKV
    MIXR = cfg.mixr
    NKM = MIXR // 128

    def din(name, shape, dt=F32):
        return nc.dram_tensor(name, list(shape), dt, kind="ExternalInput").ap()

    def dscr(name, shape, dt):
        dbg = cfg.debug and name in ('mixT',)
        return nc.dram_tensor(name, list(shape), dt, kind="ExternalOutput" if dbg else "Internal").ap()

    xT_in = din('xT', [D, T])
    memT_in = din('memT', [D, 256])
    consts_in = din('consts', [128, cfg.ncc])
    etab_in = din('etab', [128, 3 * nA * 128])
    masks_in = din('masks', [128, 642])
    w_in_e = din('w_in_e', [2, cfg.nce // 128, 128, 1024])
    w_in_o = din('w_in_o', [2, cfg.nco // 128, 128, 1024])
    w_out_in = din('w_out', [4, 8, 128, NKM * 128])
    w_kv_in = din('w_kv', [4, 2 * nM, 128, 1024])
    w_up_in = din('w_up', [16, 2 * 2 * nC * 128])
    yT = nc.dram_tensor('yT', [D, T], F32, kind="ExternalOutput").ap()

    xs = [dscr('xs0', [D, T], F32), dscr('xs1', [D, T], F32)]
    mixT = dscr('mixT', [MIXR, T], BF16)
    s_qA = dscr('s_qA', [nA * 64, T], BF16)
    s_kA = dscr('s_kA', [nKV * 64, T], BF16)
    s_gA = dscr('s_gA', [nA * 64, T], BF16)
    NSH = max(nB, nC)
    s_q = dscr('s_q', [NSH * 128, T], BF16)
    s_k = dscr('s_k', [2, NSH * 128, T], BF16)
    s_lf = dscr('s_lf', [2, NSH * 128, T], F32)
    s_g = dscr('s_g', [max(nB * 128, nC * 256), T], BF16)
    s_qM = dscr('s_qM', [nM * 128, T], BF16)
    s_gM = dscr('s_gM', [nM * 128, T], BF16)
    s_vA = dscr('s_vA', [T, nKV * 64], BF16)
    s_v = dscr('s_v', [T, max(nB * 128, nC * 256)], BF16)

    with contextlib.ExitStack() as es:
        P = Prog(nc, es)
        psb = [P.wrap('ps%d' % i, p=nc.alloc_psum_tensor('ps%d' % i, [128, 512], F32)) for i in range(7)]
        pst = P.wrap('pst', p=nc.alloc_psum_tensor('pst', [128, 1024], BF16))

        CO = P.buf('CO', c=([128, cfg.ncc], F32), lb=([128, 4 * nB], F32), oml=([128, 4 * nB], F32),
                   noml=([128, 4 * nB], F32), negb=([128, 4 * nC], F32), esink=([128, 2 * nA], F32),
                   tmp=([128, 4 * nB], F32))
        MK = P.buf('MK', f=([128, 642], F32), b=([128, 642], BF16))
        ET = P.buf('ET', b=([128, 3 * nA * 128], BF16))
        ON = P.buf('ON', b=([128, 128], BF16), rst64=([128, 512], F32), rst128=([128, 512], F32))
        MEMN = P.buf('MEMN', b=([128, 8, 256], BF16))
        WUP = P.buf('WUP', b=([16, 2 * 2 * nC * 128], BF16))
        P.persist()

        cc = cfg.cc
        P.dma('sp', [(CO.c[:], consts_in)], writes=[CO])
        P.dma('sp', [(MK.f[:], masks_in)], writes=[MK])
        cp(P, 'pool', MK, MK.b[:], MK, MK.f[:])
        mset(P, 'pool', ON, ON.b[:], 1.0)
        mset(P, 'pool', ON, ON.rst64[:], 1.0)
        mset(P, 'pool', ON, ON.rst64[:].rearrange("p (c t) -> p c t", t=64)[:, :, 0:1], 0.0)
        mset(P, 'pool', ON, ON.rst128[:], 1.0)
        mset(P, 'pool', ON, ON.rst128[:].rearrange("p (c t) -> p c t", t=128)[:, :, 0:1], 0.0)
        nlb = 2 * nB
        lbp = cc['lbp']
        mset(P, 'dve', CO, CO.lb[:, 0:nlb], 0.0)
        tt(P, 'dve', CO, CO.tmp[:, 0:nlb], CO, CO.c[:, lbp:lbp + nlb], CO, CO.c[:, lbp + nlb:lbp + 2 * nlb], ALU.subtract)
        act(P, CO, CO.tmp[:, 0:nlb], CO, CO.tmp[:, 0:nlb], AF.Exp)
        ts(P, 'dve', CO, CO.tmp[:, 0:nlb], CO, CO.tmp[:, 0:nlb], 1.0, None, ALU.add)
        P.op('dve', lambda e: e.reciprocal(out=CO.lb[:, nlb:2 * nlb], in_=CO.tmp[:, 0:nlb]), reads=[CO], writes=[CO])
        ts(P, 'dve', CO, CO.oml[:], CO, CO.lb[:], -1.0, 1.0, ALU.mult, ALU.add)
        ts(P, 'dve', CO, CO.noml[:], CO, CO.oml[:], -1.0, None, ALU.mult)
        ts(P, 'dve', CO, CO.negb[:], CO, CO.c[:, cc['bg']:cc['bg'] + 4 * nC], -1.0, None, ALU.mult)
        act(P, CO, CO.esink[:], CO, CO.c[:, cc['sink']:cc['sink'] + 2 * nA], AF.Exp)

        with_stage = P.buf('PST', f=([128, 3 * nA * 128], F32))
        P.dma('sp', [(with_stage.f[:], etab_in)], writes=[with_stage])
        cp(P, 'pool', ET, ET.b[:], with_stage, with_stage.f[:])
        wu = P.buf('WUS', f=([16, 2 * 2 * nC * 128], F32))
        P.dma('sp', [(wu.f[:], w_up_in)], writes=[wu])
        cp(P, 'pool', WUP, WUP.b[:], wu, wu.f[:])
        mm_ = P.buf('MEMS', f=([128, 8, 256], F32), sq=([128, 8, 256], BF16), r=([128, 256], F32))
        P.dma('sp', [(mm_.f[:], memT_in.rearrange("(k p) t -> p k t", p=128))], writes=[mm_])
        act(P, mm_, mm_.sq[:], mm_, mm_.f[:], AF.Square)
        for k in range(8):
            mm(P, psb[0], psb[0].p[:, 0:256], ON, ON.b[:], mm_, mm_.sq[:, k, :], start=(k == 0), stop=(k == 7))
        act(P, mm_, mm_.r[:], psb[0], psb[0].p[:, 0:256], AF.Ln, scale=1.0 / D, bias=EPS)
        act(P, mm_, mm_.r[:], mm_, mm_.r[:], AF.Exp, scale=-0.5)
        for k in range(8):
            stt(P, 'dve', MEMN, MEMN.b[:, k, :], mm_, mm_.f[:, k, :], CO.c[:, cc['mem'] + k:cc['mem'] + k + 1],
                mm_, mm_.r[:], ALU.mult, ALU.mult, extra=[CO])
        P.end_phase()

        def phase_op1(l, HT):
            last = (l == cfg.n_layers)
            src = xT_in if l <= 1 else xs[(l - 1) % 2]
            dst = xs[l % 2]
            if l > 0:
                WO = P.buf('WO', b=([128, NKM, D], BF16))
                wst = P.ring(2, 'wost', f=([128, NKM, 128], F32))
                for c8 in range(8):
                    st = wst.next()
                    P.dma('sp', [(st.f[:].rearrange("p k c -> p (k c)"), w_out_in[l - 1, c8])], writes=[st])
                    cp(P, 'pool', WO, WO.b[:, :, c8 * 128:(c8 + 1) * 128], st, st.f[:])
                mixr = P.ring(2, 'mixb', b=([128, NKM, 512], BF16))
            xr = P.ring(2, 'xb', f=([128, 8, 512], F32))
            sqr = P.ring(2, 'sqb', b=([128, 8, 512], BF16))
            rr = P.ring(2, 'rstd', f=([128, 512], F32))
            yr = P.ring(2, 'yb', f=([128, 8, 512], F32)) if last else None
            psi = 0
            for tb in range(NBLK):
                cs = slice(tb * 512, (tb + 1) * 512)
                X = xr.next()
                P.dma('sp', [(X.f[:], src.rearrange("(k p) t -> p k t", p=128)[:, :, cs])], writes=[X])
                if l > 0:
                    MX = mixr.next()
                    P.dma('sp', [(MX.b[:], mixT.rearrange("(k p) t -> p k t", p=128)[:, :, cs])], writes=[MX])
                    for c8 in range(8):
                        ps = psb[psi % 4]
                        psi += 1
                        for k in range(NKM):
                            mm(P, ps, ps.p[:], WO, WO.b[:, k, c8 * 128:(c8 + 1) * 128], MX, MX.b[:, k, :],
                               start=(k == 0), stop=(k == NKM - 1))
                        tt(P, 'dve', X, X.f[:, c8, :], ps, ps.p[:], X, X.f[:, c8, :], ALU.add)
                    if not last:
                        P.dma('pool', [(dst.rearrange("(k p) t -> p k t", p=128)[:, :, cs], X.f[:])], reads=[X])
                if last and not cfg.final_norm:
                    P.dma('pool', [(yT.rearrange("(k p) t -> p k t", p=128)[:, :, cs], X.f[:])], reads=[X])
                    continue
                SQ = sqr.next()
                act(P, SQ, SQ.b[:], X, X.f[:], AF.Square)
                pn = psb[4 + tb % 2]
                for k in range(8):
                    mm(P, pn, pn.p[:], ON, ON.b[:], SQ, SQ.b[:, k, :], start=(k == 0), stop=(k == 7))
                R = rr.next()
                act(P, R, R.f[:], pn, pn.p[:], AF.Ln, scale=1.0 / D, bias=EPS)
                act(P, R, R.f[:], R, R.f[:], AF.Exp, scale=-0.5)
                if not last:
                    gcol = cc['norm'] + l * 8
                    for k in range(8):
                        stt(P, 'dve', HT, HT.b[:, k, cs], X, X.f[:, k, :],
                            CO.c[:, gcol + k:gcol + k + 1], R, R.f[:], ALU.mult, ALU.mult, extra=[CO])
                else:
                    Y = yr.next()
                    gcol = cc['fin']
                    for k in range(8):
                        stt(P, 'dve', Y, Y.f[:, k, :], X, X.f[:, k, :],
                            CO.c[:, gcol + k:gcol + k + 1], R, R.f[:], ALU.mult, ALU.mult, extra=[CO])
                    P.dma('pool', [(yT.rearrange("(k p) t -> p k t", p=128)[:, :, cs], Y.f[:])], reads=[Y])

        def phase_p2(l, HT):
            even = (l % 2 == 0)
            i2 = l // 2
            wsrc = (w_in_e if even else w_in_o)[i2]
            G = cfg.ge if even else cfg.go
            wst = P.ring(3, 'wst', f=([128, 8, 128], F32))
            wbr = P.ring(3, 'wb', b=([128, 8, 128], BF16))
            ob = P.ring(4, 'ob', b=([128, 512], BF16))
            of = P.ring(4, 'of', f=([128, 512], F32))
            of2 = P.ring(3, 'of2', f=([128, 512], F32))
            psr = Ring(psb[0:4])
            psu = Ring(psb[4:7])
            rfr = P.ring(2, 'rfb', b=([16, 512], BF16))

            def load_w(c0, ncol):
                st = wst.next()
                assert c0 % 128 == 0
                P.dma('sp', [(st.f[:].rearrange("p k c -> p (k c)"), wsrc[c0 // 128])], writes=[st])
                wb = wbr.next()
                cp(P, 'pool', wb, wb.b[:, :, 0:ncol], st, st.f[:, :, 0:ncol])
                return wb

            def proj(wb, ncol, tb):
                ps = psr.next()
                for k in range(8):
                    mm(P, ps, ps.p[0:ncol, :], wb, wb.b[:, k, 0:ncol], HT, HT.b[:, k, tb * 512:(tb + 1) * 512],
                       start=(k == 0), stop=(k == 7))
                return ps

            def fgroup(name, kind, dst, scale=1.0, **kw):
                c0, n = G[name]
                r = 0
                while r < n:
                    ncol = min(128, n - r)
                    wb = load_w(c0 + r, ncol)
                    for tb in range(NBLK):
                        cs = slice(tb * 512, (tb + 1) * 512)
                        ps = proj(wb, ncol, tb)
                        if kind in ('copy', 'silu'):
                            O = ob.next()
                            act(P, O, O.b[0:ncol, :], ps, ps.p[0:ncol, :], AF.Copy if kind == 'copy' else AF.Silu,
                                scale=scale)
                            P.dma('act', [(dst[r:r + ncol, cs], O.b[0:ncol, :])], reads=[O])
                        elif kind == 'hz':
                            d = kw['d']
                            h = r // 128
                            col = i2 * 2 * nB + d * nB + h
                            E = of.next()
                            act(P, E, E.f[:], ps, ps.p[:], AF.Exp, scale=-1.0)
                            act(P, E, E.f[:], E, E.f[:], AF.Ln, scale=1.0, bias=1.0)
                            act(P, E, E.f[:], E, E.f[:], AF.Exp, scale=-1.0)
                            L = of2.next()
                            act(P, L, L.f[:], E, E.f[:], AF.Ln, scale=CO.oml[:, col:col + 1], bias=CO.lb[:, col:col + 1],
                                extra=[CO])
                            ts(P, 'dve', L, L.f[:], L, L.f[:], float(np.log(1e-30)), None, ALU.max)
                            P.dma('sp', [(s_lf[d, r:r + 128, cs], L.f[:])], reads=[L])
                            O = ob.next()
                            ts(P, 'dve', O, O.b[:], E, E.f[:], CO.noml[:, col:col + 1], CO.oml[:, col:col + 1],
                               ALU.mult, ALU.add, extra=[CO])
                            P.dma('sp', [(s_k[d, r:r + 128, cs], O.b[:])], reads=[O])
                        elif kind == 'gr':
                            d = kw['d']
                            RF = rfr.next()
                            act(P, RF, RF.b[:], ps, ps.p[0:16, :], AF.Copy)
                            for hc in range(nC):
                                col = i2 * 2 * nC + d * nC + hc
                                wcol = ((i2 * 2 + d) * nC + hc) * 128
                                pu = psu.next()
                                mm(P, pu, pu.p[:], WUP, WUP.b[:, wcol:wcol + 128], RF, RF.b[:])
                                E = of.next()
                                act(P, E, E.f[:], pu, pu.p[:], AF.Exp, scale=-1.0, bias=CO.negb[:, col:col + 1], extra=[CO])
                                act(P, E, E.f[:], E, E.f[:], AF.Ln, scale=1.0, bias=1.0)
                                L = of2.next()
                                ts(P, 'dve', L, L.f[:], E, E.f[:], -1.0 / 16.0, None, ALU.mult)
                                P.dma('sp', [(s_lf[d, hc * 128:(hc + 1) * 128, cs], L.f[:])], reads=[L])
                    r += ncol

            def tgroup(name, dst):
                c0, n = G[name]
                WT = P.buf('WT', b=([128, 8, n], BF16))
                r = 0
                while r < n:
                    st = wst.next()
                    P.dma('sp', [(st.f[:].rearrange("p k c -> p (k c)"), wsrc[(c0 + r) // 128])], writes=[st])
                    nn = min(128, n - r)
                    cp(P, 'pool', WT, WT.b[:, :, r:r + nn], st, st.f[:, :, 0:nn])
                    r += 128
                for tti in range(NTILE):
                    r = 0
                    while r < n:
                        ncol = min(512, n - r)
                        ps = psr.next()
                        for k in range(8):
                            mm(P, ps, ps.p[:, 0:ncol], HT, HT.b[:, k, tti * 128:(tti + 1) * 128], WT, WT.b[:, k, r:r + ncol],
                               start=(k == 0), stop=(k == 7))
                        O = ob.next()
                        act(P, O, O.b[:, 0:ncol], ps, ps.p[:, 0:ncol], AF.Copy)
                        P.dma('act', [(dst[tti * 128:(tti + 1) * 128, r:r + ncol], O.b[:, 0:ncol])], reads=[O])
                        r += ncol

            if even:
                fgroup('qA', 'copy', s_qA, scale=0.125)
                fgroup('kA', 'copy', s_kA)
                fgroup('qM', 'copy', s_qM)
                fgroup('gA', 'silu', s_gA)
                fgroup('qB', 'silu', s_q)
                fgroup('gB', 'silu', s_g)
                fgroup('gM', 'silu', s_gM)
                fgroup('zf', 'hz', None, d=0)
                fgroup('zb', 'hz', None, d=1)
                tgroup('vA', s_vA)
                tgroup('iB', s_v)
            else:
                fgroup('qC', 'copy', s_q, scale=float(128 ** -0.5))
                fgroup('kC', 'copy', s_k[0])
                fgroup('qM', 'copy', s_qM)
                fgroup('gC', 'silu', s_g)
                fgroup('gM', 'silu', s_gM)
                fgroup('rf', 'gr', None, d=0)
                fgroup('rb', 'gr', None, d=1)
                tgroup('vC', s_v)

        def phase_kv(l):
            KV = P.buf('KV', k=([128, nM, 256], BF16), v=([128, 2, nM * 128], BF16))
            WK = P.buf('WK', b=([128, 8, 2 * nM * 128], BF16))
            wst = P.ring(2, 'kvst', f=([128, 8, 128], F32))
            for c in range(2 * nM):
                st = wst.next()
                P.dma('sp', [(st.f[:].rearrange("p k c -> p (k c)"), w_kv_in[l, c])], writes=[st])
                cp(P, 'pool', WK, WK.b[:, :, c * 128:(c + 1) * 128], st, st.f[:])
            for h in range(nM):
                ps = psb[h % 2]
                for k in range(8):
                    mm(P, ps, ps.p[:, 0:256], WK, WK.b[:, k, h * 128:(h + 1) * 128], MEMN, MEMN.b[:, k, :],
                       start=(k == 0), stop=(k == 7))
                act(P, KV, KV.k[:, h, :], ps, ps.p[:, 0:256], AF.Copy)
            for sc in range(2):
                ps = psb[2 + sc]
                for k in range(8):
                    mm(P, ps, ps.p[:, 0:nM * 128], MEMN, MEMN.b[:, k, sc * 128:(sc + 1) * 128],
                       WK, WK.b[:, k, nM * 128:2 * nM * 128], start=(k == 0), stop=(k == 7))
                act(P, KV, KV.v[:, sc, :], ps, ps.p[:, 0:nM * 128], AF.Copy)
            return KV

        def phase_mem(KV, row0):
            qr = P.ring(3, 'mq', q=([128, 512], BF16), g=([128, 512], BF16))
            pr = P.ring(3, 'mp', b=([128, 2, 512], BF16))
            rr = P.ring(2, 'mr', f=([128, 512], F32), o=([128, 512], F32))
            orr = P.ring(3, 'mo', b=([128, 512], BF16))
            scale = float(128 ** -0.5)
            it = 0
            for tb in range(NBLK):
                cs = slice(tb * 512, (tb + 1) * 512)
                for h in range(nM):
                    Q = qr.next()
                    P.dma('sp', [(Q.q[:], s_qM[h * 128:(h + 1) * 128, cs]), (Q.g[:], s_gM[h * 128:(h + 1) * 128, cs])],
                          writes=[Q])
                    PB = pr.next()
                    for sc in range(2):
                        ps = psb[(it * 2 + sc) % 4]
                        mm(P, ps, ps.p[:], KV, KV.k[:, h, sc * 128:(sc + 1) * 128], Q, Q.q[:])
                        act(P, PB, PB.b[:, sc, :], ps, ps.p[:], AF.Exp, scale=scale)
                    po = psb[4 + it % 2]
                    pd = psb[6]
                    for sc in range(2):
                        mm(P, po, po.p[:], KV, KV.v[:, sc, h * 128:(h + 1) * 128], PB, PB.b[:, sc, :],
                           start=(sc == 0), stop=(sc == 1))
                    for sc in range(2):
                        mm(P, pd, pd.p[:], ON, ON.b[:], PB, PB.b[:, sc, :], start=(sc == 0), stop=(sc == 1))
                    R = rr.next()
                    act(P, R, R.f[:], pd, pd.p[:], AF.Ln)
                    act(P, R, R.f[:], R, R.f[:], AF.Exp, scale=-1.0)
                    tt(P, 'dve', R, R.o[:], po, po.p[:], R, R.f[:], ALU.mult)
                    O = orr.next()
                    tt(P, 'pool', O, O.b[:], R, R.o[:], Q, Q.g[:], ALU.mult)
                    P.dma('pool', [(mixT[row0 + h * 128:row0 + (h + 1) * 128, cs], O.b[:])], reads=[O])
                    it += 1

        def phase_attn(i2, row0):
            lr = P.ring(2, 'aq', q=([64, 4, 512], BF16), g=([64, 4, 512], BF16), k=([64, 768], BF16),
                        v=([128, 6, 64], BF16))
            pr = P.ring(4, 'ap', b=([128, 512], BF16))
            dr = P.ring(2, 'ad', f=([64, 512], F32), o=([64, 512], F32))
            orr = P.ring(2, 'ao', b=([64, 4, 512], BF16))
            pss = Ring(psb[0:4])
            it = 0
            for tb in range(NBLK):
                cs = slice(tb * 512, (tb + 1) * 512)
                for n in range(nKV):
                    L = lr.next()
                    k0 = max(0, tb * 512 - 128)
                    k1 = min(T, tb * 512 + 640)
                    ko = k0 - (tb * 512 - 128)
                    t0 = max(0, tb * 4 - 1)
                    t1 = min(NTILE, tb * 4 + 5)
                    to = t0 - (tb * 4 - 1)
                    P.dma('sp', [
                        (L.q[:], s_qA[n * 256:(n + 1) * 256, cs].rearrange("(g d) t -> d g t", d=64)),
                        (L.g[:], s_gA[n * 256:(n + 1) * 256, cs].rearrange("(g d) t -> d g t", d=64)),
                        (L.k[:, ko:ko + (k1 - k0)], s_kA[n * 64:(n + 1) * 64, k0:k1]),
                        (L.v[:, to:to + (t1 - t0), :],
                         s_vA[t0 * 128:t1 * 128, n * 64:(n + 1) * 64].rearrange("(j p) c -> p j c", p=128)),
                    ], writes=[L])
                    O = orr.next()
                    for qt in range(4):
                        c = tb * 4 + qt
                        js = [j for j in range(3) if 0 <= c - 1 + j < NTILE]
                        PBs = []
                        for j in js:
                            ps = pss.next()
                            kc = (qt + j) * 128
                            mm(P, ps, ps.p[:].rearrange("p (g t) -> p g t", g=4), L, L.k[:, kc:kc + 128], L, L.q[:, :, qt * 128:(qt + 1) * 128])
                            PB = pr.next()
                            act(P, PB, PB.b[:], ps, ps.p[:], AF.Exp)
                            e0 = (j * nA + n * 4) * 128
                            tt(P, 'pool' if j == 1 else 'dve', PB, PB.b[:], PB, PB.b[:], ET, ET.b[:, e0:e0 + 512], ALU.mult)
                            PBs.append((j, PB))
                        po = psb[4 + it % 2]
                        pd = psb[6]
                        for ii, (j, PB) in enumerate(PBs):
                            mm(P, po, po.p[0:64, :], L, L.v[:, qt + j, :], PB, PB.b[:],
                               start=(ii == 0), stop=(ii == len(PBs) - 1))
                        for ii, (j, PB) in enumerate(PBs):
                            mm(P, pd, pd.p[0:64, :], ON, ON.b[:, 0:64], PB, PB.b[:],
                               start=(ii == 0), stop=(ii == len(PBs) - 1))
                        Dn = dr.next()
                        sk = i2 * nA + n * 4
                        tt(P, 'dve', Dn, Dn.f[:].rearrange("p (g t) -> p g t", g=4),
                           pd, pd.p[0:64, :].rearrange("p (g t) -> p g t", g=4),
                           CO, CO.esink[0:64, sk:sk + 4].unsqueeze(2).to_broadcast([64, 4, 128]), ALU.add)
                        act(P, Dn, Dn.f[:], Dn, Dn.f[:], AF.Ln)
                        act(P, Dn, Dn.f[:], Dn, Dn.f[:], AF.Exp, scale=-1.0)
                        tt(P, 'dve', Dn, Dn.o[:], po, po.p[0:64, :], Dn, Dn.f[:], ALU.mult)
                        tt(P, 'pool', O, O.b[:, :, qt * 128:(qt + 1) * 128],
                           Dn, Dn.o[:].rearrange("p (g t) -> p g t", g=4),
                           L, L.g[:, :, qt * 128:(qt + 1) * 128], ALU.mult)
                        it += 1
                    P.dma('pool', [(mixT[row0 + n * 256:row0 + (n + 1) * 256, cs].rearrange("(g d) t -> d g t", d=64),
                                    O.b[:])], reads=[O])

        def phase_scan(nh, dv, C, kdirs, gcol0, row0):
            nch = T // C
            ncb = 512 // C
            wv = 512 // ncb
            nr = dv // wv
            nvh = dv // 128
            mcol = 0 if C == 64 else 256
            rst = ON.rst64 if C == 64 else ON.rst128
            QT = [[P.buf('QT%d_%d' % (d, t_), b=([128, 512], BF16)) for t_ in range(NBLK)] for d in range(2)]
            AT = [[P.buf('AT%d_%d' % (d, t_), b=([128, 4, 128], BF16)) for t_ in range(NBLK)] for d in range(2)]
            pstA = [P.wrap('pstA', p=pst.p), P.wrap('pstB', p=pst.p)]
            SIN = [P.buf('SIN%d' % d, b=([128, nch, dv], BF16)) for d in range(2)]
            DS = P.buf('DS', f=([128, dv, nch], F32))
            DA = [P.buf('DA%d' % d, f=([128, nch], F32)) for d in range(2)]
            DR = P.buf('DR', f=([128, 32, nch], F32))
            SF = P.ring(2, 'SF', f=([128, 32, nch], F32))
            ld = P.ring(2, 'sl', lf=([128, 512], F32), k=([128, 512], BF16), q=([128, 512], BF16),
                        v=([128, 4, dv], BF16))
            w1 = P.ring(2, 'sw1', pre=([128, 512], F32), b=([128, 512], F32))
            w2 = P.ring(2, 'sw2', eb=([128, 512], F32), enb=([128, 512], F32), ec=([128, 512], F32))
            w3 = P.ring(2, 'sw3', kt=([128, 512], BF16), kh=([128, 512], BF16), khT=([128, 2, 4, 128], BF16))
            l3 = P.ring(2, 'sl3', v=([128, 4, dv], BF16), g=([128, nvh, 512], BF16))
            w4 = P.ring(2, 'sw4', sq=([128, nvh, 512], BF16), r=([128, 512], F32), m=([128, nvh, 512], F32))
            w5 = P.ring(2, 'sw5', b=([128, nvh, 512], BF16))
            itc = [0]

            def stage_a(h, d, tb):
                hr = slice(h * 128, (h + 1) * 128)
                kd = d if kdirs else 0
                cs = slice(tb * 512, (tb + 1) * 512)
                L = ld.next()
                P.dma('sp', [
                    (L.lf[:], s_lf[d, hr, cs]), (L.k[:], s_k[kd, hr, cs]), (L.q[:], s_q[hr, cs]),
                    (L.v[:], s_v[cs, h * dv:(h + 1) * dv].rearrange("(j p) c -> p j c", p=128)),
                ], writes=[L])
                W1 = w1.next()
                P.op('dve', lambda e, W1=W1, L=L: e.tensor_tensor_scan(
                    out=W1.pre[:], data0=rst[:], data1=L.lf[:], initial=0.0, op0=ALU.mult, op1=ALU.add),
                    reads=[L, ON], writes=[W1])
                pre3 = W1.pre[:].rearrange("p (c t) -> p c t", t=C)
                b3 = W1.b[:].rearrange("p (c t) -> p c t", t=C)
                if d == 0:
                    bsrc = W1.pre
                    edge = pre3[:, :, C - 1:C]
                else:
                    tt(P, 'pool', W1, W1.b[:], L, L.lf[:], W1, W1.pre[:], ALU.subtract)
                    tt(P, 'dve', W1, b3, W1, b3, W1, pre3[:, :, C - 1:C].to_broadcast([128, ncb, C]), ALU.add)
                    bsrc = W1.b
                    edge = b3[:, :, 0:1]
                W2 = w2.next()
                act(P, W2, W2.eb[:], W1, bsrc[:], AF.Exp)
                ts(P, 'dve', W2, W2.ec[:], W1, bsrc[:], -80.0, None, ALU.max)
                act(P, W2, W2.enb[:], W2, W2.ec[:], AF.Exp, scale=-1.0)
                tt(P, 'dve', W2, W2.ec[:].rearrange("p (c t) -> p c t", t=C), W1,
                   edge.to_broadcast([128, ncb, C]), W1, bsrc[:].rearrange("p (c t) -> p c t", t=C), ALU.subtract)
                act(P, W2, W2.ec[:], W2, W2.ec[:], AF.Exp)
                if d == 0:
                    j0 = tb * ncb
                    act(P, DA[d], DA[d].f[:, j0:j0 + ncb], W1, edge.rearrange("p c o -> p (c o)"), AF.Exp)
                else:
                    for cl in range(ncb):
                        j = nch - 1 - (tb * ncb + cl)
                        act(P, DA[d], DA[d].f[:, j:j + 1], W1, b3[:, cl, 0:1], AF.Exp)
                tt(P, 'dve', QT[d][tb], QT[d][tb].b[:], L, L.q[:], W2, W2.eb[:], ALU.mult)
                W3 = w3.next()
                tt(P, 'dve', W3, W3.kt[:], L, L.k[:], W2, W2.enb[:], ALU.mult)
                tt(P, 'dve', W3, W3.kh[:], L, L.k[:], W2, W2.ec[:], ALU.mult)
                it = itc[0]
                itc[0] += 1
                return (d, tb, L, W3, it)

            def stage_b(ctx):
                d, tb, L, W3, it = ctx
                pa = psb[it % 2]
                for i4 in range(4):
                    ss = slice(i4 * 128, (i4 + 1) * 128)
                    mm(P, pa, pa.p[:, ss], W3, W3.kt[:, ss], QT[d][tb], QT[d][tb].b[:, ss])
                pq = pstA[it % 2]
                po_ = (it % 2) * 512
                for i4 in range(4):
                    P.op('pe', lambda e, W3=W3, i4=i4, po_=po_: e.transpose(pst.p[:, po_ + i4 * 128:po_ + (i4 + 1) * 128],
                                                                       W3.kh[:, i4 * 128:(i4 + 1) * 128], MK.b[:, 512:640]),
                         reads=[W3, MK], writes=[pq])
                mk = MK.b[:, mcol + d * 128:mcol + (d + 1) * 128]
                tt(P, 'dve', AT[d][tb], AT[d][tb].b[:], pa, pa.p[:].rearrange("p (i t) -> p i t", i=4),
                   MK, mk.unsqueeze(1).to_broadcast([128, 4, 128]), ALU.mult)
                if C == 64:
                    for half in range(2):
                        act(P, W3, W3.khT[:, half, :, :], pq, pst.p[:, po_:po_ + 512].rearrange("p (i t) -> p i t", i=4),
                            AF.Copy, scale=MK.f[:, 640 + half:641 + half], extra=[MK])
                else:
                    act(P, W3, W3.khT[:, 0, :, :], pq, pst.p[:, po_:po_ + 512].rearrange("p (i t) -> p i t", i=4), AF.Copy)
                for r in range(nr):
                    pd = psb[2 + (it * nr + r) % 2]
                    for cl in range(ncb):
                        i4 = (cl * C) // 128
                        p0 = (cl * C) % 128
                        slot = cl if d == 0 else ncb - 1 - cl
                        mm(P, pd, pd.p[:, slot * wv:(slot + 1) * wv], W3, W3.khT[:, p0 // 64, i4, :],
                           L, L.v[:, i4, r * wv:(r + 1) * wv])
                    j0 = tb * ncb if d == 0 else nch - (tb + 1) * ncb
                    if r % 2 == 0:
                        cp(P, 'dve', DS, DS.f[:, r * wv:(r + 1) * wv, j0:j0 + ncb],
                           pd, pd.p[:].rearrange("p (c v) -> p v c", v=wv))
                    else:
                        act(P, DS, DS.f[:, r * wv:(r + 1) * wv, j0:j0 + ncb],
                            pd, pd.p[:].rearrange("p (c v) -> p v c", v=wv), AF.Copy)

            def pass2(d):
                act(P, DR, DR.f[:], DA[d], DA[d].f[:].unsqueeze(1).to_broadcast([128, 32, nch]), AF.Copy)
                mset(P, 'pool', DR, DR.f[:, :, 0:1], 0.0)
                mset(P, 'pool', SIN[d], SIN[d].b[:, 0, :], 0.0)
                for v0 in range(0, dv, 32):
                    S = SF.next()
                    P.op('dve', lambda e, S=S, v0=v0: e.tensor_tensor_scan(
                        out=S.f[:].rearrange("p v c -> p (v c)"), data0=DR.f[:].rearrange("p v c -> p (v c)"),
                        data1=DS.f[:, v0:v0 + 32, :].rearrange("p v c -> p (v c)"), initial=0.0,
                        op0=ALU.mult, op1=ALU.add), reads=[DR, DS], writes=[S])
                    act(P, SIN[d], SIN[d].b[:, 1:nch, v0:v0 + 32],
                        S, S.f[:, :, 0:nch - 1].rearrange("p v c -> p c v"), AF.Copy)

            p3 = [0]

            def pass3(h, tb):
                cs = slice(tb * 512, (tb + 1) * 512)
                L3 = l3.next()
                P.dma('sp', [
                    (L3.v[:], s_v[cs, h * dv:(h + 1) * dv].rearrange("(j p) c -> p j c", p=128)),
                    (L3.g[:], s_g[h * dv:(h + 1) * dv, cs].rearrange("(a p) t -> p a t", p=128)),
                ], writes=[L3])
                W4 = w4.next()
                pos = []
                for vh in range(nvh):
                    po = psb[4 + vh] if nvh == 2 else psb[4 + p3[0] % 2]
                    pos.append(po)
                    vs = slice(vh * 128, (vh + 1) * 128)
                    for i4 in range(4):
                        ti = tb * 4 + i4
                        ts_ = slice(i4 * 128, (i4 + 1) * 128)
                        mm(P, po, po.p[:, ts_], L3, L3.v[:, i4, vs], AT[0][tb], AT[0][tb].b[:, i4, :], start=True, stop=False)
                        mm(P, po, po.p[:, ts_], L3, L3.v[:, i4, vs], AT[1][tb], AT[1][tb].b[:, i4, :], start=False, stop=False)
                        nci = 128 // C
                        for ci in range(nci):
                            cg = ti * nci + ci
                            tcs = slice(i4 * 128 + ci * C, i4 * 128 + (ci + 1) * C)
                            mm(P, po, po.p[:, tcs], SIN[0], SIN[0].b[:, cg, vs], QT[0][tb], QT[0][tb].b[:, tcs],
                               start=False, stop=False)
                            mm(P, po, po.p[:, tcs], SIN[1], SIN[1].b[:, nch - 1 - cg, vs], QT[1][tb], QT[1][tb].b[:, tcs],
                               start=False, stop=(ci == nci - 1))
                    act(P, W4, W4.sq[:, vh, :], po, po.p[:], AF.Square)
                p3[0] += 1
                pn = psb[6]
                for vh in range(nvh):
                    mm(P, pn, pn.p[:], ON, ON.b[:], W4, W4.sq[:, vh, :], start=(vh == 0), stop=(vh == nvh - 1))
                act(P, W4, W4.r[:], pn, pn.p[:], AF.Ln, scale=1.0 / dv, bias=EPS)
                act(P, W4, W4.r[:], W4, W4.r[:], AF.Exp, scale=-0.5)
                O = w5.next()
                for vh in range(nvh):
                    tt(P, 'dve', W4, W4.m[:, vh, :], pos[vh], pos[vh].p[:], W4, W4.r[:], ALU.mult)
                    gc = gcol0 + h * nvh + vh
                    stt(P, 'dve', O, O.b[:, vh, :], W4, W4.m[:, vh, :], CO.c[:, gc:gc + 1], L3, L3.g[:, vh, :],
                        ALU.mult, ALU.mult, extra=[CO])
                P.dma('sp', [(mixT[row0 + h * dv:row0 + (h + 1) * dv, cs].rearrange("(a p) t -> p a t", p=128),
                              O.b[:])], reads=[O])

            for h in range(nh):
                iters = [(d, tb) for d in range(2) for tb in range(NBLK)]
                prev = None
                for (d, tb) in iters:
                    cur = stage_a(h, d, tb)
                    if prev is not None:
                        stage_b(prev)
                        if prev[0] == 0 and prev[1] == NBLK - 1:
                            pass2(0)
                    prev = cur
                stage_b(prev)
                pass2(1)
                for tb in range(NBLK):
                    pass3(h, tb)


        stop = cfg.stop_after
        import os
        if os.environ.get('ONLY_SCAN'):
            phase_scan(nB, 128, 64, True, cc['hg'], nA * 64)
            P.end_phase()
            P.emit()
            return nc
        for l in range(cfg.n_layers):
            lastl = (l == cfg.n_layers - 1)
            HT = P.buf('HT', b=([128, 8, T], BF16))
            mark = P.sb_off
            phase_op1(l, HT)
            P.end_phase(keep=mark)
            if lastl and stop == 'op1':
                break
            phase_p2(l, HT)
            P.end_phase()
            if lastl and stop == 'p2':
                break
            i2 = l // 2
            if not os.environ.get('SKIP_MEM'):
                KV = phase_kv(l)
            if l % 2 == 0:
                if not os.environ.get('SKIP_MEM'):
                    phase_mem(KV, nA * 64 + nB * 128)
                    P.end_phase()
                if lastl and stop == 'mem':
                    break
                phase_scan(nB, 128, 64, True, cc['hg'] + i2 * nB, nA * 64)
                P.end_phase()
                if lastl and stop == 'scan0':
                    break
                if not os.environ.get('SKIP_ATTN'):
                    phase_attn(i2, 0)
                    P.end_phase()
            else:
                phase_mem(KV, nC * 256)
                P.end_phase()
                if lastl and stop == 'mem':
                    break
                phase_scan(nC, 256, 128, False, cc['gl'] + i2 * nC * 2, 0)
                P.end_phase()
        if stop is None:
            phase_op1(cfg.n_layers, None)
            P.end_phase()
        P.emit()
        print('ops recorded:', P.nops, {e: len(s) for e, s in P.streams.items()})
    return nc


def core_inputs(cfg, b, heads, inp):
    f32 = np.float32
    hA, hKV, hB, hM, hC = heads['A'], heads['KV'], heads['B'], heads['M'], heads['C']
    out = {}
    out['xT'] = np.ascontiguousarray(inp['x'][b].T)
    out['memT'] = np.ascontiguousarray(inp['mem'][b].T)

    def cols(base, hs, w):
        return np.concatenate([np.arange(base + h * w, base + (h + 1) * w) for h in hs])

    def padded(groups):
        sel = []
        for g in groups:
            sel.append(g)
            pad = (-len(g)) % 128
            if pad:
                sel.append(-np.ones(pad, dtype=np.int64))
        return np.concatenate(sel)

    def chunkify(w, sel):
        L, K = w.shape[0], w.shape[1]
        wp = np.zeros((L, K, len(sel)), f32)
        ok = sel >= 0
        wp[:, :, ok] = w[:, :, sel[ok]]
        nk, nch = K // 128, len(sel) // 128
        return np.ascontiguousarray(wp.reshape(L, nk, 128, nch, 128).transpose(0, 3, 2, 1, 4)).reshape(L, nch, 128, nk * 128)
    EV = np.cumsum([0, 512, 128, 128, 512, 512, 512, 512, 512, 512, 512, 512])
    names = ['qA', 'kA', 'vA', 'gA', 'qB', 'zf', 'zb', 'iB', 'gB', 'qM', 'gM']
    eo = dict(zip(names, EV[:-1]))
    sel_e = padded([
        cols(eo['qA'], hA, 64), cols(eo['kA'], hKV, 64), cols(eo['gA'], hA, 64),
        cols(eo['qB'], hB, 128), cols(eo['gB'], hB, 128), cols(eo['qM'], hM, 128), cols(eo['gM'], hM, 128),
        cols(eo['zf'], hB, 128), cols(eo['zb'], hB, 128), cols(eo['vA'], hKV, 64), cols(eo['iB'], hB, 128)])
    assert len(sel_e) == cfg.nce, (len(sel_e), cfg.nce)
    out['w_in_e'] = chunkify(inp['w_in_even'], sel_e)
    OD = np.cumsum([0, 512, 512, 1024, 1024, 16, 16, 512, 512])
    oo = dict(zip(['qC', 'kC', 'vC', 'gC', 'rf', 'rb', 'qM', 'gM'], OD[:-1]))
    sel_o = padded([
        cols(oo['qC'], hC, 128), cols(oo['kC'], hC, 128), cols(oo['gC'], hC, 256),
        cols(oo['qM'], hM, 128), cols(oo['gM'], hM, 128),
        np.arange(oo['rf'], oo['rf'] + 16), np.arange(oo['rb'], oo['rb'] + 16), cols(oo['vC'], hC, 256)])
    assert len(sel_o) == cfg.nco, (len(sel_o), cfg.nco)
    out['w_in_o'] = chunkify(inp['w_in_odd'], sel_o)
    rows_e = np.concatenate([cols(0, hA, 64), cols(512, hB, 128), cols(1024, hM, 128)])
    rows_o = np.concatenate([cols(0, hC, 256), cols(1024, hM, 128)])
    wo = np.zeros((4, cfg.mixr, D), f32)
    for l in range(4):
        if l % 2 == 0:
            wo[l] = inp['w_out_even'][l // 2][rows_e]
        else:
            wo[l] = inp['w_out_odd'][l // 2][rows_o]
    out['w_out'] = chunkify(wo, np.arange(D))
    kvc = np.concatenate([cols(0, hM, 128), cols(512, hM, 128)])
    out['w_kv'] = chunkify(inp['w_mem_kv'], kvc)
    wu = inp['w_gate_up'][:, :, :, cols(0, hC, 128)]
    out['w_up'] = np.ascontiguousarray(wu.transpose(2, 0, 1, 3).reshape(16, -1))
    cc = cfg.cc
    C = np.zeros((128, cfg.ncc), f32)
    for l in range(4):
        g = inp['norm_even'][l // 2] if l % 2 == 0 else inp['norm_odd'][l // 2]
        C[:, cc['norm'] + l * 8:cc['norm'] + (l + 1) * 8] = g.reshape(8, 128).T
    C[:, cc['fin']:cc['fin'] + 8] = inp['final_norm'].reshape(8, 128).T
    C[:, cc['mem']:cc['mem'] + 8] = inp['mem_norm'].reshape(8, 128).T
    nB, nC, nA = cfg.nB, cfg.nC, cfg.nA
    for i in range(2):
        for hi, h in enumerate(hB):
            C[:, cc['hg'] + i * nB + hi] = inp['hgrn_norm'][i, h * 128:(h + 1) * 128]
        for hi, h in enumerate(hC):
            for vh in range(2):
                C[:, cc['gl'] + i * nC * 2 + hi * 2 + vh] = inp['gla_norm'][i, h * 256 + vh * 128:h * 256 + (vh + 1) * 128]
        for d in range(2):
            for hi, h in enumerate(hB):
                C[:, cc['lbp'] + i * 2 * nB + d * nB + hi] = inp['lb_param'][i, d, h * 128:(h + 1) * 128]
            for hi, h in enumerate(hC):
                C[:, cc['bg'] + i * 2 * nC + d * nC + hi] = inp['b_gate'][i, d, h * 128:(h + 1) * 128]
        for hi, h in enumerate(hA):
            C[:, cc['sink'] + i * nA + hi] = inp['sink'][i, h]
    out['consts'] = C
    out['etab'] = np.ascontiguousarray(alibi_table(hA).reshape(128, -1))
    out['masks'] = scan_masks()
    return out


_CACHE = {}


def kernel(**inputs):
    inp = {k: np.asarray(v) for k, v in inputs.items()}
    cfg = Cfg(8, 4, 4, 4)
    heads = {'A': list(range(8)), 'KV': [0, 1], 'B': list(range(4)), 'M': list(range(4)), 'C': list(range(4))}
    if 'nc' not in _CACHE:
        _CACHE['nc'] = build(cfg)
    nc = _CACHE['nc']
    in_maps = [core_inputs(cfg, c % 4, heads, inp) for c in range(N_CORES)]
    res = run_bass_kernel_spmd(nc, in_maps, core_ids=list(range(N_CORES)))
    out = np.stack([np.ascontiguousarray(res.results[b]['yT'].T) for b in range(4)], axis=0)
    return out.astype(np.float32)
```

```python
import contextlib
import numpy as np
import concourse.bass as bass
import concourse.mybir as mybir
from concourse.bass_utils import run_bass_kernel_spmd

F32 = mybir.dt.float32
BF16 = mybir.dt.bfloat16
ALU = mybir.AluOpType
AF = mybir.ActivationFunctionType
DT_SIZE = {F32: 4, BF16: 2}

T = 4096
D = 1024
NBLK = T // 512
NTILE = T // 128
EPS = 1e-6
N_CORES = 8


class Buf:
    def __init__(self, name):
        self.name = name
        self.lw = None
        self.rd = {}
        self.dsem = None
        self.tt = {}

    def __getattr__(self, k):
        t = self.__dict__.get('tt', {})
        if k in t:
            return t[k]
        raise AttributeError(k)


class Prog:
    COMPUTE = ('pe', 'act', 'dve', 'pool')
    ENG = ('pe', 'act', 'dve', 'pool', 'sp')

    def __init__(self, nc, es, n_dsem=88, sb_limit=229000, sb_start=16640):
        self.nc = nc
        self.semh = {}
        for e in self.COMPUTE:
            self.semh[e] = es.enter_context(nc.semaphore('c_' + e))
        self.free_dsem = []
        for i in range(n_dsem):
            k = ('d', i)
            self.semh[k] = es.enter_context(nc.semaphore('d%d' % i))
            self.free_dsem.append(k)
        self.cnt = {k: 0 for k in self.semh}
        self.streams = {e: [] for e in self.ENG}
        self.waited = {e: {} for e in self.ENG}
        self.sb_off = sb_start
        self.sb_base = sb_start
        self.sb_limit = sb_limit
        self.uid = 0
        self.phase_dsems = []
        self.nops = 0

    def sbuf(self, shape, dtype, name='t'):
        self.uid += 1
        per_part = int(np.prod(shape[1:])) * DT_SIZE[dtype]
        off = (self.sb_off + 63) // 64 * 64
        assert off + per_part <= self.sb_limit, ('SBUF overflow', name, off, per_part)
        h = self.nc.alloc_sbuf_tensor_at('%s_%d' % (name, self.uid), list(shape), dtype, offset=off)
        self.sb_off = off + per_part
        return h

    def buf(self, name, **tensors):
        b = Buf(name)
        for k, (shape, dtype) in tensors.items():
            b.tt[k] = self.sbuf(shape, dtype, name + '_' + k)
        return b

    def ring(self, n, name, **tensors):
        return Ring([self.buf('%s%d' % (name, i), **tensors) for i in range(n)])

    def wrap(self, name, **handles):
        b = Buf(name)
        b.tt.update(handles)
        return b

    def need_dsem(self, b):
        if b.dsem is None:
            b.dsem = self.free_dsem.pop()
            self.phase_dsems.append(b.dsem)
        return b.dsem

    def persist(self):
        self.sb_base = self.sb_off
        self.phase_dsems = []

    def _deps(self, eng, reads, writes, is_dma):
        deps = {}

        def need(tok):
            if tok is None:
                return
            k, v = tok
            if deps.get(k, 0) < v:
                deps[k] = v
        for b in reads:
            need(b.lw)
        for b in writes:
            if b.lw is not None and (is_dma or b.lw[0] != eng):
                need(b.lw)
            for k, v in b.rd.items():
                if is_dma or k != eng:
                    need((k, v))
        w = self.waited[eng]
        out = []
        for k, v in deps.items():
            if w.get(k, 0) < v:
                w[k] = v
                out.append((k, v))
        return out

    def _commit(self, tok, reads, writes):
        for b in writes:
            b.lw = tok
            b.rd = {}
        k, v = tok
        for b in reads:
            if b in writes:
                continue
            if b.rd.get(k, 0) < v:
                b.rd[k] = v

    def op(self, eng, fn, reads=(), writes=()):
        waits = self._deps(eng, reads, writes, False)
        self.cnt[eng] += 1
        tok = (eng, self.cnt[eng])
        self.streams[eng].append((waits, fn, eng, 1))
        self._commit(tok, reads, writes)
        self.nops += 1

    def dma(self, queue, pairs, reads=(), writes=()):
        onchip = list(writes) + list(reads)
        key = self.need_dsem(onchip[0])
        for b in onchip[1:]:
            assert b.dsem is None or b.dsem == key
            b.dsem = key
        waits = self._deps(queue, reads, writes, True)
        first = True
        for (o, i) in pairs:
            self.cnt[key] += 16
            self.streams[queue].append((waits if first else [],
                                        (lambda e, o=o, i=i: e.dma_start(out=o, in_=i)), key, 16))
            first = False
            self.nops += 1
        self._commit((key, self.cnt[key]), reads, writes)

    def barrier(self):
        for e in self.ENG:
            waits = []
            w = self.waited[e]
            for k, v in self.cnt.items():
                if v > 0 and w.get(k, 0) < v and k != e:
                    w[k] = v
                    waits.append((k, v))
            if waits:
                self.streams[e].append((waits, None, None, 0))

    def end_phase(self, keep=None):
        self.barrier()
        self.free_dsem.extend(self.phase_dsems)
        self.phase_dsems = []
        self.sb_off = self.sb_base if keep is None else keep

    def emit(self):
        nc = self.nc
        semh = self.semh
        streams = self.streams
        with nc.Block() as block:
            def body(name):
                def f(e):
                    for (waits, fn, key, inc) in streams[name]:
                        for (k, v) in waits:
                            e.wait_ge(semh[k], v)
                        if fn is not None:
                            fn(e).then_inc(semh[key], inc)
                return f
            block.tensor(body('pe'))
            block.scalar(body('act'))
            block.vector(body('dve'))
            block.gpsimd(body('pool'))
            block.sync(body('sp'))


class Ring:
    def __init__(self, bufs):
        self.bufs = bufs
        self.i = 0

    def next(self):
        b = self.bufs[self.i % len(self.bufs)]
        self.i += 1
        return b


def mm(P, ob, o, lb, l, rb, r, start=True, stop=True):
    P.op('pe', lambda e: e.matmul(o, lhsT=l, rhs=r, start=start, stop=stop), reads=[lb, rb], writes=[ob])


def act(P, ob, o, ib, i, func, scale=1.0, bias=0.0, extra=()):
    P.op('act', lambda e: e.activation(out=o, in_=i, func=func, scale=scale, bias=bias),
         reads=[ib] + list(extra), writes=[ob])


def tt(P, eng, ob, o, ab, a, bb, b, op):
    P.op(eng, lambda e: e.tensor_tensor(out=o, in0=a, in1=b, op=op), reads=[ab, bb], writes=[ob])


def ts(P, eng, ob, o, ab, a, s1, s2, op0, op1=None, extra=()):
    if op1 is None:
        P.op(eng, lambda e: e.tensor_scalar(out=o, in0=a, scalar1=s1, scalar2=None, op0=op0),
             reads=[ab] + list(extra), writes=[ob])
    else:
        P.op(eng, lambda e: e.tensor_scalar(out=o, in0=a, scalar1=s1, scalar2=s2, op0=op0, op1=op1),
             reads=[ab] + list(extra), writes=[ob])


def stt(P, eng, ob, o, ab, a, scalar, bb, b, op0, op1, extra=()):
    P.op(eng, lambda e: e.scalar_tensor_tensor(out=o, in0=a, scalar=scalar, in1=b, op0=op0, op1=op1),
         reads=[ab, bb] + list(extra), writes=[ob])


def cp(P, eng, ob, o, ib, i):
    P.op(eng, lambda e: e.tensor_copy(out=o, in_=i), reads=[ib], writes=[ob])


def mset(P, eng, ob, o, val):
    P.op(eng, lambda e: e.memset(o, val), writes=[ob])


class Cfg:
    def __init__(self, nA, nB, nM, nC, n_layers=4, final_norm=True, debug=False, stop_after=None):
        self.nA, self.nB, self.nM, self.nC = nA, nB, nM, nC
        self.debug = debug
        self.stop_after = stop_after
        self.nKV = nA // 4
        self.n_layers = n_layers
        self.final_norm = final_norm
        g = []
        o = 0

        def add(name, n):
            nonlocal o
            g.append((name, o, n))
            o += (n + 127) // 128 * 128
        add('qA', nA * 64); add('kA', self.nKV * 64); add('gA', nA * 64)
        add('qB', nB * 128); add('gB', nB * 128); add('qM', nM * 128); add('gM', nM * 128)
        add('zf', nB * 128); add('zb', nB * 128)
        add('vA', self.nKV * 64); add('iB', nB * 128)
        self.ge = {n: (s, c) for n, s, c in g}
        self.nce = o
        g = []
        o = 0
        add('qC', nC * 128); add('kC', nC * 128); add('gC', nC * 256); add('qM', nM * 128); add('gM', nM * 128)
        add('rf', 16); add('rb', 16); add('vC', nC * 256)
        self.go = {n: (s, c) for n, s, c in g}
        self.nco = o
        self.mix_e = nA * 64 + nB * 128 + nM * 128
        self.mix_o = nC * 256 + nM * 128
        self.mixr = max(self.mix_e, self.mix_o)
        assert self.mix_e == self.mix_o
        c = {}
        o = 0
        for nm, n in (('norm', 32), ('fin', 8), ('mem', 8), ('hg', 2 * nB), ('gl', 2 * nC * 2),
                      ('lbp', 4 * nB), ('bg', 4 * nC), ('sink', 2 * nA)):
            c[nm] = o
            o += n
        self.cc = c
        self.ncc = o


def alibi_table(heads):
    s = np.arange(128)[:, None, None, None]
    j = np.arange(3)[None, :, None, None]
    t = np.arange(128)[None, None, None, :]
    dist = np.abs(t - s - (j - 1) * 128).astype(np.float64)
    slopes = np.array([2.0 ** (-8.0 * (h + 1) / 8) for h in heads])[None, None, :, None]
    e = np.exp(-slopes * dist) * (dist <= 128)
    return e.astype(np.float32)


def scan_masks():
    s = np.arange(128)[:, None]
    t = np.arange(128)[None, :]
    m = []
    for C in (64, 128):
        same = (s // C) == (t // C)
        m.append(((s <= t) & same).astype(np.float32))
        m.append(((s >= t) & same).astype(np.float32))
    m.append(np.eye(128, dtype=np.float32))
    p = np.arange(128)[:, None]
    m.append((p < 64).astype(np.float32))
    m.append((p >= 64).astype(np.float32))
    return np.concatenate(m, axis=1)


def build(cfg):
    nc = bass.Bass("TRN2", target_bir_lowering=False)
    nA, nB, nM, nC, nKV = cfg.nA, cfg.nB, cfg.nM, cfg.nC, cfg.nKV
    MIXR = cfg.mixr
    NKM = MIXR // 128

    def din(name, shape, dt=F32):
        return nc.dram_tensor(name, list(shape), dt, kind="ExternalInput").ap()

    def dscr(name, shape, dt):
        dbg = cfg.debug and name in ('mixT',)
        return nc.dram_tensor(name, list(shape), dt, kind="ExternalOutput" if dbg else "Internal").ap()

    xT_in = din('xT', [D, T])
    memT_in = din('memT', [D, 256])
    consts_in = din('consts', [128, cfg.ncc])
    etab_in = din('etab', [128, 3 * nA * 128])
    masks_in = din('masks', [128, 642])
    w_in_e = din('w_in_e', [2, cfg.nce // 128, 128, 1024])
    w_in_o = din('w_in_o', [2, cfg.nco // 128, 128, 1024])
    w_out_in = din('w_out', [4, 8, 128, NKM * 128])
    w_kv_in = din('w_kv', [4, 2 * nM, 128, 1024])
    w_up_in = din('w_up', [16, 2 * 2 * nC * 128])
    yT = nc.dram_tensor('yT', [D, T], F32, kind="ExternalOutput").ap()

    xs = [dscr('xs0', [D, T], F32), dscr('xs1', [D, T], F32)]
    mixT = dscr('mixT', [MIXR, T], BF16)
    s_qA = dscr('s_qA', [nA * 64, T], BF16)
    s_kA = dscr('s_kA', [nKV * 64, T], BF16)
    s_gA = dscr('s_gA', [nA * 64, T], BF16)
    NSH = max(nB, nC)
    s_q = dscr('s_q', [NSH * 128, T], BF16)
    s_k = dscr('s_k', [2, NSH * 128, T], BF16)
    s_lf = dscr('s_lf', [2, NSH * 128, T], F32)
    s_g = dscr('s_g', [max(nB * 128, nC * 256), T], BF16)
    s_qM = dscr('s_qM', [nM * 128, T], BF16)
    s_gM = dscr('s_gM', [nM * 128, T], BF16)
    s_vA = dscr('s_vA', [T, nKV * 64], BF16)
    s_v = dscr('s_v', [T, max(nB * 128, nC * 256)], BF16)

    with contextlib.ExitStack() as es:
        P = Prog(nc, es)
        psb = [P.wrap('ps%d' % i, p=nc.alloc_psum_tensor('ps%d' % i, [128, 512], F32)) for i in range(7)]
        pst = P.wrap('pst', p=nc.alloc_psum_tensor('pst', [128, 1024], BF16))

        CO = P.buf('CO', c=([128, cfg.ncc], F32), lb=([128, 4 * nB], F32), oml=([128, 4 * nB], F32),
                   noml=([128, 4 * nB], F32), negb=([128, 4 * nC], F32), esink=([128, 2 * nA], F32),
                   tmp=([128, 4 * nB], F32))
        MK = P.buf('MK', f=([128, 642], F32), b=([128, 642], BF16))
        ET = P.buf('ET', b=([128, 3 * nA * 128], BF16))
        ON = P.buf('ON', b=([128, 128], BF16), rst64=([128, 512], F32), rst128=([128, 512], F32))
        MEMN = P.buf('MEMN', b=([128, 8, 256], BF16))
        WUP = P.buf('WUP', b=([16, 2 * 2 * nC * 128], BF16))
        P.persist()

        cc = cfg.cc
        P.dma('sp', [(CO.c[:], consts_in)], writes=[CO])
        P.dma('sp', [(MK.f[:], masks_in)], writes=[MK])
        cp(P, 'pool', MK, MK.b[:], MK, MK.f[:])
        mset(P, 'pool', ON, ON.b[:], 1.0)
        mset(P, 'pool', ON, ON.rst64[:], 1.0)
        mset(P, 'pool', ON, ON.rst64[:].rearrange("p (c t) -> p c t", t=64)[:, :, 0:1], 0.0)
        mset(P, 'pool', ON, ON.rst128[:], 1.0)
        mset(P, 'pool', ON, ON.rst128[:].rearrange("p (c t) -> p c t", t=128)[:, :, 0:1], 0.0)
        nlb = 2 * nB
        lbp = cc['lbp']
        mset(P, 'dve', CO, CO.lb[:, 0:nlb], 0.0)
        tt(P, 'dve', CO, CO.tmp[:, 0:nlb], CO, CO.c[:, lbp:lbp + nlb], CO, CO.c[:, lbp + nlb:lbp + 2 * nlb], ALU.subtract)
        act(P, CO, CO.tmp[:, 0:nlb], CO, CO.tmp[:, 0:nlb], AF.Exp)
        ts(P, 'dve', CO, CO.tmp[:, 0:nlb], CO, CO.tmp[:, 0:nlb], 1.0, None, ALU.add)
        P.op('dve', lambda e: e.reciprocal(out=CO.lb[:, nlb:2 * nlb], in_=CO.tmp[:, 0:nlb]), reads=[CO], writes=[CO])
        ts(P, 'dve', CO, CO.oml[:], CO, CO.lb[:], -1.0, 1.0, ALU.mult, ALU.add)
        ts(P, 'dve', CO, CO.noml[:], CO, CO.oml[:], -1.0, None, ALU.mult)
        ts(P, 'dve', CO, CO.negb[:], CO, CO.c[:, cc['bg']:cc['bg'] + 4 * nC], -1.0, None, ALU.mult)
        act(P, CO, CO.esink[:], CO, CO.c[:, cc['sink']:cc['sink'] + 2 * nA], AF.Exp)

        with_stage = P.buf('PST', f=([128, 3 * nA * 128], F32))
        P.dma('sp', [(with_stage.f[:], etab_in)], writes=[with_stage])
        cp(P, 'pool', ET, ET.b[:], with_stage, with_stage.f[:])
        wu = P.buf('WUS', f=([16, 2 * 2 * nC * 128], F32))
        P.dma('sp', [(wu.f[:], w_up_in)], writes=[wu])
        cp(P, 'pool', WUP, WUP.b[:], wu, wu.f[:])
        mm_ = P.buf('MEMS', f=([128, 8, 256], F32), sq=([128, 8, 256], BF16), r=([128, 256], F32))
        P.dma('sp', [(mm_.f[:], memT_in.rearrange("(k p) t -> p k t", p=128))], writes=[mm_])
        act(P, mm_, mm_.sq[:], mm_, mm_.f[:], AF.Square)
        for k in range(8):
            mm(P, psb[0], psb[0].p[:, 0:256], ON, ON.b[:], mm_, mm_.sq[:, k, :], start=(k == 0), stop=(k == 7))
        act(P, mm_, mm_.r[:], psb[0], psb[0].p[:, 0:256], AF.Ln, scale=1.0 / D, bias=EPS)
        act(P, mm_, mm_.r[:], mm_, mm_.r[:], AF.Exp, scale=-0.5)
        for k in range(8):
            stt(P, 'dve', MEMN, MEMN.b[:, k, :], mm_, mm_.f[:, k, :], CO.c[:, cc['mem'] + k:cc['mem'] + k + 1],
                mm_, mm_.r[:], ALU.mult, ALU.mult, extra=[CO])
        P.end_phase()

        def phase_op1(l, HT):
            last = (l == cfg.n_layers)
            src = xT_in if l <= 1 else xs[(l - 1) % 2]
            dst = xs[l % 2]
            if l > 0:
                WO = P.buf('WO', b=([128, NKM, D], BF16))
                wst = P.ring(2, 'wost', f=([128, NKM, 128], F32))
                for c8 in range(8):
                    st = wst.next()
                    P.dma('sp', [(st.f[:].rearrange("p k c -> p (k c)"), w_out_in[l - 1, c8])], writes=[st])
                    cp(P, 'pool', WO, WO.b[:, :, c8 * 128:(c8 + 1) * 128], st, st.f[:])
                mixr = P.ring(2, 'mixb', b=([128, NKM, 512], BF16))
            xr = P.ring(2, 'xb', f=([128, 8, 512], F32))
            sqr = P.ring(2, 'sqb', b=([128, 8, 512], BF16))
            rr = P.ring(2, 'rstd', f=([128, 512], F32))
            yr = P.ring(2, 'yb', f=([128, 8, 512], F32)) if last else None
            psi = 0
            for tb in range(NBLK):
                cs = slice(tb * 512, (tb + 1) * 512)
                X = xr.next()
                P.dma('sp', [(X.f[:], src.rearrange("(k p) t -> p k t", p=128)[:, :, cs])], writes=[X])
                if l > 0:
                    MX = mixr.next()
                    P.dma('sp', [(MX.b[:], mixT.rearrange("(k p) t -> p k t", p=128)[:, :, cs])], writes=[MX])
                    for c8 in range(8):
                        ps = psb[psi % 4]
                        psi += 1
                        for k in range(NKM):
                            mm(P, ps, ps.p[:], WO, WO.b[:, k, c8 * 128:(c8 + 1) * 128], MX, MX.b[:, k, :],
                               start=(k == 0), stop=(k == NKM - 1))
                        tt(P, 'dve', X, X.f[:, c8, :], ps, ps.p[:], X, X.f[:, c8, :], ALU.add)
                    if not last:
                        P.dma('pool', [(dst.rearrange("(k p) t -> p k t", p=128)[:, :, cs], X.f[:])], reads=[X])
                if last and not cfg.final_norm:
                    P.dma('pool', [(yT.rearrange("(k p) t -> p k t", p=128)[:, :, cs], X.f[:])], reads=[X])
                    continue
                SQ = sqr.next()
                act(P, SQ, SQ.b[:], X, X.f[:], AF.Square)
                pn = psb[4 + tb % 2]
                for k in range(8):
                    mm(P, pn, pn.p[:], ON, ON.b[:], SQ, SQ.b[:, k, :], start=(k == 0), stop=(k == 7))
                R = rr.next()
                act(P, R, R.f[:], pn, pn.p[:], AF.Ln, scale=1.0 / D, bias=EPS)
                act(P, R, R.f[:], R, R.f[:], AF.Exp, scale=-0.5)
                if not last:
                    gcol = cc['norm'] + l * 8
                    for k in range(8):
                        stt(P, 'dve', HT, HT.b[:, k, cs], X, X.f[:, k, :],
                            CO.c[:, gcol + k:gcol + k + 1], R, R.f[:], ALU.mult, ALU.mult, extra=[CO])
                else:
                    Y = yr.next()
                    gcol = cc['fin']
                    for k in range(8):
                        stt(P, 'dve', Y, Y.f[:, k, :], X, X.f[:, k, :],
                            CO.c[:, gcol + k:gcol + k + 1], R, R.f[:], ALU.mult, ALU.mult, extra=[CO])
                    P.dma('pool', [(yT.rearrange("(k p) t -> p k t", p=128)[:, :, cs], Y.f[:])], reads=[Y])

        def phase_p2(l, HT):
            even = (l % 2 == 0)
            i2 = l // 2
            wsrc = (w_in_e if even else w_in_o)[i2]
            G = cfg.ge if even else cfg.go
            wst = P.ring(3, 'wst', f=([128, 8, 128], F32))
            wbr = P.ring(3, 'wb', b=([128, 8, 128], BF16))
            ob = P.ring(4, 'ob', b=([128, 512], BF16))
            of = P.ring(4, 'of', f=([128, 512], F32))
            of2 = P.ring(3, 'of2', f=([128, 512], F32))
            psr = Ring(psb[0:4])
            psu = Ring(psb[4:7])
            rfr = P.ring(2, 'rfb', b=([16, 512], BF16))

            def load_w(c0, ncol):
                st = wst.next()
                assert c0 % 128 == 0
                P.dma('sp', [(st.f[:].rearrange("p k c -> p (k c)"), wsrc[c0 // 128])], writes=[st])
                wb = wbr.next()
                cp(P, 'pool', wb, wb.b[:, :, 0:ncol], st, st.f[:, :, 0:ncol])
                return wb

            def proj(wb, ncol, tb):
                ps = psr.next()
                for k in range(8):
                    mm(P, ps, ps.p[0:ncol, :], wb, wb.b[:, k, 0:ncol], HT, HT.b[:, k, tb * 512:(tb + 1) * 512],
                       start=(k == 0), stop=(k == 7))
                return ps

            def fgroup(name, kind, dst, scale=1.0, **kw):
                c0, n = G[name]
                r = 0
                while r < n:
                    ncol = min(128, n - r)
                    wb = load_w(c0 + r, ncol)
                    for tb in range(NBLK):
                        cs = slice(tb * 512, (tb + 1) * 512)
                        ps = proj(wb, ncol, tb)
                        if kind in ('copy', 'silu'):
                            O = ob.next()
                            act(P, O, O.b[0:ncol, :], ps, ps.p[0:ncol, :], AF.Copy if kind == 'copy' else AF.Silu,
                                scale=scale)
                            P.dma('act', [(dst[r:r + ncol, cs], O.b[0:ncol, :])], reads=[O])
                        elif kind == 'hz':
                            d = kw['d']
                            h = r // 128
                            col = i2 * 2 * nB + d * nB + h
                            E = of.next()
                            act(P, E, E.f[:], ps, ps.p[:], AF.Exp, scale=-1.0)
                            act(P, E, E.f[:], E, E.f[:], AF.Ln, scale=1.0, bias=1.0)
                            act(P, E, E.f[:], E, E.f[:], AF.Exp, scale=-1.0)
                            L = of2.next()
                            act(P, L, L.f[:], E, E.f[:], AF.Ln, scale=CO.oml[:, col:col + 1], bias=CO.lb[:, col:col + 1],
                                extra=[CO])
                            ts(P, 'dve', L, L.f[:], L, L.f[:], float(np.log(1e-30)), None, ALU.max)
                            P.dma('sp', [(s_lf[d, r:r + 128, cs], L.f[:])], reads=[L])
                            O = ob.next()
                            ts(P, 'dve', O, O.b[:], E, E.f[:], CO.noml[:, col:col + 1], CO.oml[:, col:col + 1],
                               ALU.mult, ALU.add, extra=[CO])
                            P.dma('sp', [(s_k[d, r:r + 128, cs], O.b[:])], reads=[O])
                        elif kind == 'gr':
                            d = kw['d']
                            RF = rfr.next()
                            act(P, RF, RF.b[:], ps, ps.p[0:16, :], AF.Copy)
                            for hc in range(nC):
                                col = i2 * 2 * nC + d * nC + hc
                                wcol = ((i2 * 2 + d) * nC + hc) * 128
                                pu = psu.next()
                                mm(P, pu, pu.p[:], WUP, WUP.b[:, wcol:wcol + 128], RF, RF.b[:])
                                E = of.next()
                                act(P, E, E.f[:], pu, pu.p[:], AF.Exp, scale=-1.0, bias=CO.negb[:, col:col + 1], extra=[CO])
                                act(P, E, E.f[:], E, E.f[:], AF.Ln, scale=1.0, bias=1.0)
                                L = of2.next()
                                ts(P, 'dve', L, L.f[:], E, E.f[:], -1.0 / 16.0, None, ALU.mult)
                                P.dma('sp', [(s_lf[d, hc * 128:(hc + 1) * 128, cs], L.f[:])], reads=[L])
                    r += ncol

            def tgroup(name, dst):
                c0, n = G[name]
                WT = P.buf('WT', b=([128, 8, n], BF16))
                r = 0
                while r < n:
                    st = wst.next()
                    P.dma('sp', [(st.f[:].rearrange("p k c -> p (k c)"), wsrc[(c0 + r) // 128])], writes=[st])
                    nn = min(128, n - r)
                    cp(P, 'pool', WT, WT.b[:, :, r:r + nn], st, st.f[:, :, 0:nn])
                    r += 128
                for tti in range(NTILE):
                    r = 0
                    while r < n:
                        ncol = min(512, n - r)
                        ps = psr.next()
                        for k in range(8):
                            mm(P, ps, ps.p[:, 0:ncol], HT, HT.b[:, k, tti * 128:(tti + 1) * 128], WT, WT.b[:, k, r:r + ncol],
                               start=(k == 0), stop=(k == 7))
                        O = ob.next()
                        act(P, O, O.b[:, 0:ncol], ps, ps.p[:, 0:ncol], AF.Copy)
                        P.dma('act', [(dst[tti * 128:(tti + 1) * 128, r:r + ncol], O.b[:, 0:ncol])], reads=[O])
                        r += ncol

            if even:
                fgroup('qA', 'copy', s_qA, scale=0.125)
                fgroup('kA', 'copy', s_kA)
                fgroup('qM', 'copy', s_qM)
                fgroup('gA', 'silu', s_gA)
                fgroup('qB', 'silu', s_q)
                fgroup('gB', 'silu', s_g)
                fgroup('gM', 'silu', s_gM)
                fgroup('zf', 'hz', None, d=0)
                fgroup('zb', 'hz', None, d=1)
                tgroup('vA', s_vA)
                tgroup('iB', s_v)
            else:
                fgroup('qC', 'copy', s_q, scale=float(128 ** -0.5))
                fgroup('kC', 'copy', s_k[0])
                fgroup('qM', 'copy', s_qM)
                fgroup('gC', 'silu', s_g)
                fgroup('gM', 'silu', s_gM)
                fgroup('rf', 'gr', None, d=0)
                fgroup('rb', 'gr', None, d=1)
                tgroup('vC', s_v)

        def phase_kv(l):
            KV = P.buf('KV', k=([128, nM, 256], BF16), v=([128, 2, nM * 128], BF16))
            WK = P.buf('WK', b=([128, 8, 2 * nM * 128], BF16))
            wst = P.ring(2, 'kvst', f=([128, 8, 128], F32))
            for c in range(2 * nM):
                st = wst.next()
                P.dma('sp', [(st.f[:].rearrange("p k c -> p (k c)"), w_kv_in[l, c])], writes=[st])
                cp(P, 'pool', WK, WK.b[:, :, c * 128:(c + 1) * 128], st, st.f[:])
            for h in range(nM):
                ps = psb[h % 2]
                for k in range(8):
                    mm(P, ps, ps.p[:, 0:256], WK, WK.b[:, k, h * 128:(h + 1) * 128], MEMN, MEMN.b[:, k, :],
                       start=(k == 0), stop=(k == 7))
                act(P, KV, KV.k[:, h, :], ps, ps.p[:, 0:256], AF.Copy)
            for sc in range(2):
                ps = psb[2 + sc]
                for k in range(8):
                    mm(P, ps, ps.p[:, 0:nM * 128], MEMN, MEMN.b[:, k, sc * 128:(sc + 1) * 128],
                       WK, WK.b[:, k, nM * 128:2 * nM * 128], start=(k == 0), stop=(k == 7))
                act(P, KV, KV.v[:, sc, :], ps, ps.p[:, 0:nM * 128], AF.Copy)
            return KV

        def phase_mem(KV, row0):
            qr = P.ring(3, 'mq', q=([128, 512], BF16), g=([128, 512], BF16))
            pr = P.ring(3, 'mp', b=([128, 2, 512], BF16))
            rr = P.ring(2, 'mr', f=([128, 512], F32), o=([128, 512], F32))
            orr = P.ring(3, 'mo', b=([128, 512], BF16))
            scale = float(128 ** -0.5)
            it = 0
            for tb in range(NBLK):
                cs = slice(tb * 512, (tb + 1) * 512)
                for h in range(nM):
                    Q = qr.next()
                    P.dma('sp', [(Q.q[:], s_qM[h * 128:(h + 1) * 128, cs]), (Q.g[:], s_gM[h * 128:(h + 1) * 128, cs])],
                          writes=[Q])
                    PB = pr.next()
                    for sc in range(2):
                        ps = psb[(it * 2 + sc) % 4]
                        mm(P, ps, ps.p[:], KV, KV.k[:, h, sc * 128:(sc + 1) * 128], Q, Q.q[:])
                        act(P, PB, PB.b[:, sc, :], ps, ps.p[:], AF.Exp, scale=scale)
                    po = psb[4 + it % 2]
                    pd = psb[6]
                    for sc in range(2):
                        mm(P, po, po.p[:], KV, KV.v[:, sc, h * 128:(h + 1) * 128], PB, PB.b[:, sc, :],
                           start=(sc == 0), stop=(sc == 1))
                    for sc in range(2):
                        mm(P, pd, pd.p[:], ON, ON.b[:], PB, PB.b[:, sc, :], start=(sc == 0), stop=(sc == 1))
                    R = rr.next()
                    act(P, R, R.f[:], pd, pd.p[:], AF.Ln)
                    act(P, R, R.f[:], R, R.f[:], AF.Exp, scale=-1.0)
                    tt(P, 'dve', R, R.o[:], po, po.p[:], R, R.f[:], ALU.mult)
                    O = orr.next()
                    tt(P, 'pool', O, O.b[:], R, R.o[:], Q, Q.g[:], ALU.mult)
                    P.dma('pool', [(mixT[row0 + h * 128:row0 + (h + 1) * 128, cs], O.b[:])], reads=[O])
                    it += 1

        def phase_attn(i2, row0):
            lr = P.ring(2, 'aq', q=([64, 4, 512], BF16), g=([64, 4, 512], BF16), k=([64, 768], BF16),
                        v=([128, 6, 64], BF16))
            pr = P.ring(4, 'ap', b=([128, 512], BF16))
            dr = P.ring(2, 'ad', f=([64, 512], F32), o=([64, 512], F32))
            orr = P.ring(2, 'ao', b=([64, 4, 512], BF16))
            pss = Ring(psb[0:4])
            it = 0
            for tb in range(NBLK):
                cs = slice(tb * 512, (tb + 1) * 512)
                for n in range(nKV):
                    L = lr.next()
                    k0 = max(0, tb * 512 - 128)
                    k1 = min(T, tb * 512 + 640)
                    ko = k0 - (tb * 512 - 128)
                    t0 = max(0, tb * 4 - 1)
                    t1 = min(NTILE, tb * 4 + 5)
                    to = t0 - (tb * 4 - 1)
                    P.dma('sp', [
                        (L.q[:], s_qA[n * 256:(n + 1) * 256, cs].rearrange("(g d) t -> d g t", d=64)),
                        (L.g[:], s_gA[n * 256:(n + 1) * 256, cs].rearrange("(g d) t -> d g t", d=64)),
                        (L.k[:, ko:ko + (k1 - k0)], s_kA[n * 64:(n + 1) * 64, k0:k1]),
                        (L.v[:, to:to + (t1 - t0), :],
                         s_vA[t0 * 128:t1 * 128, n * 64:(n + 1) * 64].rearrange("(j p) c -> p j c", p=128)),
                    ], writes=[L])
                    O = orr.next()
                    for qt in range(4):
                        c = tb * 4 + qt
                        js = [j for j in range(3) if 0 <= c - 1 + j < NTILE]
                        PBs = []
                        for j in js:
                            ps = pss.next()
                            kc = (qt + j) * 128
                            mm(P, ps, ps.p[:].rearrange("p (g t) -> p g t", g=4), L, L.k[:, kc:kc + 128], L, L.q[:, :, qt * 128:(qt + 1) * 128])
                            PB = pr.next()
                            act(P, PB, PB.b[:], ps, ps.p[:], AF.Exp)
                            e0 = (j * nA + n * 4) * 128
                            tt(P, 'pool' if j == 1 else 'dve', PB, PB.b[:], PB, PB.b[:], ET, ET.b[:, e0:e0 + 512], ALU.mult)
                            PBs.append((j, PB))
                        po = psb[4 + it % 2]
                        pd = psb[6]
                        for ii, (j, PB) in enumerate(PBs):
                            mm(P, po, po.p[0:64, :], L, L.v[:, qt + j, :], PB, PB.b[:],
                               start=(ii == 0), stop=(ii == len(PBs) - 1))
                        for ii, (j, PB) in enumerate(PBs):
                            mm(P, pd, pd.p[0:64, :], ON, ON.b[:, 0:64], PB, PB.b[:],
                               start=(ii == 0), stop=(ii == len(PBs) - 1))
                        Dn = dr.next()
                        sk = i2 * nA + n * 4
                        tt(P, 'dve', Dn, Dn.f[:].rearrange("p (g t) -> p g t", g=4),
                           pd, pd.p[0:64, :].rearrange("p (g t) -> p g t", g=4),
                           CO, CO.esink[0:64, sk:sk + 4].unsqueeze(2).to_broadcast([64, 4, 128]), ALU.add)
                        act(P, Dn, Dn.f[:], Dn, Dn.f[:], AF.Ln)
                        act(P, Dn, Dn.f[:], Dn, Dn.f[:], AF.Exp, scale=-1.0)
                        tt(P, 'dve', Dn, Dn.o[:], po, po.p[0:64, :], Dn, Dn.f[:], ALU.mult)
                        tt(P, 'pool', O, O.b[:, :, qt * 128:(qt + 1) * 128],
                           Dn, Dn.o[:].rearrange("p (g t) -> p g t", g=4),
                           L, L.g[:, :, qt * 128:(qt + 1) * 128], ALU.mult)
                        it += 1
                    P.dma('pool', [(mixT[row0 + n * 256:row0 + (n + 1) * 256, cs].rearrange("(g d) t -> d g t", d=64),
                                    O.b[:])], reads=[O])

        def phase_scan(nh, dv, C, kdirs, gcol0, row0):
            nch = T // C
            ncb = 512 // C
            wv = 512 // ncb
            nr = dv // wv
            nvh = dv // 128
            mcol = 0 if C == 64 else 256
            rst = ON.rst64 if C == 64 else ON.rst128
            QT = [[P.buf('QT%d_%d' % (d, t_), b=([128, 512], BF16)) for t_ in range(NBLK)] for d in range(2)]
            AT = [[P.buf('AT%d_%d' % (d, t_), b=([128, 4, 128], BF16)) for t_ in range(NBLK)] for d in range(2)]
            pstA = [P.wrap('pstA', p=pst.p), P.wrap('pstB', p=pst.p)]
            SIN = [P.buf('SIN%d' % d, b=([128, nch, dv], BF16)) for d in range(2)]
            DS = P.buf('DS', f=([128, dv, nch], F32))
            DA = [P.buf('DA%d' % d, f=([128, nch], F32)) for d in range(2)]
            DR = P.buf('DR', f=([128, 32, nch], F32))
            SF = P.ring(2, 'SF', f=([128, 32, nch], F32))
            ld = P.ring(3, 'sl', lf=([128, 512], F32), k=([128, 512], BF16), q=([128, 512], BF16),
                        v=([128, 4, dv], BF16))
            w1 = P.ring(3, 'sw1', pre=([128, 512], F32), b=([128, 512], F32))
            w2 = P.ring(3, 'sw2', eb=([128, 512], F32), enb=([128, 512], F32), dl=([128, 8], F32))
            w3 = P.ring(3, 'sw3', kt=([128, 512], BF16), kh=([128, 512], BF16), khT=([128, 2, 4, 128], BF16))
            l3 = P.ring(2, 'sl3', v=([128, 4, dv], BF16), g=([128, nvh, 512], BF16))
            w4 = P.ring(2, 'sw4', sq=([128, nvh, 512], BF16), r=([128, 512], F32))
            w5 = P.ring(2, 'sw5', b=([128, nvh, 512], BF16))
            itc = [0]

            def stage_a(h, d, tb):
                hr = slice(h * 128, (h + 1) * 128)
                kd = d if kdirs else 0
                cs = slice(tb * 512, (tb + 1) * 512)
                L = ld.next()
                P.dma('sp', [
                    (L.lf[:], s_lf[d, hr, cs]), (L.k[:], s_k[kd, hr, cs]), (L.q[:], s_q[hr, cs]),
                    (L.v[:], s_v[cs, h * dv:(h + 1) * dv].rearrange("(j p) c -> p j c", p=128)),
                ], writes=[L])
                W1 = w1.next()
                pre3 = W1.pre[:].rearrange("p (c t) -> p c t", t=C)
                b3 = W1.b[:].rearrange("p (c t) -> p c t", t=C)
                if d == 0:
                    P.op('dve', lambda e, W1=W1, L=L: e.tensor_tensor_scan(
                        out=W1.pre[:], data0=rst[:], data1=L.lf[:], initial=0.0, op0=ALU.mult, op1=ALU.add),
                        reads=[L, ON], writes=[W1])
                    bsrc = W1.pre
                    edge = pre3[:, :, C - 1:C]
                else:
                    P.op('dve', lambda e, W1=W1, L=L: e.tensor_tensor_scan(
                        out=W1.b[:, ::-1], data0=rst[:], data1=L.lf[:, ::-1], initial=0.0, op0=ALU.mult, op1=ALU.add),
                        reads=[L, ON], writes=[W1])
                    bsrc = W1.b
                    edge = b3[:, :, 0:1]
                W2 = w2.next()
                act(P, W2, W2.eb[:], W1, bsrc[:], AF.Exp)
                act(P, W2, W2.dl[:, 0:ncb], W1, edge.rearrange("p c o -> p (c o)"), AF.Exp)
                btmp = W1.b if d == 0 else W1.pre
                ts(P, 'dve', W1, btmp[:], W1, bsrc[:], -80.0, None, ALU.max)
                act(P, W2, W2.enb[:], W1, btmp[:], AF.Exp, scale=-1.0)
                if d == 0:
                    j0 = tb * ncb
                    act(P, DA[d], DA[d].f[:, j0:j0 + ncb], W1, edge.rearrange("p c o -> p (c o)"), AF.Exp)
                else:
                    j0 = nch - (tb + 1) * ncb
                    act(P, DA[d], DA[d].f[:, j0:j0 + ncb][:, ::-1], W1, edge.rearrange("p c o -> p (c o)"), AF.Exp)
                tt(P, 'dve', QT[d][tb], QT[d][tb].b[:], L, L.q[:], W2, W2.eb[:], ALU.mult)
                W3 = w3.next()
                tt(P, 'dve', W3, W3.kt[:], L, L.k[:], W2, W2.enb[:], ALU.mult)
                tt(P, 'dve', W3, W3.kh[:].rearrange("p (c t) -> p c t", t=C), W3, W3.kt[:].rearrange("p (c t) -> p c t", t=C),
                   W2, W2.dl[:, 0:ncb].unsqueeze(2).to_broadcast([128, ncb, C]), ALU.mult)
                it = itc[0]
                itc[0] += 1
                return (d, tb, L, W3, it)

            def stage_b(ctx):
                d, tb, L, W3, it = ctx
                pa = psb[it % 2]
                for i4 in range(4):
                    ss = slice(i4 * 128, (i4 + 1) * 128)
                    mm(P, pa, pa.p[:, ss], W3, W3.kt[:, ss], QT[d][tb], QT[d][tb].b[:, ss])
                pq = pstA[it % 2]
                po_ = (it % 2) * 512
                for i4 in range(4):
                    P.op('pe', lambda e, W3=W3, i4=i4, po_=po_: e.transpose(pst.p[:, po_ + i4 * 128:po_ + (i4 + 1) * 128],
                                                                       W3.kh[:, i4 * 128:(i4 + 1) * 128], MK.b[:, 512:640]),
                         reads=[W3, MK], writes=[pq])
                mk = MK.b[:, mcol + d * 128:mcol + (d + 1) * 128]
                tt(P, 'dve', AT[d][tb], AT[d][tb].b[:], pa, pa.p[:].rearrange("p (i t) -> p i t", i=4),
                   MK, mk.unsqueeze(1).to_broadcast([128, 4, 128]), ALU.mult)
                if C == 64:
                    for half in range(2):
                        act(P, W3, W3.khT[:, half, :, :], pq, pst.p[:, po_:po_ + 512].rearrange("p (i t) -> p i t", i=4),
                            AF.Copy, scale=MK.f[:, 640 + half:641 + half], extra=[MK])
                else:
                    act(P, W3, W3.khT[:, 0, :, :], pq, pst.p[:, po_:po_ + 512].rearrange("p (i t) -> p i t", i=4), AF.Copy)
                for r in range(nr):
                    pd = psb[2 + (it * nr + r) % 2]
                    for cl in range(ncb):
                        i4 = (cl * C) // 128
                        p0 = (cl * C) % 128
                        slot = cl if d == 0 else ncb - 1 - cl
                        mm(P, pd, pd.p[:, slot * wv:(slot + 1) * wv], W3, W3.khT[:, p0 // 64, i4, :],
                           L, L.v[:, i4, r * wv:(r + 1) * wv])
                    j0 = tb * ncb if d == 0 else nch - (tb + 1) * ncb
                    if r % 2 == 0:
                        cp(P, 'dve', DS, DS.f[:, r * wv:(r + 1) * wv, j0:j0 + ncb],
                           pd, pd.p[:].rearrange("p (c v) -> p v c", v=wv))
                    else:
                        act(P, DS, DS.f[:, r * wv:(r + 1) * wv, j0:j0 + ncb],
                            pd, pd.p[:].rearrange("p (c v) -> p v c", v=wv), AF.Copy)

            def pass2(d):
                act(P, DR, DR.f[:], DA[d], DA[d].f[:].unsqueeze(1).to_broadcast([128, 32, nch]), AF.Copy)
                mset(P, 'pool', DR, DR.f[:, :, 0:1], 0.0)
                mset(P, 'pool', SIN[d], SIN[d].b[:, 0, :], 0.0)
                for v0 in range(0, dv, 32):
                    S = SF.next()
                    P.op('dve', lambda e, S=S, v0=v0: e.tensor_tensor_scan(
                        out=S.f[:].rearrange("p v c -> p (v c)"), data0=DR.f[:].rearrange("p v c -> p (v c)"),
                        data1=DS.f[:, v0:v0 + 32, :].rearrange("p v c -> p (v c)"), initial=0.0,
                        op0=ALU.mult, op1=ALU.add), reads=[DR, DS], writes=[S])
                    act(P, SIN[d], SIN[d].b[:, 1:nch, v0:v0 + 32],
                        S, S.f[:, :, 0:nch - 1].rearrange("p v c -> p c v"), AF.Copy)

            p3 = [0]

            def pass3_load(h, tb):
                cs = slice(tb * 512, (tb + 1) * 512)
                L3 = l3.next()
                P.dma('sp', [
                    (L3.v[:], s_v[cs, h * dv:(h + 1) * dv].rearrange("(j p) c -> p j c", p=128)),
                    (L3.g[:], s_g[h * dv:(h + 1) * dv, cs].rearrange("(a p) t -> p a t", p=128)),
                ], writes=[L3])
                return L3

            def pass3(h, tb, L3):
                cs = slice(tb * 512, (tb + 1) * 512)
                W4 = w4.next()
                pos = []
                for vh in range(nvh):
                    po = psb[4 + vh] if nvh == 2 else psb[4 + p3[0] % 2]
                    pos.append(po)
                    vs = slice(vh * 128, (vh + 1) * 128)
                    for i4 in range(4):
                        ti = tb * 4 + i4
                        ts_ = slice(i4 * 128, (i4 + 1) * 128)
                        mm(P, po, po.p[:, ts_], L3, L3.v[:, i4, vs], AT[0][tb], AT[0][tb].b[:, i4, :], start=True, stop=False)
                        mm(P, po, po.p[:, ts_], L3, L3.v[:, i4, vs], AT[1][tb], AT[1][tb].b[:, i4, :], start=False, stop=False)
                        nci = 128 // C
                        for ci in range(nci):
                            cg = ti * nci + ci
                            tcs = slice(i4 * 128 + ci * C, i4 * 128 + (ci + 1) * C)
                            mm(P, po, po.p[:, tcs], SIN[0], SIN[0].b[:, cg, vs], QT[0][tb], QT[0][tb].b[:, tcs],
                               start=False, stop=False)
                            mm(P, po, po.p[:, tcs], SIN[1], SIN[1].b[:, nch - 1 - cg, vs], QT[1][tb], QT[1][tb].b[:, tcs],
                               start=False, stop=(ci == nci - 1))
                    act(P, W4, W4.sq[:, vh, :], po, po.p[:], AF.Square)
                p3[0] += 1
                pn = psb[6]
                for vh in range(nvh):
                    mm(P, pn, pn.p[:], ON, ON.b[:], W4, W4.sq[:, vh, :], start=(vh == 0), stop=(vh == nvh - 1))
                act(P, W4, W4.r[:], pn, pn.p[:], AF.Ln, scale=1.0 / dv, bias=EPS)
                act(P, W4, W4.r[:], W4, W4.r[:], AF.Exp, scale=-0.5)
                O = w5.next()
                for vh in range(nvh):
                    gc = gcol0 + h * nvh + vh
                    stt(P, 'dve', O, O.b[:, vh, :], pos[vh], pos[vh].p[:], CO.c[:, gc:gc + 1], W4, W4.r[:],
                        ALU.mult, ALU.mult, extra=[CO])
                    tt(P, 'pool', O, O.b[:, vh, :], O, O.b[:, vh, :], L3, L3.g[:, vh, :], ALU.mult)
                P.dma('sp', [(mixT[row0 + h * dv:row0 + (h + 1) * dv, cs].rearrange("(a p) t -> p a t", p=128),
                              O.b[:])], reads=[O])

            for h in range(nh):
                iters = [(d, tb) for d in range(2) for tb in range(NBLK)]
                prev = None
                for (d, tb) in iters:
                    cur = stage_a(h, d, tb)
                    if prev is not None:
                        stage_b(prev)
                        if prev[0] == 0 and prev[1] == NBLK - 1:
                            pass2(0)
                    prev = cur
                stage_b(prev)
                pass2(1)
                nxt = pass3_load(h, 0)
                for tb in range(NBLK):
                    cur3 = nxt
                    if tb + 1 < NBLK:
                        nxt = pass3_load(h, tb + 1)
                    pass3(h, tb, cur3)


        stop = cfg.stop_after
        import os
        if os.environ.get('ONLY_SCAN'):
            phase_scan(nB, 128, 64, True, cc['hg'], nA * 64)
            P.end_phase()
            P.emit()
            return nc
        for l in range(cfg.n_layers):
            lastl = (l == cfg.n_layers - 1)
            HT = P.buf('HT', b=([128, 8, T], BF16))
            mark = P.sb_off
            phase_op1(l, HT)
            P.end_phase(keep=mark)
            if lastl and stop == 'op1':
                break
            phase_p2(l, HT)
            P.end_phase()
            if lastl and stop == 'p2':
                break
            i2 = l // 2
            if not os.environ.get('SKIP_MEM'):
                KV = phase_kv(l)
            if l % 2 == 0:
                if not os.environ.get('SKIP_MEM'):
                    phase_mem(KV, nA * 64 + nB * 128)
                    P.end_phase()
                if lastl and stop == 'mem':
                    break
                phase_scan(nB, 128, 64, True, cc['hg'] + i2 * nB, nA * 64)
                P.end_phase()
                if lastl and stop == 'scan0':
                    break
                if not os.environ.get('SKIP_ATTN'):
                    phase_attn(i2, 0)
                    P.end_phase()
            else:
                phase_mem(KV, nC * 256)
                P.end_phase()
                if lastl and stop == 'mem':
                    break
                phase_scan(nC, 256, 128, False, cc['gl'] + i2 * nC * 2, 0)
                P.end_phase()
        if stop is None:
            phase_op1(cfg.n_layers, None)
            P.end_phase()
        P.emit()
        print('ops recorded:', P.nops, {e: len(s) for e, s in P.streams.items()})
    return nc


def core_inputs(cfg, b, heads, inp):
    f32 = np.float32
    hA, hKV, hB, hM, hC = heads['A'], heads['KV'], heads['B'], heads['M'], heads['C']
    out = {}
    out['xT'] = np.ascontiguousarray(inp['x'][b].T)
    out['memT'] = np.ascontiguousarray(inp['mem'][b].T)

    def cols(base, hs, w):
        return np.concatenate([np.arange(base + h * w, base + (h + 1) * w) for h in hs])

    def padded(groups):
        sel = []
        for g in groups:
            sel.append(g)
            pad = (-len(g)) % 128
            if pad:
                sel.append(-np.ones(pad, dtype=np.int64))
        return np.concatenate(sel)

    def chunkify(w, sel):
        L, K = w.shape[0], w.shape[1]
        wp = np.zeros((L, K, len(sel)), f32)
        ok = sel >= 0
        wp[:, :, ok] = w[:, :, sel[ok]]
        nk, nch = K // 128, len(sel) // 128
        return np.ascontiguousarray(wp.reshape(L, nk, 128, nch, 128).transpose(0, 3, 2, 1, 4)).reshape(L, nch, 128, nk * 128)
    EV = np.cumsum([0, 512, 128, 128, 512, 512, 512, 512, 512, 512, 512, 512])
    names = ['qA', 'kA', 'vA', 'gA', 'qB', 'zf', 'zb', 'iB', 'gB', 'qM', 'gM']
    eo = dict(zip(names, EV[:-1]))
    sel_e = padded([
        cols(eo['qA'], hA, 64), cols(eo['kA'], hKV, 64), cols(eo['gA'], hA, 64),
        cols(eo['qB'], hB, 128), cols(eo['gB'], hB, 128), cols(eo['qM'], hM, 128), cols(eo['gM'], hM, 128),
        cols(eo['zf'], hB, 128), cols(eo['zb'], hB, 128), cols(eo['vA'], hKV, 64), cols(eo['iB'], hB, 128)])
    assert len(sel_e) == cfg.nce, (len(sel_e), cfg.nce)
    out['w_in_e'] = chunkify(inp['w_in_even'], sel_e)
    OD = np.cumsum([0, 512, 512, 1024, 1024, 16, 16, 512, 512])
    oo = dict(zip(['qC', 'kC', 'vC', 'gC', 'rf', 'rb', 'qM', 'gM'], OD[:-1]))
    sel_o = padded([
        cols(oo['qC'], hC, 128), cols(oo['kC'], hC, 128), cols(oo['gC'], hC, 256),
        cols(oo['qM'], hM, 128), cols(oo['gM'], hM, 128),
        np.arange(oo['rf'], oo['rf'] + 16), np.arange(oo['rb'], oo['rb'] + 16), cols(oo['vC'], hC, 256)])
    assert len(sel_o) == cfg.nco, (len(sel_o), cfg.nco)
    out['w_in_o'] = chunkify(inp['w_in_odd'], sel_o)
    rows_e = np.concatenate([cols(0, hA, 64), cols(512, hB, 128), cols(1024, hM, 128)])
    rows_o = np.concatenate([cols(0, hC, 256), cols(1024, hM, 128)])
    wo = np.zeros((4, cfg.mixr, D), f32)
    for l in range(4):
        if l % 2 == 0:
            wo[l] = inp['w_out_even'][l // 2][rows_e]
        else:
            wo[l] = inp['w_out_odd'][l // 2][rows_o]
    out['w_out'] = chunkify(wo, np.arange(D))
    kvc = np.concatenate([cols(0, hM, 128), cols(512, hM, 128)])
    out['w_kv'] = chunkify(inp['w_mem_kv'], kvc)
    wu = inp['w_gate_up'][:, :, :, cols(0, hC, 128)]
    out['w_up'] = np.ascontiguousarray(wu.transpose(2, 0, 1, 3).reshape(16, -1))
    cc = cfg.cc
    C = np.zeros((128, cfg.ncc), f32)
    for l in range(4):
        g = inp['norm_even'][l // 2] if l % 2 == 0 else inp['norm_odd'][l // 2]
        C[:, cc['norm'] + l * 8:cc['norm'] + (l + 1) * 8] = g.reshape(8, 128).T
    C[:, cc['fin']:cc['fin'] + 8] = inp['final_norm'].reshape(8, 128).T
    C[:, cc['mem']:cc['mem'] + 8] = inp['mem_norm'].reshape(8, 128).T
    nB, nC, nA = cfg.nB, cfg.nC, cfg.nA
    for i in range(2):
        for hi, h in enumerate(hB):
            C[:, cc['hg'] + i * nB + hi] = inp['hgrn_norm'][i, h * 128:(h + 1) * 128]
        for hi, h in enumerate(hC):
            for vh in range(2):
                C[:, cc['gl'] + i * nC * 2 + hi * 2 + vh] = inp['gla_norm'][i, h * 256 + vh * 128:h * 256 + (vh + 1) * 128]
        for d in range(2):
            for hi, h in enumerate(hB):
                C[:, cc['lbp'] + i * 2 * nB + d * nB + hi] = inp['lb_param'][i, d, h * 128:(h + 1) * 128]
            for hi, h in enumerate(hC):
                C[:, cc['bg'] + i * 2 * nC + d * nC + hi] = inp['b_gate'][i, d, h * 128:(h + 1) * 128]
        for hi, h in enumerate(hA):
            C[:, cc['sink'] + i * nA + hi] = inp['sink'][i, h]
    out['consts'] = C
    out['etab'] = np.ascontiguousarray(alibi_table(hA).reshape(128, -1))
    out['masks'] = scan_masks()
    return out


_CACHE = {}


def kernel(**inputs):
    inp = {k: np.asarray(v) for k, v in inputs.items()}
    cfg = Cfg(8, 4, 4, 4)
    heads = {'A': list(range(8)), 'KV': [0, 1], 'B': list(range(4)), 'M': list(range(4)), 'C': list(range(4))}
    if 'nc' not in _CACHE:
        _CACHE['nc'] = build(cfg)
    nc = _CACHE['nc']
    in_maps = [core_inputs(cfg, c % 4, heads, inp) for c in range(N_CORES)]
    res = run_bass_kernel_spmd(nc, in_maps, core_ids=list(range(N_CORES)))
    out = np.stack([np.ascontiguousarray(res.results[b]['yT'].T) for b in range(4)], axis=0)
    return out.astype(np.float32)
```

```python
import contextlib
import numpy as np
import concourse.bass as bass
import concourse.mybir as mybir
from concourse.bass_utils import run_bass_kernel_spmd

F32 = mybir.dt.float32
BF16 = mybir.dt.bfloat16
ALU = mybir.AluOpType
AF = mybir.ActivationFunctionType
DT_SIZE = {F32: 4, BF16: 2}

T = 4096
D = 1024
NBLK = T // 512
NTILE = T // 128
EPS = 1e-6
N_CORES = 8


class Buf:
    def __init__(self, name):
        self.name = name
        self.lw = None
        self.rd = {}
        self.dsem = None
        self.tt = {}

    def __getattr__(self, k):
        t = self.__dict__.get('tt', {})
        if k in t:
            return t[k]
        raise AttributeError(k)


class Prog:
    COMPUTE = ('pe', 'act', 'dve', 'pool')
    ENG = ('pe', 'act', 'dve', 'pool', 'sp')

    def __init__(self, nc, es, n_dsem=88, sb_limit=229000, sb_start=16640):
        self.nc = nc
        self.semh = {}
        for e in self.COMPUTE:
            self.semh[e] = es.enter_context(nc.semaphore('c_' + e))
        self.free_dsem = []
        for i in range(n_dsem):
            k = ('d', i)
            self.semh[k] = es.enter_context(nc.semaphore('d%d' % i))
            self.free_dsem.append(k)
        self.cnt = {k: 0 for k in self.semh}
        self.streams = {e: [] for e in self.ENG}
        self.waited = {e: {} for e in self.ENG}
        self.sb_off = sb_start
        self.sb_base = sb_start
        self.sb_limit = sb_limit
        self.uid = 0
        self.phase_dsems = []
        self.nops = 0

    def sbuf(self, shape, dtype, name='t'):
        self.uid += 1
        per_part = int(np.prod(shape[1:])) * DT_SIZE[dtype]
        off = (self.sb_off + 63) // 64 * 64
        assert off + per_part <= self.sb_limit, ('SBUF overflow', name, off, per_part)
        h = self.nc.alloc_sbuf_tensor_at('%s_%d' % (name, self.uid), list(shape), dtype, offset=off)
        self.sb_off = off + per_part
        return h

    def buf(self, name, **tensors):
        b = Buf(name)
        for k, (shape, dtype) in tensors.items():
            b.tt[k] = self.sbuf(shape, dtype, name + '_' + k)
        return b

    def ring(self, n, name, **tensors):
        return Ring([self.buf('%s%d' % (name, i), **tensors) for i in range(n)])

    def wrap(self, name, **handles):
        b = Buf(name)
        b.tt.update(handles)
        return b

    def need_dsem(self, b):
        if b.dsem is None:
            b.dsem = self.free_dsem.pop()
            self.phase_dsems.append(b.dsem)
        return b.dsem

    def persist(self):
        self.sb_base = self.sb_off
        self.phase_dsems = []

    def _deps(self, eng, reads, writes, is_dma):
        deps = {}

        def need(tok):
            if tok is None:
                return
            k, v = tok
            if deps.get(k, 0) < v:
                deps[k] = v
        for b in reads:
            need(b.lw)
        for b in writes:
            if b.lw is not None and (is_dma or b.lw[0] != eng):
                need(b.lw)
            for k, v in b.rd.items():
                if is_dma or k != eng:
                    need((k, v))
        w = self.waited[eng]
        out = []
        for k, v in deps.items():
            if w.get(k, 0) < v:
                w[k] = v
                out.append((k, v))
        return out

    def _commit(self, tok, reads, writes):
        for b in writes:
            b.lw = tok
            b.rd = {}
        k, v = tok
        for b in reads:
            if b in writes:
                continue
            if b.rd.get(k, 0) < v:
                b.rd[k] = v

    def op(self, eng, fn, reads=(), writes=()):
        waits = self._deps(eng, reads, writes, False)
        self.cnt[eng] += 1
        tok = (eng, self.cnt[eng])
        self.streams[eng].append((waits, fn, eng, 1))
        self._commit(tok, reads, writes)
        self.nops += 1

    def dma(self, queue, pairs, reads=(), writes=()):
        onchip = list(writes) + list(reads)
        key = self.need_dsem(onchip[0])
        for b in onchip[1:]:
            assert b.dsem is None or b.dsem == key
            b.dsem = key
        waits = self._deps(queue, reads, writes, True)
        first = True
        for (o, i) in pairs:
            self.cnt[key] += 16
            self.streams[queue].append((waits if first else [],
                                        (lambda e, o=o, i=i: e.dma_start(out=o, in_=i)), key, 16))
            first = False
            self.nops += 1
        self._commit((key, self.cnt[key]), reads, writes)

    def barrier(self):
        for e in self.ENG:
            waits = []
            w = self.waited[e]
            for k, v in self.cnt.items():
                if v > 0 and w.get(k, 0) < v and k != e:
                    w[k] = v
                    waits.append((k, v))
            if waits:
                self.streams[e].append((waits, None, None, 0))

    def end_phase(self, keep=None):
        self.barrier()
        self.free_dsem.extend(self.phase_dsems)
        self.phase_dsems = []
        self.sb_off = self.sb_base if keep is None else keep

    def emit(self):
        nc = self.nc
        semh = self.semh
        streams = self.streams
        with nc.Block() as block:
            def body(name):
                def f(e):
                    for (waits, fn, key, inc) in streams[name]:
                        for (k, v) in waits:
                            e.wait_ge(semh[k], v)
                        if fn is not None:
                            fn(e).then_inc(semh[key], inc)
                return f
            block.tensor(body('pe'))
            block.scalar(body('act'))
            block.vector(body('dve'))
            block.gpsimd(body('pool'))
            block.sync(body('sp'))


class Ring:
    def __init__(self, bufs):
        self.bufs = bufs
        self.i = 0

    def next(self):
        b = self.bufs[self.i % len(self.bufs)]
        self.i += 1
        return b


def mm(P, ob, o, lb, l, rb, r, start=True, stop=True):
    P.op('pe', lambda e: e.matmul(o, lhsT=l, rhs=r, start=start, stop=stop), reads=[lb, rb], writes=[ob])


def act(P, ob, o, ib, i, func, scale=1.0, bias=0.0, extra=()):
    P.op('act', lambda e: e.activation(out=o, in_=i, func=func, scale=scale, bias=bias),
         reads=[ib] + list(extra), writes=[ob])


def tt(P, eng, ob, o, ab, a, bb, b, op):
    P.op(eng, lambda e: e.tensor_tensor(out=o, in0=a, in1=b, op=op), reads=[ab, bb], writes=[ob])


def ts(P, eng, ob, o, ab, a, s1, s2, op0, op1=None, extra=()):
    if op1 is None:
        P.op(eng, lambda e: e.tensor_scalar(out=o, in0=a, scalar1=s1, scalar2=None, op0=op0),
             reads=[ab] + list(extra), writes=[ob])
    else:
        P.op(eng, lambda e: e.tensor_scalar(out=o, in0=a, scalar1=s1, scalar2=s2, op0=op0, op1=op1),
             reads=[ab] + list(extra), writes=[ob])


def stt(P, eng, ob, o, ab, a, scalar, bb, b, op0, op1, extra=()):
    P.op(eng, lambda e: e.scalar_tensor_tensor(out=o, in0=a, scalar=scalar, in1=b, op0=op0, op1=op1),
         reads=[ab, bb] + list(extra), writes=[ob])


def cp(P, eng, ob, o, ib, i):
    P.op(eng, lambda e: e.tensor_copy(out=o, in_=i), reads=[ib], writes=[ob])


def mset(P, eng, ob, o, val):
    P.op(eng, lambda e: e.memset(o, val), writes=[ob])


class Cfg:
    def __init__(self, nA, nB, nM, nC, n_layers=4, final_norm=True, debug=False, stop_after=None):
        self.nA, self.nB, self.nM, self.nC = nA, nB, nM, nC
        self.debug = debug
        self.stop_after = stop_after
        self.nKV = nA // 4
        self.n_layers = n_layers
        self.final_norm = final_norm
        g = []
        o = 0

        def add(name, n):
            nonlocal o
            g.append((name, o, n))
            o += (n + 127) // 128 * 128
        add('qA', nA * 64); add('kA', self.nKV * 64); add('gA', nA * 64)
        add('qB', nB * 128); add('gB', nB * 128); add('qM', nM * 128); add('gM', nM * 128)
        add('zf', nB * 128); add('zb', nB * 128)
        add('vA', self.nKV * 64); add('iB', nB * 128)
        self.ge = {n: (s, c) for n, s, c in g}
        self.nce = o
        g = []
        o = 0
        add('qC', nC * 128); add('kC', nC * 128); add('gC', nC * 256); add('qM', nM * 128); add('gM', nM * 128)
        add('rf', 16); add('rb', 16); add('vC', nC * 256)
        self.go = {n: (s, c) for n, s, c in g}
        self.nco = o
        self.mix_e = nA * 64 + nB * 128 + nM * 128
        self.mix_o = nC * 256 + nM * 128
        self.mixr = max(self.mix_e, self.mix_o)
        assert self.mix_e == self.mix_o
        c = {}
        o = 0
        for nm, n in (('norm', 32), ('fin', 8), ('mem', 8), ('hg', 2 * nB), ('gl', 2 * nC * 2),
                      ('lbp', 4 * nB), ('bg', 4 * nC), ('sink', 2 * nA)):
            c[nm] = o
            o += n
        self.cc = c
        self.ncc = o


def alibi_table(heads):
    s = np.arange(128)[:, None, None, None]
    j = np.arange(3)[None, :, None, None]
    t = np.arange(128)[None, None, None, :]
    dist = np.abs(t - s - (j - 1) * 128).astype(np.float64)
    slopes = np.array([2.0 ** (-8.0 * (h + 1) / 8) for h in heads])[None, None, :, None]
    e = np.exp(-slopes * dist) * (dist <= 128)
    return e.astype(np.float32)


def scan_masks():
    s = np.arange(128)[:, None]
    t = np.arange(128)[None, :]
    m = []
    for C in (64, 128):
        same = (s // C) == (t // C)
        m.append(((s <= t) & same).astype(np.float32))
        m.append(((s >= t) & same).astype(np.float32))
    m.append(np.eye(128, dtype=np.float32))
    p = np.arange(128)[:, None]
    m.append((p < 64).astype(np.float32))
    m.append((p >= 64).astype(np.float32))
    return np.concatenate(m, axis=1)


def build(cfg):
    nc = bass.Bass("TRN2", target_bir_lowering=False)
    nA, nB, nM, nC, nKV = cfg.nA, cfg.nB, cfg.nM, cfg.nC, cfg.nKV
    MIXR = cfg.mixr
    NKM = MIXR // 128

    def din(name, shape, dt=F32):
        return nc.dram_tensor(name, list(shape), dt, kind="ExternalInput").ap()

    def dscr(name, shape, dt):
        dbg = cfg.debug and name in ('mixT',)
        return nc.dram_tensor(name, list(shape), dt, kind="ExternalOutput" if dbg else "Internal").ap()

    xT_in = din('xT', [D, T])
    memT_in = din('memT', [D, 256])
    consts_in = din('consts', [128, cfg.ncc])
    etab_in = din('etab', [128, 3 * nA * 128])
    masks_in = din('masks', [128, 642])
    w_in_e = din('w_in_e', [2, cfg.nce // 128, 128, 1024])
    w_in_o = din('w_in_o', [2, cfg.nco // 128, 128, 1024])
    w_out_in = din('w_out', [4, 8, 128, NKM * 128])
    w_kv_in = din('w_kv', [4, 2 * nM, 128, 1024])
    w_up_in = din('w_up', [16, 2 * 2 * nC * 128])
    yT = nc.dram_tensor('yT', [D, T], F32, kind="ExternalOutput").ap()

    xs = [dscr('xs0', [D, T], F32), dscr('xs1', [D, T], F32)]
    mixT = dscr('mixT', [MIXR, T], BF16)
    s_qA = dscr('s_qA', [nA * 64, T], BF16)
    s_kA = dscr('s_kA', [nKV * 64, T], BF16)
    s_gA = dscr('s_gA', [nA * 64, T], BF16)
    NSH = max(nB, nC)
    s_q = dscr('s_q', [NSH * 128, T], BF16)
    s_k = dscr('s_k', [2, NSH * 128, T], BF16)
    s_lf = dscr('s_lf', [2, NSH * 128, T], F32)
    s_g = dscr('s_g', [max(nB * 128, nC * 256), T], BF16)
    s_qM = dscr('s_qM', [nM * 128, T], BF16)
    s_gM = dscr('s_gM', [nM * 128, T], BF16)
    s_vA = dscr('s_vA', [T, nKV * 64], BF16)
    s_v = dscr('s_v', [T, max(nB * 128, nC * 256)], BF16)

    with contextlib.ExitStack() as es:
        P = Prog(nc, es)
        psb = [P.wrap('ps%d' % i, p=nc.alloc_psum_tensor('ps%d' % i, [128, 512], F32)) for i in range(7)]
        pst = P.wrap('pst', p=nc.alloc_psum_tensor('pst', [128, 1024], BF16))

        CO = P.buf('CO', c=([128, cfg.ncc], F32), lb=([128, 4 * nB], F32), oml=([128, 4 * nB], F32),
                   noml=([128, 4 * nB], F32), negb=([128, 4 * nC], F32), esink=([128, 2 * nA], F32),
                   tmp=([128, 4 * nB], F32))
        MK = P.buf('MK', f=([128, 642], F32), b=([128, 642], BF16))
        ET = P.buf('ET', b=([128, 3 * nA * 128], BF16))
        ON = P.buf('ON', b=([128, 128], BF16), rst64=([128, 512], F32), rst128=([128, 512], F32))
        MEMN = P.buf('MEMN', b=([128, 8, 256], BF16))
        WUP = P.buf('WUP', b=([16, 2 * 2 * nC * 128], BF16))
        P.persist()

        cc = cfg.cc
        P.dma('sp', [(CO.c[:], consts_in)], writes=[CO])
        P.dma('sp', [(MK.f[:], masks_in)], writes=[MK])
        cp(P, 'pool', MK, MK.b[:], MK, MK.f[:])
        mset(P, 'pool', ON, ON.b[:], 1.0)
        mset(P, 'pool', ON, ON.rst64[:], 1.0)
        mset(P, 'pool', ON, ON.rst64[:].rearrange("p (c t) -> p c t", t=64)[:, :, 0:1], 0.0)
        mset(P, 'pool', ON, ON.rst128[:], 1.0)
        mset(P, 'pool', ON, ON.rst128[:].rearrange("p (c t) -> p c t", t=128)[:, :, 0:1], 0.0)
        nlb = 2 * nB
        lbp = cc['lbp']
        mset(P, 'dve', CO, CO.lb[:, 0:nlb], 0.0)
        tt(P, 'dve', CO, CO.tmp[:, 0:nlb], CO, CO.c[:, lbp:lbp + nlb], CO, CO.c[:, lbp + nlb:lbp + 2 * nlb], ALU.subtract)
        act(P, CO, CO.tmp[:, 0:nlb], CO, CO.tmp[:, 0:nlb], AF.Exp)
        ts(P, 'dve', CO, CO.tmp[:, 0:nlb], CO, CO.tmp[:, 0:nlb], 1.0, None, ALU.add)
        P.op('dve', lambda e: e.reciprocal(out=CO.lb[:, nlb:2 * nlb], in_=CO.tmp[:, 0:nlb]), reads=[CO], writes=[CO])
        ts(P, 'dve', CO, CO.oml[:], CO, CO.lb[:], -1.0, 1.0, ALU.mult, ALU.add)
        ts(P, 'dve', CO, CO.noml[:], CO, CO.oml[:], -1.0, None, ALU.mult)
        ts(P, 'dve', CO, CO.negb[:], CO, CO.c[:, cc['bg']:cc['bg'] + 4 * nC], -1.0, None, ALU.mult)
        act(P, CO, CO.esink[:], CO, CO.c[:, cc['sink']:cc['sink'] + 2 * nA], AF.Exp)

        with_stage = P.buf('PST', f=([128, 3 * nA * 128], F32))
        P.dma('sp', [(with_stage.f[:], etab_in)], writes=[with_stage])
        cp(P, 'pool', ET, ET.b[:], with_stage, with_stage.f[:])
        wu = P.buf('WUS', f=([16, 2 * 2 * nC * 128], F32))
        P.dma('sp', [(wu.f[:], w_up_in)], writes=[wu])
        cp(P, 'pool', WUP, WUP.b[:], wu, wu.f[:])
        mm_ = P.buf('MEMS', f=([128, 8, 256], F32), sq=([128, 8, 256], BF16), r=([128, 256], F32))
        P.dma('sp', [(mm_.f[:], memT_in.rearrange("(k p) t -> p k t", p=128))], writes=[mm_])
        act(P, mm_, mm_.sq[:], mm_, mm_.f[:], AF.Square)
        for k in range(8):
            mm(P, psb[0], psb[0].p[:, 0:256], ON, ON.b[:], mm_, mm_.sq[:, k, :], start=(k == 0), stop=(k == 7))
        act(P, mm_, mm_.r[:], psb[0], psb[0].p[:, 0:256], AF.Ln, scale=1.0 / D, bias=EPS)
        act(P, mm_, mm_.r[:], mm_, mm_.r[:], AF.Exp, scale=-0.5)
        for k in range(8):
            stt(P, 'dve', MEMN, MEMN.b[:, k, :], mm_, mm_.f[:, k, :], CO.c[:, cc['mem'] + k:cc['mem'] + k + 1],
                mm_, mm_.r[:], ALU.mult, ALU.mult, extra=[CO])
        P.end_phase()

        def phase_op1(l, HT):
            last = (l == cfg.n_layers)
            src = xT_in if l <= 1 else xs[(l - 1) % 2]
            dst = xs[l % 2]
            if l > 0:
                WO = P.buf('WO', b=([128, NKM, D], BF16))
                wst = P.ring(2, 'wost', f=([128, NKM, 128], F32))
                for c8 in range(8):
                    st = wst.next()
                    P.dma('sp', [(st.f[:].rearrange("p k c -> p (k c)"), w_out_in[l - 1, c8])], writes=[st])
                    cp(P, 'pool', WO, WO.b[:, :, c8 * 128:(c8 + 1) * 128], st, st.f[:])
                mixr = P.ring(2, 'mixb', b=([128, NKM, 512], BF16))
            xr = P.ring(2, 'xb', f=([128, 8, 512], F32))
            sqr = P.ring(2, 'sqb', b=([128, 8, 512], BF16))
            rr = P.ring(2, 'rstd', f=([128, 512], F32))
            yr = P.ring(2, 'yb', f=([128, 8, 512], F32)) if last else None
            psi = 0
            for tb in range(NBLK):
                cs = slice(tb * 512, (tb + 1) * 512)
                X = xr.next()
                P.dma('sp', [(X.f[:], src.rearrange("(k p) t -> p k t", p=128)[:, :, cs])], writes=[X])
                if l > 0:
                    MX = mixr.next()
                    P.dma('sp', [(MX.b[:], mixT.rearrange("(k p) t -> p k t", p=128)[:, :, cs])], writes=[MX])
                    for c8 in range(8):
                        ps = psb[psi % 4]
                        psi += 1
                        for k in range(NKM):
                            mm(P, ps, ps.p[:], WO, WO.b[:, k, c8 * 128:(c8 + 1) * 128], MX, MX.b[:, k, :],
                               start=(k == 0), stop=(k == NKM - 1))
                        tt(P, 'dve', X, X.f[:, c8, :], ps, ps.p[:], X, X.f[:, c8, :], ALU.add)
                    if not last:
                        P.dma('pool', [(dst.rearrange("(k p) t -> p k t", p=128)[:, :, cs], X.f[:])], reads=[X])
                if last and not cfg.final_norm:
                    P.dma('pool', [(yT.rearrange("(k p) t -> p k t", p=128)[:, :, cs], X.f[:])], reads=[X])
                    continue
                SQ = sqr.next()
                act(P, SQ, SQ.b[:], X, X.f[:], AF.Square)
                pn = psb[4 + tb % 2]
                for k in range(8):
                    mm(P, pn, pn.p[:], ON, ON.b[:], SQ, SQ.b[:, k, :], start=(k == 0), stop=(k == 7))
                R = rr.next()
                act(P, R, R.f[:], pn, pn.p[:], AF.Ln, scale=1.0 / D, bias=EPS)
                act(P, R, R.f[:], R, R.f[:], AF.Exp, scale=-0.5)
                if not last:
                    gcol = cc['norm'] + l * 8
                    for k in range(8):
                        stt(P, 'dve', HT, HT.b[:, k, cs], X, X.f[:, k, :],
                            CO.c[:, gcol + k:gcol + k + 1], R, R.f[:], ALU.mult, ALU.mult, extra=[CO])
                else:
                    Y = yr.next()
                    gcol = cc['fin']
                    for k in range(8):
                        stt(P, 'dve', Y, Y.f[:, k, :], X, X.f[:, k, :],
                            CO.c[:, gcol + k:gcol + k + 1], R, R.f[:], ALU.mult, ALU.mult, extra=[CO])
                    P.dma('pool', [(yT.rearrange("(k p) t -> p k t", p=128)[:, :, cs], Y.f[:])], reads=[Y])

        def phase_p2(l, HT):
            even = (l % 2 == 0)
            i2 = l // 2
            wsrc = (w_in_e if even else w_in_o)[i2]
            G = cfg.ge if even else cfg.go
            wst = P.ring(3, 'wst', f=([128, 8, 128], F32))
            wbr = P.ring(3, 'wb', b=([128, 8, 128], BF16))
            ob = P.ring(4, 'ob', b=([128, 512], BF16))
            of = P.ring(4, 'of', f=([128, 512], F32))
            of2 = P.ring(4, 'of2', f=([128, 512], F32))
            psr = Ring(psb[0:4])
            psu = Ring(psb[4:7])
            rfr = P.ring(2, 'rfb', b=([16, 512], BF16))

            def load_w(c0, ncol):
                st = wst.next()
                assert c0 % 128 == 0
                P.dma('sp', [(st.f[:].rearrange("p k c -> p (k c)"), wsrc[c0 // 128])], writes=[st])
                wb = wbr.next()
                cp(P, 'pool', wb, wb.b[:, :, 0:ncol], st, st.f[:, :, 0:ncol])
                return wb

            def proj(wb, ncol, tb):
                ps = psr.next()
                for k in range(8):
                    mm(P, ps, ps.p[0:ncol, :], wb, wb.b[:, k, 0:ncol], HT, HT.b[:, k, tb * 512:(tb + 1) * 512],
                       start=(k == 0), stop=(k == 7))
                return ps

            def fgroup(name, kind, dst, scale=1.0, **kw):
                c0, n = G[name]
                r = 0
                while r < n:
                    ncol = min(128, n - r)
                    wb = load_w(c0 + r, ncol)
                    for tb in range(NBLK):
                        cs = slice(tb * 512, (tb + 1) * 512)
                        if kind == 'hz' and tb % 2 == 1:
                            continue
                        ps = proj(wb, ncol, tb)
                        if kind in ('copy', 'silu'):
                            O = ob.next()
                            act(P, O, O.b[0:ncol, :], ps, ps.p[0:ncol, :], AF.Copy if kind == 'copy' else AF.Silu,
                                scale=scale)
                            P.dma('act', [(dst[r:r + ncol, cs], O.b[0:ncol, :])], reads=[O])
                        elif kind == 'hz':
                            d = kw['d']
                            h = r // 128
                            col = i2 * 2 * nB + d * nB + h
                            tbs = [tb, tb + 1]
                            pss2 = [ps, proj(wb, ncol, tb + 1)]
                            Es = [of.next(), of.next()]
                            for E, p_ in zip(Es, pss2):
                                act(P, E, E.f[:], p_, p_.p[:], AF.Exp, scale=-1.0)
                            for E in Es:
                                act(P, E, E.f[:], E, E.f[:], AF.Ln, scale=1.0, bias=1.0)
                            for E in Es:
                                act(P, E, E.f[:], E, E.f[:], AF.Exp, scale=-1.0)
                            Ls2 = [of2.next(), of2.next()]
                            for E, L in zip(Es, Ls2):
                                act(P, L, L.f[:], E, E.f[:], AF.Ln, scale=CO.oml[:, col:col + 1], bias=CO.lb[:, col:col + 1],
                                    extra=[CO])
                            for L, tbx in zip(Ls2, tbs):
                                ts(P, 'dve', L, L.f[:], L, L.f[:], float(np.log(1e-30)), None, ALU.max)
                                P.dma('sp', [(s_lf[d, r:r + 128, tbx * 512:(tbx + 1) * 512], L.f[:])], reads=[L])
                            for E, tbx in zip(Es, tbs):
                                O = ob.next()
                                ts(P, 'dve', O, O.b[:], E, E.f[:], CO.noml[:, col:col + 1], CO.oml[:, col:col + 1],
                                   ALU.mult, ALU.add, extra=[CO])
                                P.dma('sp', [(s_k[d, r:r + 128, tbx * 512:(tbx + 1) * 512], O.b[:])], reads=[O])
                        elif kind == 'gr':
                            d = kw['d']
                            RF = rfr.next()
                            act(P, RF, RF.b[:], ps, ps.p[0:16, :], AF.Copy)
                            for hc0 in range(0, nC, 2):
                                hcs = [hc0, hc0 + 1] if hc0 + 1 < nC else [hc0]
                                pus, Es = [], []
                                for hc in hcs:
                                    wcol = ((i2 * 2 + d) * nC + hc) * 128
                                    pu = psu.next()
                                    mm(P, pu, pu.p[:], WUP, WUP.b[:, wcol:wcol + 128], RF, RF.b[:])
                                    pus.append(pu)
                                for hc, pu in zip(hcs, pus):
                                    col = i2 * 2 * nC + d * nC + hc
                                    E = of.next()
                                    act(P, E, E.f[:], pu, pu.p[:], AF.Exp, scale=-1.0, bias=CO.negb[:, col:col + 1], extra=[CO])
                                    Es.append(E)
                                for E in Es:
                                    act(P, E, E.f[:], E, E.f[:], AF.Ln, scale=1.0, bias=1.0)
                                for hc, E in zip(hcs, Es):
                                    L = of2.next()
                                    ts(P, 'dve', L, L.f[:], E, E.f[:], -1.0 / 16.0, None, ALU.mult)
                                    P.dma('sp', [(s_lf[d, hc * 128:(hc + 1) * 128, cs], L.f[:])], reads=[L])
                    r += ncol

            def tgroup(name, dst):
                c0, n = G[name]
                WT = P.buf('WT', b=([128, 8, n], BF16))
                r = 0
                while r < n:
                    st = wst.next()
                    P.dma('sp', [(st.f[:].rearrange("p k c -> p (k c)"), wsrc[(c0 + r) // 128])], writes=[st])
                    nn = min(128, n - r)
                    cp(P, 'pool', WT, WT.b[:, :, r:r + nn], st, st.f[:, :, 0:nn])
                    r += 128
                for tti in range(NTILE):
                    r = 0
                    while r < n:
                        ncol = min(512, n - r)
                        ps = psr.next()
                        for k in range(8):
                            mm(P, ps, ps.p[:, 0:ncol], HT, HT.b[:, k, tti * 128:(tti + 1) * 128], WT, WT.b[:, k, r:r + ncol],
                               start=(k == 0), stop=(k == 7))
                        O = ob.next()
                        act(P, O, O.b[:, 0:ncol], ps, ps.p[:, 0:ncol], AF.Copy)
                        P.dma('act', [(dst[tti * 128:(tti + 1) * 128, r:r + ncol], O.b[:, 0:ncol])], reads=[O])
                        r += ncol

            if even:
                fgroup('qA', 'copy', s_qA, scale=0.125)
                fgroup('kA', 'copy', s_kA)
                fgroup('qM', 'copy', s_qM)
                fgroup('gA', 'silu', s_gA)
                fgroup('qB', 'silu', s_q)
                fgroup('gB', 'silu', s_g)
                fgroup('gM', 'silu', s_gM)
                fgroup('zf', 'hz', None, d=0)
                fgroup('zb', 'hz', None, d=1)
                tgroup('vA', s_vA)
                tgroup('iB', s_v)
            else:
                fgroup('qC', 'copy', s_q, scale=float(128 ** -0.5))
                fgroup('kC', 'copy', s_k[0])
                fgroup('qM', 'copy', s_qM)
                fgroup('gC', 'silu', s_g)
                fgroup('gM', 'silu', s_gM)
                fgroup('rf', 'gr', None, d=0)
                fgroup('rb', 'gr', None, d=1)
                tgroup('vC', s_v)

        def phase_kv(l):
            KV = P.buf('KV', k=([128, nM, 256], BF16), v=([128, 2, nM * 128], BF16))
            WK = P.buf('WK', b=([128, 8, 2 * nM * 128], BF16))
            wst = P.ring(2, 'kvst', f=([128, 8, 128], F32))
            for c in range(2 * nM):
                st = wst.next()
                P.dma('sp', [(st.f[:].rearrange("p k c -> p (k c)"), w_kv_in[l, c])], writes=[st])
                cp(P, 'pool', WK, WK.b[:, :, c * 128:(c + 1) * 128], st, st.f[:])
            for h in range(nM):
                ps = psb[h % 2]
                for k in range(8):
                    mm(P, ps, ps.p[:, 0:256], WK, WK.b[:, k, h * 128:(h + 1) * 128], MEMN, MEMN.b[:, k, :],
                       start=(k == 0), stop=(k == 7))
                act(P, KV, KV.k[:, h, :], ps, ps.p[:, 0:256], AF.Copy)
            for sc in range(2):
                ps = psb[2 + sc]
                for k in range(8):
                    mm(P, ps, ps.p[:, 0:nM * 128], MEMN, MEMN.b[:, k, sc * 128:(sc + 1) * 128],
                       WK, WK.b[:, k, nM * 128:2 * nM * 128], start=(k == 0), stop=(k == 7))
                act(P, KV, KV.v[:, sc, :], ps, ps.p[:, 0:nM * 128], AF.Copy)
            return KV

        def phase_mem(KV, row0):
            qr = P.ring(3, 'mq', q=([128, 512], BF16), g=([128, 512], BF16))
            pr = P.ring(3, 'mp', b=([128, 2, 512], BF16))
            rr = P.ring(2, 'mr', f=([128, 512], F32), o=([128, 512], F32))
            orr = P.ring(3, 'mo', b=([128, 512], BF16))
            scale = float(128 ** -0.5)
            it = 0
            for tb in range(NBLK):
                cs = slice(tb * 512, (tb + 1) * 512)
                for h in range(nM):
                    Q = qr.next()
                    P.dma('sp', [(Q.q[:], s_qM[h * 128:(h + 1) * 128, cs]), (Q.g[:], s_gM[h * 128:(h + 1) * 128, cs])],
                          writes=[Q])
                    PB = pr.next()
                    for sc in range(2):
                        ps = psb[(it * 2 + sc) % 4]
                        mm(P, ps, ps.p[:], KV, KV.k[:, h, sc * 128:(sc + 1) * 128], Q, Q.q[:])
                        act(P, PB, PB.b[:, sc, :], ps, ps.p[:], AF.Exp, scale=scale)
                    po = psb[4 + it % 2]
                    pd = psb[6]
                    for sc in range(2):
                        mm(P, po, po.p[:], KV, KV.v[:, sc, h * 128:(h + 1) * 128], PB, PB.b[:, sc, :],
                           start=(sc == 0), stop=(sc == 1))
                    for sc in range(2):
                        mm(P, pd, pd.p[:], ON, ON.b[:], PB, PB.b[:, sc, :], start=(sc == 0), stop=(sc == 1))
                    R = rr.next()
                    act(P, R, R.f[:], pd, pd.p[:], AF.Ln)
                    act(P, R, R.f[:], R, R.f[:], AF.Exp, scale=-1.0)
                    tt(P, 'dve', R, R.o[:], po, po.p[:], R, R.f[:], ALU.mult)
                    O = orr.next()
                    tt(P, 'pool', O, O.b[:], R, R.o[:], Q, Q.g[:], ALU.mult)
                    P.dma('pool', [(mixT[row0 + h * 128:row0 + (h + 1) * 128, cs], O.b[:])], reads=[O])
                    it += 1

        def phase_attn(i2, row0):
            lr = P.ring(2, 'aq', q=([64, 4, 512], BF16), g=([64, 4, 512], BF16), k=([64, 768], BF16),
                        v=([128, 6, 64], BF16))
            pr = P.ring(9, 'ap', b=([128, 512], BF16))
            dr = P.ring(2, 'ad', f=([64, 512], F32), o=([64, 512], F32))
            orr = P.ring(2, 'ao', b=([64, 4, 512], BF16))
            pss = Ring(psb[0:4])
            cnt = [0]

            def load(tb, n):
                cs = slice(tb * 512, (tb + 1) * 512)
                L = lr.next()
                k0 = max(0, tb * 512 - 128)
                k1 = min(T, tb * 512 + 640)
                ko = k0 - (tb * 512 - 128)
                t0 = max(0, tb * 4 - 1)
                t1 = min(NTILE, tb * 4 + 5)
                to = t0 - (tb * 4 - 1)
                P.dma('sp', [
                    (L.q[:], s_qA[n * 256:(n + 1) * 256, cs].rearrange("(g d) t -> d g t", d=64)),
                    (L.g[:], s_gA[n * 256:(n + 1) * 256, cs].rearrange("(g d) t -> d g t", d=64)),
                    (L.k[:, ko:ko + (k1 - k0)], s_kA[n * 64:(n + 1) * 64, k0:k1]),
                    (L.v[:, to:to + (t1 - t0), :],
                     s_vA[t0 * 128:t1 * 128, n * 64:(n + 1) * 64].rearrange("(j p) c -> p j c", p=128)),
                ], writes=[L])
                return L

            def stage_a(L, tb, n, qt):
                c = tb * 4 + qt
                js = [j for j in range(3) if 0 <= c - 1 + j < NTILE]
                PBs = []
                for j in js:
                    ps = pss.next()
                    kc = (qt + j) * 128
                    mm(P, ps, ps.p[:].rearrange("p (g t) -> p g t", g=4), L, L.k[:, kc:kc + 128], L, L.q[:, :, qt * 128:(qt + 1) * 128])
                    PB = pr.next()
                    act(P, PB, PB.b[:], ps, ps.p[:], AF.Exp)
                    e0 = (j * nA + n * 4) * 128
                    tt(P, 'pool' if j == 1 else 'dve', PB, PB.b[:], PB, PB.b[:], ET, ET.b[:, e0:e0 + 512], ALU.mult)
                    PBs.append((j, PB))
                return PBs

            def stage_b(L, O, tb, n, qt, PBs):
                it = cnt[0]
                cnt[0] += 1
                po = psb[4 + it % 2]
                pd = psb[6]
                for ii, (j, PB) in enumerate(PBs):
                    mm(P, po, po.p[0:64, :], L, L.v[:, qt + j, :], PB, PB.b[:],
                       start=(ii == 0), stop=(ii == len(PBs) - 1))
                for ii, (j, PB) in enumerate(PBs):
                    mm(P, pd, pd.p[0:64, :], ON, ON.b[:, 0:64], PB, PB.b[:],
                       start=(ii == 0), stop=(ii == len(PBs) - 1))
                Dn = dr.next()
                sk = i2 * nA + n * 4
                tt(P, 'dve', Dn, Dn.f[:].rearrange("p (g t) -> p g t", g=4),
                   pd, pd.p[0:64, :].rearrange("p (g t) -> p g t", g=4),
                   CO, CO.esink[0:64, sk:sk + 4].unsqueeze(2).to_broadcast([64, 4, 128]), ALU.add)
                act(P, Dn, Dn.f[:], Dn, Dn.f[:], AF.Ln)
                act(P, Dn, Dn.f[:], Dn, Dn.f[:], AF.Exp, scale=-1.0)
                tt(P, 'dve', Dn, Dn.o[:], po, po.p[0:64, :], Dn, Dn.f[:], ALU.mult)
                tt(P, 'pool', O, O.b[:, :, qt * 128:(qt + 1) * 128],
                   Dn, Dn.o[:].rearrange("p (g t) -> p g t", g=4),
                   L, L.g[:, :, qt * 128:(qt + 1) * 128], ALU.mult)

            items = [(tb, n, qt) for tb in range(NBLK) for n in range(nKV) for qt in range(4)]
            Ls = {}
            Os = {}
            pend = None
            for (tb, n, qt) in items:
                if qt == 0:
                    Ls[(tb, n)] = load(tb, n)
                L = Ls[(tb, n)]
                PBs = stage_a(L, tb, n, qt)
                if pend is not None:
                    ptb, pn_, pqt, pL, pPBs = pend
                    if pqt == 0:
                        Os[(ptb, pn_)] = orr.next()
                    stage_b(pL, Os[(ptb, pn_)], ptb, pn_, pqt, pPBs)
                    if pqt == 3:
                        cs = slice(ptb * 512, (ptb + 1) * 512)
                        P.dma('pool', [(mixT[row0 + pn_ * 256:row0 + (pn_ + 1) * 256, cs].rearrange("(g d) t -> d g t", d=64),
                                        Os[(ptb, pn_)].b[:])], reads=[Os[(ptb, pn_)]])
                pend = (tb, n, qt, L, PBs)
            ptb, pn_, pqt, pL, pPBs = pend
            stage_b(pL, Os[(ptb, pn_)], ptb, pn_, pqt, pPBs)
            cs = slice(ptb * 512, (ptb + 1) * 512)
            P.dma('pool', [(mixT[row0 + pn_ * 256:row0 + (pn_ + 1) * 256, cs].rearrange("(g d) t -> d g t", d=64),
                            Os[(ptb, pn_)].b[:])], reads=[Os[(ptb, pn_)]])


        def phase_scan(nh, dv, C, kdirs, gcol0, row0):
            nch = T // C
            ncb = 512 // C
            wv = 512 // ncb
            nr = dv // wv
            nvh = dv // 128
            mcol = 0 if C == 64 else 256
            rst = ON.rst64 if C == 64 else ON.rst128
            QT = [[P.buf('QT%d_%d' % (d, t_), b=([128, 512], BF16)) for t_ in range(NBLK)] for d in range(2)]
            AT = [[P.buf('AT%d_%d' % (d, t_), b=([128, 4, 128], BF16)) for t_ in range(NBLK)] for d in range(2)]
            pstA = [P.wrap('pstA', p=pst.p), P.wrap('pstB', p=pst.p)]
            SIN = [P.buf('SIN%d' % d, b=([128, nch, dv], BF16)) for d in range(2)]
            DS = P.buf('DS', f=([128, dv, nch], F32))
            DA = [P.buf('DA%d' % d, f=([128, nch], F32)) for d in range(2)]
            DR = P.buf('DR', f=([128, 32, nch], F32))
            SF = P.ring(2, 'SF', f=([128, 32, nch], F32))
            ld = P.ring(3, 'sl', lf=([128, 512], F32), k=([128, 512], BF16), q=([128, 512], BF16),
                        v=([128, 4, dv], BF16))
            w1 = P.ring(3, 'sw1', pre=([128, 512], F32), b=([128, 512], F32))
            w2 = P.ring(3, 'sw2', eb=([128, 512], F32), enb=([128, 512], F32), dl=([128, 8], F32))
            w3 = P.ring(3, 'sw3', kt=([128, 512], BF16), kh=([128, 512], BF16), khT=([128, 2, 4, 128], BF16))
            l3 = P.ring(2, 'sl3', v=([128, 4, dv], BF16), g=([128, nvh, 512], BF16))
            w4 = P.ring(2, 'sw4', sq=([128, nvh, 512], BF16), r=([128, 512], F32))
            w5 = P.ring(2, 'sw5', b=([128, nvh, 512], BF16))
            itc = [0]

            def stage_a(h, d, tb):
                hr = slice(h * 128, (h + 1) * 128)
                kd = d if kdirs else 0
                cs = slice(tb * 512, (tb + 1) * 512)
                L = ld.next()
                P.dma('sp', [
                    (L.lf[:], s_lf[d, hr, cs]), (L.k[:], s_k[kd, hr, cs]), (L.q[:], s_q[hr, cs]),
                    (L.v[:], s_v[cs, h * dv:(h + 1) * dv].rearrange("(j p) c -> p j c", p=128)),
                ], writes=[L])
                W1 = w1.next()
                pre3 = W1.pre[:].rearrange("p (c t) -> p c t", t=C)
                b3 = W1.b[:].rearrange("p (c t) -> p c t", t=C)
                if d == 0:
                    P.op('dve', lambda e, W1=W1, L=L: e.tensor_tensor_scan(
                        out=W1.pre[:], data0=rst[:], data1=L.lf[:], initial=0.0, op0=ALU.mult, op1=ALU.add),
                        reads=[L, ON], writes=[W1])
                    bsrc = W1.pre
                    edge = pre3[:, :, C - 1:C]
                else:
                    P.op('dve', lambda e, W1=W1, L=L: e.tensor_tensor_scan(
                        out=W1.b[:, ::-1], data0=rst[:], data1=L.lf[:, ::-1], initial=0.0, op0=ALU.mult, op1=ALU.add),
                        reads=[L, ON], writes=[W1])
                    bsrc = W1.b
                    edge = b3[:, :, 0:1]
                W2 = w2.next()
                act(P, W2, W2.eb[:], W1, bsrc[:], AF.Exp)
                act(P, W2, W2.dl[:, 0:ncb], W1, edge.rearrange("p c o -> p (c o)"), AF.Exp)
                btmp = W1.b if d == 0 else W1.pre
                ts(P, 'dve', W1, btmp[:], W1, bsrc[:], -80.0, None, ALU.max)
                act(P, W2, W2.enb[:], W1, btmp[:], AF.Exp, scale=-1.0)
                if d == 0:
                    j0 = tb * ncb
                    act(P, DA[d], DA[d].f[:, j0:j0 + ncb], W1, edge.rearrange("p c o -> p (c o)"), AF.Exp)
                else:
                    j0 = nch - (tb + 1) * ncb
                    act(P, DA[d], DA[d].f[:, j0:j0 + ncb][:, ::-1], W1, edge.rearrange("p c o -> p (c o)"), AF.Exp)
                tt(P, 'dve', QT[d][tb], QT[d][tb].b[:], L, L.q[:], W2, W2.eb[:], ALU.mult)
                W3 = w3.next()
                tt(P, 'dve', W3, W3.kt[:], L, L.k[:], W2, W2.enb[:], ALU.mult)
                tt(P, 'dve', W3, W3.kh[:].rearrange("p (c t) -> p c t", t=C), W3, W3.kt[:].rearrange("p (c t) -> p c t", t=C),
                   W2, W2.dl[:, 0:ncb].unsqueeze(2).to_broadcast([128, ncb, C]), ALU.mult)
                it = itc[0]
                itc[0] += 1
                return (d, tb, L, W3, it)

            def stage_b(ctx):
                d, tb, L, W3, it = ctx
                pa = psb[it % 2]
                for i4 in range(4):
                    ss = slice(i4 * 128, (i4 + 1) * 128)
                    mm(P, pa, pa.p[:, ss], W3, W3.kt[:, ss], QT[d][tb], QT[d][tb].b[:, ss])
                pq = pstA[it % 2]
                po_ = (it % 2) * 512
                for i4 in range(4):
                    P.op('pe', lambda e, W3=W3, i4=i4, po_=po_: e.transpose(pst.p[:, po_ + i4 * 128:po_ + (i4 + 1) * 128],
                                                                       W3.kh[:, i4 * 128:(i4 + 1) * 128], MK.b[:, 512:640]),
                         reads=[W3, MK], writes=[pq])
                mk = MK.b[:, mcol + d * 128:mcol + (d + 1) * 128]
                tt(P, 'dve', AT[d][tb], AT[d][tb].b[:], pa, pa.p[:].rearrange("p (i t) -> p i t", i=4),
                   MK, mk.unsqueeze(1).to_broadcast([128, 4, 128]), ALU.mult)
                if C == 64:
                    for half in range(2):
                        act(P, W3, W3.khT[:, half, :, :], pq, pst.p[:, po_:po_ + 512].rearrange("p (i t) -> p i t", i=4),
                            AF.Copy, scale=MK.f[:, 640 + half:641 + half], extra=[MK])
                else:
                    act(P, W3, W3.khT[:, 0, :, :], pq, pst.p[:, po_:po_ + 512].rearrange("p (i t) -> p i t", i=4), AF.Copy)
                for r in range(nr):
                    pd = psb[2 + (it * nr + r) % 2]
                    for cl in range(ncb):
                        i4 = (cl * C) // 128
                        p0 = (cl * C) % 128
                        slot = cl if d == 0 else ncb - 1 - cl
                        mm(P, pd, pd.p[:, slot * wv:(slot + 1) * wv], W3, W3.khT[:, p0 // 64, i4, :],
                           L, L.v[:, i4, r * wv:(r + 1) * wv])
                    j0 = tb * ncb if d == 0 else nch - (tb + 1) * ncb
                    if r % 2 == 0:
                        cp(P, 'dve', DS, DS.f[:, r * wv:(r + 1) * wv, j0:j0 + ncb],
                           pd, pd.p[:].rearrange("p (c v) -> p v c", v=wv))
                    else:
                        act(P, DS, DS.f[:, r * wv:(r + 1) * wv, j0:j0 + ncb],
                            pd, pd.p[:].rearrange("p (c v) -> p v c", v=wv), AF.Copy)

            def pass2(d):
                act(P, DR, DR.f[:], DA[d], DA[d].f[:].unsqueeze(1).to_broadcast([128, 32, nch]), AF.Copy)
                mset(P, 'pool', DR, DR.f[:, :, 0:1], 0.0)
                mset(P, 'pool', SIN[d], SIN[d].b[:, 0, :], 0.0)
                for v0 in range(0, dv, 32):
                    S = SF.next()
                    P.op('dve', lambda e, S=S, v0=v0: e.tensor_tensor_scan(
                        out=S.f[:].rearrange("p v c -> p (v c)"), data0=DR.f[:].rearrange("p v c -> p (v c)"),
                        data1=DS.f[:, v0:v0 + 32, :].rearrange("p v c -> p (v c)"), initial=0.0,
                        op0=ALU.mult, op1=ALU.add), reads=[DR, DS], writes=[S])
                    act(P, SIN[d], SIN[d].b[:, 1:nch, v0:v0 + 32],
                        S, S.f[:, :, 0:nch - 1].rearrange("p v c -> p c v"), AF.Copy)

            p3 = [0]

            def pass3_load(h, tb):
                cs = slice(tb * 512, (tb + 1) * 512)
                L3 = l3.next()
                P.dma('sp', [
                    (L3.v[:], s_v[cs, h * dv:(h + 1) * dv].rearrange("(j p) c -> p j c", p=128)),
                    (L3.g[:], s_g[h * dv:(h + 1) * dv, cs].rearrange("(a p) t -> p a t", p=128)),
                ], writes=[L3])
                return L3

            def pass3(h, tb, L3):
                cs = slice(tb * 512, (tb + 1) * 512)
                W4 = w4.next()
                pos = []
                for vh in range(nvh):
                    po = psb[4 + vh] if nvh == 2 else psb[4 + p3[0] % 2]
                    pos.append(po)
                    vs = slice(vh * 128, (vh + 1) * 128)
                    for i4 in range(4):
                        ti = tb * 4 + i4
                        ts_ = slice(i4 * 128, (i4 + 1) * 128)
                        mm(P, po, po.p[:, ts_], L3, L3.v[:, i4, vs], AT[0][tb], AT[0][tb].b[:, i4, :], start=True, stop=False)
                        mm(P, po, po.p[:, ts_], L3, L3.v[:, i4, vs], AT[1][tb], AT[1][tb].b[:, i4, :], start=False, stop=False)
                        nci = 128 // C
                        for ci in range(nci):
                            cg = ti * nci + ci
                            tcs = slice(i4 * 128 + ci * C, i4 * 128 + (ci + 1) * C)
                            mm(P, po, po.p[:, tcs], SIN[0], SIN[0].b[:, cg, vs], QT[0][tb], QT[0][tb].b[:, tcs],
                               start=False, stop=False)
                            mm(P, po, po.p[:, tcs], SIN[1], SIN[1].b[:, nch - 1 - cg, vs], QT[1][tb], QT[1][tb].b[:, tcs],
                               start=False, stop=(ci == nci - 1))
                    act(P, W4, W4.sq[:, vh, :], po, po.p[:], AF.Square)
                p3[0] += 1
                pn = psb[6]
                for vh in range(nvh):
                    mm(P, pn, pn.p[:], ON, ON.b[:], W4, W4.sq[:, vh, :], start=(vh == 0), stop=(vh == nvh - 1))
                act(P, W4, W4.r[:], pn, pn.p[:], AF.Ln, scale=1.0 / dv, bias=EPS)
                act(P, W4, W4.r[:], W4, W4.r[:], AF.Exp, scale=-0.5)
                O = w5.next()
                for vh in range(nvh):
                    gc = gcol0 + h * nvh + vh
                    stt(P, 'dve', O, O.b[:, vh, :], pos[vh], pos[vh].p[:], CO.c[:, gc:gc + 1], W4, W4.r[:],
                        ALU.mult, ALU.mult, extra=[CO])
                    tt(P, 'pool', O, O.b[:, vh, :], O, O.b[:, vh, :], L3, L3.g[:, vh, :], ALU.mult)
                P.dma('sp', [(mixT[row0 + h * dv:row0 + (h + 1) * dv, cs].rearrange("(a p) t -> p a t", p=128),
                              O.b[:])], reads=[O])

            for h in range(nh):
                iters = [(d, tb) for d in range(2) for tb in range(NBLK)]
                prev = None
                for (d, tb) in iters:
                    cur = stage_a(h, d, tb)
                    if prev is not None:
                        stage_b(prev)
                        if prev[0] == 0 and prev[1] == NBLK - 1:
                            pass2(0)
                    prev = cur
                stage_b(prev)
                pass2(1)
                nxt = pass3_load(h, 0)
                for tb in range(NBLK):
                    cur3 = nxt
                    if tb + 1 < NBLK:
                        nxt = pass3_load(h, tb + 1)
                    pass3(h, tb, cur3)


        stop = cfg.stop_after
        import os
        if os.environ.get('ONLY_SCAN'):
            phase_scan(nB, 128, 64, True, cc['hg'], nA * 64)
            P.end_phase()
            P.emit()
            return nc
        for l in range(cfg.n_layers):
            lastl = (l == cfg.n_layers - 1)
            HT = P.buf('HT', b=([128, 8, T], BF16))
            mark = P.sb_off
            phase_op1(l, HT)
            P.end_phase(keep=mark)
            if lastl and stop == 'op1':
                break
            phase_p2(l, HT)
            P.end_phase()
            if lastl and stop == 'p2':
                break
            i2 = l // 2
            if not os.environ.get('SKIP_MEM'):
                KV = phase_kv(l)
            if l % 2 == 0:
                if not os.environ.get('SKIP_MEM'):
                    phase_mem(KV, nA * 64 + nB * 128)
                    P.end_phase()
                if lastl and stop == 'mem':
                    break
                phase_scan(nB, 128, 64, True, cc['hg'] + i2 * nB, nA * 64)
                P.end_phase()
                if lastl and stop == 'scan0':
                    break
                if not os.environ.get('SKIP_ATTN'):
                    phase_attn(i2, 0)
                    P.end_phase()
            else:
                phase_mem(KV, nC * 256)
                P.end_phase()
                if lastl and stop == 'mem':
                    break
                phase_scan(nC, 256, 128, False, cc['gl'] + i2 * nC * 2, 0)
                P.end_phase()
        if stop is None:
            phase_op1(cfg.n_layers, None)
            P.end_phase()
        P.emit()
        print('ops recorded:', P.nops, {e: len(s) for e, s in P.streams.items()})
    return nc


def core_inputs(cfg, b, heads, inp):
    f32 = np.float32
    hA, hKV, hB, hM, hC = heads['A'], heads['KV'], heads['B'], heads['M'], heads['C']
    out = {}
    out['xT'] = np.ascontiguousarray(inp['x'][b].T)
    out['memT'] = np.ascontiguousarray(inp['mem'][b].T)

    def cols(base, hs, w):
        return np.concatenate([np.arange(base + h * w, base + (h + 1) * w) for h in hs])

    def padded(groups):
        sel = []
        for g in groups:
            sel.append(g)
            pad = (-len(g)) % 128
            if pad:
                sel.append(-np.ones(pad, dtype=np.int64))
        return np.concatenate(sel)

    def chunkify(w, sel):
        L, K = w.shape[0], w.shape[1]
        wp = np.zeros((L, K, len(sel)), f32)
        ok = sel >= 0
        wp[:, :, ok] = w[:, :, sel[ok]]
        nk, nch = K // 128, len(sel) // 128
        return np.ascontiguousarray(wp.reshape(L, nk, 128, nch, 128).transpose(0, 3, 2, 1, 4)).reshape(L, nch, 128, nk * 128)
    EV = np.cumsum([0, 512, 128, 128, 512, 512, 512, 512, 512, 512, 512, 512])
    names = ['qA', 'kA', 'vA', 'gA', 'qB', 'zf', 'zb', 'iB', 'gB', 'qM', 'gM']
    eo = dict(zip(names, EV[:-1]))
    sel_e = padded([
        cols(eo['qA'], hA, 64), cols(eo['kA'], hKV, 64), cols(eo['gA'], hA, 64),
        cols(eo['qB'], hB, 128), cols(eo['gB'], hB, 128), cols(eo['qM'], hM, 128), cols(eo['gM'], hM, 128),
        cols(eo['zf'], hB, 128), cols(eo['zb'], hB, 128), cols(eo['vA'], hKV, 64), cols(eo['iB'], hB, 128)])
    assert len(sel_e) == cfg.nce, (len(sel_e), cfg.nce)
    out['w_in_e'] = chunkify(inp['w_in_even'], sel_e)
    OD = np.cumsum([0, 512, 512, 1024, 1024, 16, 16, 512, 512])
    oo = dict(zip(['qC', 'kC', 'vC', 'gC', 'rf', 'rb', 'qM', 'gM'], OD[:-1]))
    sel_o = padded([
        cols(oo['qC'], hC, 128), cols(oo['kC'], hC, 128), cols(oo['gC'], hC, 256),
        cols(oo['qM'], hM, 128), cols(oo['gM'], hM, 128),
        np.arange(oo['rf'], oo['rf'] + 16), np.arange(oo['rb'], oo['rb'] + 16), cols(oo['vC'], hC, 256)])
    assert len(sel_o) == cfg.nco, (len(sel_o), cfg.nco)
    out['w_in_o'] = chunkify(inp['w_in_odd'], sel_o)
    rows_e = np.concatenate([cols(0, hA, 64), cols(512, hB, 128), cols(1024, hM, 128)])
    rows_o = np.concatenate([cols(0, hC, 256), cols(1024, hM, 128)])
    wo = np.zeros((4, cfg.mixr, D), f32)
    for l in range(4):
        if l % 2 == 0:
            wo[l] = inp['w_out_even'][l // 2][rows_e]
        else:
            wo[l] = inp['w_out_odd'][l // 2][rows_o]
    out['w_out'] = chunkify(wo, np.arange(D))
    kvc = np.concatenate([cols(0, hM, 128), cols(512, hM, 128)])
    out['w_kv'] = chunkify(inp['w_mem_kv'], kvc)
    wu = inp['w_gate_up'][:, :, :, cols(0, hC, 128)]
    out['w_up'] = np.ascontiguousarray(wu.transpose(2, 0, 1, 3).reshape(16, -1))
    cc = cfg.cc
    C = np.zeros((128, cfg.ncc), f32)
    for l in range(4):
        g = inp['norm_even'][l // 2] if l % 2 == 0 else inp['norm_odd'][l // 2]
        C[:, cc['norm'] + l * 8:cc['norm'] + (l + 1) * 8] = g.reshape(8, 128).T
    C[:, cc['fin']:cc['fin'] + 8] = inp['final_norm'].reshape(8, 128).T
    C[:, cc['mem']:cc['mem'] + 8] = inp['mem_norm'].reshape(8, 128).T
    nB, nC, nA = cfg.nB, cfg.nC, cfg.nA
    for i in range(2):
        for hi, h in enumerate(hB):
            C[:, cc['hg'] + i * nB + hi] = inp['hgrn_norm'][i, h * 128:(h + 1) * 128]
        for hi, h in enumerate(hC):
            for vh in range(2):
                C[:, cc['gl'] + i * nC * 2 + hi * 2 + vh] = inp['gla_norm'][i, h * 256 + vh * 128:h * 256 + (vh + 1) * 128]
        for d in range(2):
            for hi, h in enumerate(hB):
                C[:, cc['lbp'] + i * 2 * nB + d * nB + hi] = inp['lb_param'][i, d, h * 128:(h + 1) * 128]
            for hi, h in enumerate(hC):
                C[:, cc['bg'] + i * 2 * nC + d * nC + hi] = inp['b_gate'][i, d, h * 128:(h + 1) * 128]
        for hi, h in enumerate(hA):
            C[:, cc['sink'] + i * nA + hi] = inp['sink'][i, h]
    out['consts'] = C
    out['etab'] = np.ascontiguousarray(alibi_table(hA).reshape(128, -1))
    out['masks'] = scan_masks()
    return out


_CACHE = {}


def kernel(**inputs):
    inp = {k: np.asarray(v) for k, v in inputs.items()}
    cfg = Cfg(8, 4, 4, 4)
    heads = {'A': list(range(8)), 'KV': [0, 1], 'B': list(range(4)), 'M': list(range(4)), 'C': list(range(4))}
    if 'nc' not in _CACHE:
        _CACHE['nc'] = build(cfg)
    nc = _CACHE['nc']
    in_maps = [core_inputs(cfg, c % 4, heads, inp) for c in range(N_CORES)]
    res = run_bass_kernel_spmd(nc, in_maps, core_ids=list(range(N_CORES)))
    out = np.stack([np.ascontiguousarray(res.results[b]['yT'].T) for b in range(4)], axis=0)
    return out.astype(np.float32)
```

```python
import contextlib
import numpy as np
import concourse.bass as bass
import concourse.mybir as mybir
from concourse.bass_utils import run_bass_kernel_spmd

F32 = mybir.dt.float32
BF16 = mybir.dt.bfloat16
ALU = mybir.AluOpType
AF = mybir.ActivationFunctionType
DT_SIZE = {F32: 4, BF16: 2}

T = 4096
D = 1024
NBLK = T // 512
NTILE = T // 128
EPS = 1e-6
N_CORES = 8


class Buf:
    def __init__(self, name):
        self.name = name
        self.lw = None
        self.rd = {}
        self.dsem = None
        self.tt = {}

    def __getattr__(self, k):
        t = self.__dict__.get('tt', {})
        if k in t:
            return t[k]
        raise AttributeError(k)


class Prog:
    COMPUTE = ('pe', 'act', 'dve', 'pool')
    ENG = ('pe', 'act', 'dve', 'pool', 'sp')

    def __init__(self, nc, es, n_dsem=88, sb_limit=229000, sb_start=16640):
        self.nc = nc
        self.semh = {}
        for e in self.COMPUTE:
            self.semh[e] = es.enter_context(nc.semaphore('c_' + e))
        self.free_dsem = []
        for i in range(n_dsem):
            k = ('d', i)
            self.semh[k] = es.enter_context(nc.semaphore('d%d' % i))
            self.free_dsem.append(k)
        self.cnt = {k: 0 for k in self.semh}
        self.streams = {e: [] for e in self.ENG}
        self.waited = {e: {} for e in self.ENG}
        self.sb_off = sb_start
        self.sb_base = sb_start
        self.sb_limit = sb_limit
        self.uid = 0
        self.phase_dsems = []
        self.nops = 0

    def sbuf(self, shape, dtype, name='t'):
        self.uid += 1
        per_part = int(np.prod(shape[1:])) * DT_SIZE[dtype]
        off = (self.sb_off + 63) // 64 * 64
        assert off + per_part <= self.sb_limit, ('SBUF overflow', name, off, per_part)
        h = self.nc.alloc_sbuf_tensor_at('%s_%d' % (name, self.uid), list(shape), dtype, offset=off)
        self.sb_off = off + per_part
        return h

    def buf(self, name, **tensors):
        b = Buf(name)
        for k, (shape, dtype) in tensors.items():
            b.tt[k] = self.sbuf(shape, dtype, name + '_' + k)
        return b

    def ring(self, n, name, **tensors):
        return Ring([self.buf('%s%d' % (name, i), **tensors) for i in range(n)])

    def wrap(self, name, **handles):
        b = Buf(name)
        b.tt.update(handles)
        return b

    def need_dsem(self, b):
        if b.dsem is None:
            b.dsem = self.free_dsem.pop()
            self.phase_dsems.append(b.dsem)
        return b.dsem

    def persist(self):
        self.sb_base = self.sb_off
        self.phase_dsems = []

    def _deps(self, eng, reads, writes, is_dma):
        deps = {}

        def need(tok):
            if tok is None:
                return
            k, v = tok
            if deps.get(k, 0) < v:
                deps[k] = v
        for b in reads:
            need(b.lw)
        for b in writes:
            if b.lw is not None and (is_dma or b.lw[0] != eng):
                need(b.lw)
            for k, v in b.rd.items():
                if is_dma or k != eng:
                    need((k, v))
        w = self.waited[eng]
        out = []
        for k, v in deps.items():
            if w.get(k, 0) < v:
                w[k] = v
                out.append((k, v))
        return out

    def _commit(self, tok, reads, writes):
        for b in writes:
            b.lw = tok
            b.rd = {}
        k, v = tok
        for b in reads:
            if b in writes:
                continue
            if b.rd.get(k, 0) < v:
                b.rd[k] = v

    def op(self, eng, fn, reads=(), writes=()):
        waits = self._deps(eng, reads, writes, False)
        self.cnt[eng] += 1
        tok = (eng, self.cnt[eng])
        self.streams[eng].append((waits, fn, eng, 1))
        self._commit(tok, reads, writes)
        self.nops += 1

    def dma(self, queue, pairs, reads=(), writes=()):
        onchip = list(writes) + list(reads)
        key = self.need_dsem(onchip[0])
        for b in onchip[1:]:
            assert b.dsem is None or b.dsem == key
            b.dsem = key
        waits = self._deps(queue, reads, writes, True)
        first = True
        for (o, i) in pairs:
            self.cnt[key] += 16
            self.streams[queue].append((waits if first else [],
                                        (lambda e, o=o, i=i: e.dma_start(out=o, in_=i)), key, 16))
            first = False
            self.nops += 1
        self._commit((key, self.cnt[key]), reads, writes)

    def barrier(self):
        for e in self.ENG:
            waits = []
            w = self.waited[e]
            for k, v in self.cnt.items():
                if v > 0 and w.get(k, 0) < v and k != e:
                    w[k] = v
                    waits.append((k, v))
            if waits:
                self.streams[e].append((waits, None, None, 0))

    def end_phase(self, keep=None):
        self.barrier()
        self.free_dsem.extend(self.phase_dsems)
        self.phase_dsems = []
        self.sb_off = self.sb_base if keep is None else keep

    def emit(self):
        nc = self.nc
        semh = self.semh
        streams = self.streams
        with nc.Block() as block:
            def body(name):
                def f(e):
                    for (waits, fn, key, inc) in streams[name]:
                        for (k, v) in waits:
                            e.wait_ge(semh[k], v)
                        if fn is not None:
                            fn(e).then_inc(semh[key], inc)
                return f
            block.tensor(body('pe'))
            block.scalar(body('act'))
            block.vector(body('dve'))
            block.gpsimd(body('pool'))
            block.sync(body('sp'))


class Ring:
    def __init__(self, bufs):
        self.bufs = bufs
        self.i = 0

    def next(self):
        b = self.bufs[self.i % len(self.bufs)]
        self.i += 1
        return b


def mm(P, ob, o, lb, l, rb, r, start=True, stop=True):
    P.op('pe', lambda e: e.matmul(o, lhsT=l, rhs=r, start=start, stop=stop), reads=[lb, rb], writes=[ob])


def act(P, ob, o, ib, i, func, scale=1.0, bias=0.0, extra=()):
    P.op('act', lambda e: e.activation(out=o, in_=i, func=func, scale=scale, bias=bias),
         reads=[ib] + list(extra), writes=[ob])


def tt(P, eng, ob, o, ab, a, bb, b, op):
    P.op(eng, lambda e: e.tensor_tensor(out=o, in0=a, in1=b, op=op), reads=[ab, bb], writes=[ob])


def ts(P, eng, ob, o, ab, a, s1, s2, op0, op1=None, extra=()):
    if op1 is None:
        P.op(eng, lambda e: e.tensor_scalar(out=o, in0=a, scalar1=s1, scalar2=None, op0=op0),
             reads=[ab] + list(extra), writes=[ob])
    else:
        P.op(eng, lambda e: e.tensor_scalar(out=o, in0=a, scalar1=s1, scalar2=s2, op0=op0, op1=op1),
             reads=[ab] + list(extra), writes=[ob])


def stt(P, eng, ob, o, ab, a, scalar, bb, b, op0, op1, extra=()):
    P.op(eng, lambda e: e.scalar_tensor_tensor(out=o, in0=a, scalar=scalar, in1=b, op0=op0, op1=op1),
         reads=[ab, bb] + list(extra), writes=[ob])


def cp(P, eng, ob, o, ib, i):
    P.op(eng, lambda e: e.tensor_copy(out=o, in_=i), reads=[ib], writes=[ob])


def mset(P, eng, ob, o, val):
    P.op(eng, lambda e: e.memset(o, val), writes=[ob])


class Cfg:
    def __init__(self, nA, nB, nM, nC, n_layers=4, final_norm=True, debug=False, stop_after=None):
        self.nA, self.nB, self.nM, self.nC = nA, nB, nM, nC
        self.debug = debug
        self.stop_after = stop_after
        self.nKV = nA // 4
        self.n_layers = n_layers
        self.final_norm = final_norm
        g = []
        o = 0

        def add(name, n):
            nonlocal o
            g.append((name, o, n))
            o += (n + 127) // 128 * 128
        add('qA', nA * 64); add('kA', self.nKV * 64); add('gA', nA * 64)
        add('qB', nB * 128); add('gB', nB * 128); add('qM', nM * 128); add('gM', nM * 128)
        add('zf', nB * 128); add('zb', nB * 128)
        add('vA', self.nKV * 64); add('iB', nB * 128)
        self.ge = {n: (s, c) for n, s, c in g}
        self.nce = o
        g = []
        o = 0
        add('qC', nC * 128); add('kC', nC * 128); add('gC', nC * 256); add('qM', nM * 128); add('gM', nM * 128)
        add('rf', 16); add('rb', 16); add('vC', nC * 256)
        self.go = {n: (s, c) for n, s, c in g}
        self.nco = o
        self.mix_e = nA * 64 + nB * 128 + nM * 128
        self.mix_o = nC * 256 + nM * 128
        self.mixr = max(self.mix_e, self.mix_o)
        assert self.mix_e == self.mix_o
        c = {}
        o = 0
        for nm, n in (('norm', 32), ('fin', 8), ('mem', 8), ('hg', 2 * nB), ('gl', 2 * nC * 2),
                      ('lbp', 4 * nB), ('bg', 4 * nC), ('sink', 2 * nA)):
            c[nm] = o
            o += n
        self.cc = c
        self.ncc = o


def alibi_table(heads):
    s = np.arange(128)[:, None, None, None]
    j = np.arange(3)[None, :, None, None]
    t = np.arange(128)[None, None, None, :]
    dist = np.abs(t - s - (j - 1) * 128).astype(np.float64)
    slopes = np.array([2.0 ** (-8.0 * (h + 1) / 8) for h in heads])[None, None, :, None]
    e = np.exp(-slopes * dist) * (dist <= 128)
    return e.astype(np.float32)


def scan_masks():
    s = np.arange(128)[:, None]
    t = np.arange(128)[None, :]
    m = []
    for C in (64, 128):
        same = (s // C) == (t // C)
        m.append(((s <= t) & same).astype(np.float32))
        m.append(((s >= t) & same).astype(np.float32))
    m.append(np.eye(128, dtype=np.float32))
    p = np.arange(128)[:, None]
    m.append((p < 64).astype(np.float32))
    m.append((p >= 64).astype(np.float32))
    return np.concatenate(m, axis=1)


def build(cfg):
    nc = bass.Bass("TRN2", target_bir_lowering=False)
    nA, nB, nM, nC, nKV = cfg.nA, cfg.nB, cfg.nM, cfg.nC, cfg.nKV
    MIXR = cfg.mixr
    NKM = MIXR // 128

    def din(name, shape, dt=F32):
        return nc.dram_tensor(name, list(shape), dt, kind="ExternalInput").ap()

    def dscr(name, shape, dt):
        dbg = cfg.debug and name in ('mixT',)
        return nc.dram_tensor(name, list(shape), dt, kind="ExternalOutput" if dbg else "Internal").ap()

    xT_in = din('xT', [D, T])
    memT_in = din('memT', [D, 256])
    consts_in = din('consts', [128, cfg.ncc])
    etab_in = din('etab', [128, 3 * nA * 128])
    masks_in = din('masks', [128, 642])
    w_in_e = din('w_in_e', [2, cfg.nce // 128, 128, 1024])
    w_in_o = din('w_in_o', [2, cfg.nco // 128, 128, 1024])
    w_out_in = din('w_out', [4, 8, 128, NKM * 128])
    w_kv_in = din('w_kv', [4, 2 * nM, 128, 1024])
    w_up_in = din('w_up', [16, 2 * 2 * nC * 128])
    yT = nc.dram_tensor('yT', [D, T], F32, kind="ExternalOutput").ap()

    xs = [dscr('xs0', [D, T], F32), dscr('xs1', [D, T], F32)]
    mixT = dscr('mixT', [MIXR, T], BF16)
    s_qA = dscr('s_qA', [nA * 64, T], BF16)
    s_kA = dscr('s_kA', [nKV * 64, T], BF16)
    s_gA = dscr('s_gA', [nA * 64, T], BF16)
    NSH = max(nB, nC)
    s_q = dscr('s_q', [NSH * 128, T], BF16)
    s_k = dscr('s_k', [2, NSH * 128, T], BF16)
    s_lf = dscr('s_lf', [2, NSH * 128, T], F32)
    s_g = dscr('s_g', [max(nB * 128, nC * 256), T], BF16)
    s_qM = dscr('s_qM', [nM * 128, T], BF16)
    s_gM = dscr('s_gM', [nM * 128, T], BF16)
    s_vA = dscr('s_vA', [T, nKV * 64], BF16)
    s_v = dscr('s_v', [T, max(nB * 128, nC * 256)], BF16)

    with contextlib.ExitStack() as es:
        P = Prog(nc, es)
        psb = [P.wrap('ps%d' % i, p=nc.alloc_psum_tensor('ps%d' % i, [128, 512], F32)) for i in range(7)]
        pst = P.wrap('pst', p=nc.alloc_psum_tensor('pst', [128, 1024], BF16))

        CO = P.buf('CO', c=([128, cfg.ncc], F32), lb=([128, 4 * nB], F32), oml=([128, 4 * nB], F32),
                   noml=([128, 4 * nB], F32), negb=([128, 4 * nC], F32), esink=([128, 2 * nA], F32),
                   tmp=([128, 4 * nB], F32))
        MK = P.buf('MK', f=([128, 642], F32), b=([128, 642], BF16))
        ET = P.buf('ET', b=([128, 3 * nA * 128], BF16))
        ON = P.buf('ON', b=([128, 128], BF16), rst64=([128, 512], F32), rst128=([128, 512], F32))
        MEMN = P.buf('MEMN', b=([128, 8, 256], BF16))
        WUP = P.buf('WUP', b=([16, 2 * 2 * nC * 128], BF16))
        P.persist()

        cc = cfg.cc
        P.dma('sp', [(CO.c[:], consts_in)], writes=[CO])
        P.dma('sp', [(MK.f[:], masks_in)], writes=[MK])
        cp(P, 'pool', MK, MK.b[:], MK, MK.f[:])
        mset(P, 'pool', ON, ON.b[:], 1.0)
        mset(P, 'pool', ON, ON.rst64[:], 1.0)
        mset(P, 'pool', ON, ON.rst64[:].rearrange("p (c t) -> p c t", t=64)[:, :, 0:1], 0.0)
        mset(P, 'pool', ON, ON.rst128[:], 1.0)
        mset(P, 'pool', ON, ON.rst128[:].rearrange("p (c t) -> p c t", t=128)[:, :, 0:1], 0.0)
        nlb = 2 * nB
        lbp = cc['lbp']
        mset(P, 'dve', CO, CO.lb[:, 0:nlb], 0.0)
        tt(P, 'dve', CO, CO.tmp[:, 0:nlb], CO, CO.c[:, lbp:lbp + nlb], CO, CO.c[:, lbp + nlb:lbp + 2 * nlb], ALU.subtract)
        act(P, CO, CO.tmp[:, 0:nlb], CO, CO.tmp[:, 0:nlb], AF.Exp)
        ts(P, 'dve', CO, CO.tmp[:, 0:nlb], CO, CO.tmp[:, 0:nlb], 1.0, None, ALU.add)
        P.op('dve', lambda e: e.reciprocal(out=CO.lb[:, nlb:2 * nlb], in_=CO.tmp[:, 0:nlb]), reads=[CO], writes=[CO])
        ts(P, 'dve', CO, CO.oml[:], CO, CO.lb[:], -1.0, 1.0, ALU.mult, ALU.add)
        ts(P, 'dve', CO, CO.noml[:], CO, CO.oml[:], -1.0, None, ALU.mult)
        ts(P, 'dve', CO, CO.negb[:], CO, CO.c[:, cc['bg']:cc['bg'] + 4 * nC], -1.0, None, ALU.mult)
        act(P, CO, CO.esink[:], CO, CO.c[:, cc['sink']:cc['sink'] + 2 * nA], AF.Exp)

        with_stage = P.buf('PST', f=([128, 3 * nA * 128], F32))
        P.dma('sp', [(with_stage.f[:], etab_in)], writes=[with_stage])
        cp(P, 'pool', ET, ET.b[:], with_stage, with_stage.f[:])
        wu = P.buf('WUS', f=([16, 2 * 2 * nC * 128], F32))
        P.dma('sp', [(wu.f[:], w_up_in)], writes=[wu])
        cp(P, 'pool', WUP, WUP.b[:], wu, wu.f[:])
        mm_ = P.buf('MEMS', f=([128, 8, 256], F32), sq=([128, 8, 256], BF16), r=([128, 256], F32))
        P.dma('sp', [(mm_.f[:], memT_in.rearrange("(k p) t -> p k t", p=128))], writes=[mm_])
        act(P, mm_, mm_.sq[:], mm_, mm_.f[:], AF.Square)
        for k in range(8):
            mm(P, psb[0], psb[0].p[:, 0:256], ON, ON.b[:], mm_, mm_.sq[:, k, :], start=(k == 0), stop=(k == 7))
        act(P, mm_, mm_.r[:], psb[0], psb[0].p[:, 0:256], AF.Ln, scale=1.0 / D, bias=EPS)
        act(P, mm_, mm_.r[:], mm_, mm_.r[:], AF.Exp, scale=-0.5)
        for k in range(8):
            stt(P, 'dve', MEMN, MEMN.b[:, k, :], mm_, mm_.f[:, k, :], CO.c[:, cc['mem'] + k:cc['mem'] + k + 1],
                mm_, mm_.r[:], ALU.mult, ALU.mult, extra=[CO])
        P.end_phase()

        def phase_op1(l, HT):
            last = (l == cfg.n_layers)
            src = xT_in if l <= 1 else xs[(l - 1) % 2]
            dst = xs[l % 2]
            if l > 0:
                WO = P.buf('WO', b=([128, NKM, D], BF16))
                wst = P.ring(2, 'wost', f=([128, NKM, 128], F32))
                for c8 in range(8):
                    st = wst.next()
                    P.dma('sp', [(st.f[:].rearrange("p k c -> p (k c)"), w_out_in[l - 1, c8])], writes=[st])
                    cp(P, 'pool', WO, WO.b[:, :, c8 * 128:(c8 + 1) * 128], st, st.f[:])
                mixr = P.ring(2, 'mixb', b=([128, NKM, 512], BF16))
            xr = P.ring(2, 'xb', f=([128, 8, 512], F32))
            sqr = P.ring(2, 'sqb', b=([128, 8, 512], BF16))
            rr = P.ring(2, 'rstd', f=([128, 512], F32))
            yr = P.ring(2, 'yb', f=([128, 8, 512], F32)) if last else None
            psi = 0
            for tb in range(NBLK):
                cs = slice(tb * 512, (tb + 1) * 512)
                X = xr.next()
                P.dma('sp', [(X.f[:], src.rearrange("(k p) t -> p k t", p=128)[:, :, cs])], writes=[X])
                if l > 0:
                    MX = mixr.next()
                    P.dma('sp', [(MX.b[:], mixT.rearrange("(k p) t -> p k t", p=128)[:, :, cs])], writes=[MX])
                    for c8 in range(8):
                        ps = psb[psi % 4]
                        psi += 1
                        for k in range(NKM):
                            mm(P, ps, ps.p[:], WO, WO.b[:, k, c8 * 128:(c8 + 1) * 128], MX, MX.b[:, k, :],
                               start=(k == 0), stop=(k == NKM - 1))
                        tt(P, 'dve', X, X.f[:, c8, :], ps, ps.p[:], X, X.f[:, c8, :], ALU.add)
                    if not last:
                        P.dma('pool', [(dst.rearrange("(k p) t -> p k t", p=128)[:, :, cs], X.f[:])], reads=[X])
                if last and not cfg.final_norm:
                    P.dma('pool', [(yT.rearrange("(k p) t -> p k t", p=128)[:, :, cs], X.f[:])], reads=[X])
                    continue
                SQ = sqr.next()
                act(P, SQ, SQ.b[:], X, X.f[:], AF.Square)
                pn = psb[4 + tb % 2]
                for k in range(8):
                    mm(P, pn, pn.p[:], ON, ON.b[:], SQ, SQ.b[:, k, :], start=(k == 0), stop=(k == 7))
                R = rr.next()
                act(P, R, R.f[:], pn, pn.p[:], AF.Ln, scale=1.0 / D, bias=EPS)
                act(P, R, R.f[:], R, R.f[:], AF.Exp, scale=-0.5)
                if not last:
                    gcol = cc['norm'] + l * 8
                    for k in range(8):
                        stt(P, 'dve', HT, HT.b[:, k, cs], X, X.f[:, k, :],
                            CO.c[:, gcol + k:gcol + k + 1], R, R.f[:], ALU.mult, ALU.mult, extra=[CO])
                else:
                    Y = yr.next()
                    gcol = cc['fin']
                    for k in range(8):
                        stt(P, 'dve', Y, Y.f[:, k, :], X, X.f[:, k, :],
                            CO.c[:, gcol + k:gcol + k + 1], R, R.f[:], ALU.mult, ALU.mult, extra=[CO])
                    P.dma('pool', [(yT.rearrange("(k p) t -> p k t", p=128)[:, :, cs], Y.f[:])], reads=[Y])

        def phase_p2(l, HT):
            even = (l % 2 == 0)
            i2 = l // 2
            wsrc = (w_in_e if even else w_in_o)[i2]
            G = cfg.ge if even else cfg.go
            wst = P.ring(3, 'wst', f=([128, 8, 128], F32))
            wbr = P.ring(3, 'wb', b=([128, 8, 128], BF16))
            ob = P.ring(4, 'ob', b=([128, 512], BF16))
            of = P.ring(4, 'of', f=([128, 512], F32))
            of2 = P.ring(4, 'of2', f=([128, 512], F32))
            psr = Ring(psb[0:4])
            psu = Ring(psb[4:7])
            rfr = P.ring(2, 'rfb', b=([16, 512], BF16))

            def load_w(c0, ncol):
                st = wst.next()
                assert c0 % 128 == 0
                P.dma('sp', [(st.f[:].rearrange("p k c -> p (k c)"), wsrc[c0 // 128])], writes=[st])
                wb = wbr.next()
                cp(P, 'pool', wb, wb.b[:, :, 0:ncol], st, st.f[:, :, 0:ncol])
                return wb

            def proj(wb, ncol, tb):
                ps = psr.next()
                for k in range(8):
                    mm(P, ps, ps.p[0:ncol, :], wb, wb.b[:, k, 0:ncol], HT, HT.b[:, k, tb * 512:(tb + 1) * 512],
                       start=(k == 0), stop=(k == 7))
                return ps

            def fgroup(name, kind, dst, scale=1.0, **kw):
                c0, n = G[name]
                r = 0
                while r < n:
                    ncol = min(128, n - r)
                    wb = load_w(c0 + r, ncol)
                    for tb in range(NBLK):
                        cs = slice(tb * 512, (tb + 1) * 512)
                        if kind == 'hz' and tb % 2 == 1:
                            continue
                        ps = proj(wb, ncol, tb)
                        if kind in ('copy', 'silu'):
                            O = ob.next()
                            act(P, O, O.b[0:ncol, :], ps, ps.p[0:ncol, :], AF.Copy if kind == 'copy' else AF.Silu,
                                scale=scale)
                            P.dma('act', [(dst[r:r + ncol, cs], O.b[0:ncol, :])], reads=[O])
                        elif kind == 'hz':
                            d = kw['d']
                            h = r // 128
                            col = i2 * 2 * nB + d * nB + h
                            tbs = [tb, tb + 1]
                            pss2 = [ps, proj(wb, ncol, tb + 1)]
                            Es = [of.next(), of.next()]
                            for E, p_ in zip(Es, pss2):
                                act(P, E, E.f[:], p_, p_.p[:], AF.Exp, scale=-1.0)
                            for E in Es:
                                act(P, E, E.f[:], E, E.f[:], AF.Ln, scale=1.0, bias=1.0)
                            for E in Es:
                                act(P, E, E.f[:], E, E.f[:], AF.Exp, scale=-1.0)
                            Ls2 = [of2.next(), of2.next()]
                            for E, L in zip(Es, Ls2):
                                act(P, L, L.f[:], E, E.f[:], AF.Ln, scale=CO.oml[:, col:col + 1], bias=CO.lb[:, col:col + 1],
                                    extra=[CO])
                            for L, tbx in zip(Ls2, tbs):
                                ts(P, 'dve', L, L.f[:], L, L.f[:], float(np.log(1e-30)), None, ALU.max)
                                P.dma('sp', [(s_lf[d, r:r + 128, tbx * 512:(tbx + 1) * 512], L.f[:])], reads=[L])
                            for E, tbx in zip(Es, tbs):
                                O = ob.next()
                                ts(P, 'dve', O, O.b[:], E, E.f[:], CO.noml[:, col:col + 1], CO.oml[:, col:col + 1],
                                   ALU.mult, ALU.add, extra=[CO])
                                P.dma('sp', [(s_k[d, r:r + 128, tbx * 512:(tbx + 1) * 512], O.b[:])], reads=[O])
                        elif kind == 'gr':
                            d = kw['d']
                            RF = rfr.next()
                            act(P, RF, RF.b[:], ps, ps.p[0:16, :], AF.Copy)
                            for hc0 in range(0, nC, 2):
                                hcs = [hc0, hc0 + 1] if hc0 + 1 < nC else [hc0]
                                pus, Es = [], []
                                for hc in hcs:
                                    wcol = ((i2 * 2 + d) * nC + hc) * 128
                                    pu = psu.next()
                                    mm(P, pu, pu.p[:], WUP, WUP.b[:, wcol:wcol + 128], RF, RF.b[:])
                                    pus.append(pu)
                                for hc, pu in zip(hcs, pus):
                                    col = i2 * 2 * nC + d * nC + hc
                                    E = of.next()
                                    act(P, E, E.f[:], pu, pu.p[:], AF.Exp, scale=-1.0, bias=CO.negb[:, col:col + 1], extra=[CO])
                                    Es.append(E)
                                for E in Es:
                                    act(P, E, E.f[:], E, E.f[:], AF.Ln, scale=1.0, bias=1.0)
                                for hc, E in zip(hcs, Es):
                                    L = of2.next()
                                    ts(P, 'dve', L, L.f[:], E, E.f[:], -1.0 / 16.0, None, ALU.mult)
                                    P.dma('sp', [(s_lf[d, hc * 128:(hc + 1) * 128, cs], L.f[:])], reads=[L])
                    r += ncol

            def tgroup(name, dst):
                c0, n = G[name]
                WT = P.buf('WT', b=([128, 8, n], BF16))
                r = 0
                while r < n:
                    st = wst.next()
                    P.dma('sp', [(st.f[:].rearrange("p k c -> p (k c)"), wsrc[(c0 + r) // 128])], writes=[st])
                    nn = min(128, n - r)
                    cp(P, 'pool', WT, WT.b[:, :, r:r + nn], st, st.f[:, :, 0:nn])
                    r += 128
                for tti in range(NTILE):
                    r = 0
                    while r < n:
                        ncol = min(512, n - r)
                        ps = psr.next()
                        for k in range(8):
                            mm(P, ps, ps.p[:, 0:ncol], HT, HT.b[:, k, tti * 128:(tti + 1) * 128], WT, WT.b[:, k, r:r + ncol],
                               start=(k == 0), stop=(k == 7))
                        O = ob.next()
                        act(P, O, O.b[:, 0:ncol], ps, ps.p[:, 0:ncol], AF.Copy)
                        P.dma('act', [(dst[tti * 128:(tti + 1) * 128, r:r + ncol], O.b[:, 0:ncol])], reads=[O])
                        r += ncol

            if even:
                fgroup('qA', 'copy', s_qA, scale=0.125)
                fgroup('kA', 'copy', s_kA)
                fgroup('qM', 'copy', s_qM)
                fgroup('gA', 'silu', s_gA)
                fgroup('qB', 'silu', s_q)
                fgroup('gB', 'silu', s_g)
                fgroup('gM', 'silu', s_gM)
                fgroup('zf', 'hz', None, d=0)
                fgroup('zb', 'hz', None, d=1)
                tgroup('vA', s_vA)
                tgroup('iB', s_v)
            else:
                fgroup('qC', 'copy', s_q, scale=float(128 ** -0.5))
                fgroup('kC', 'copy', s_k[0])
                fgroup('qM', 'copy', s_qM)
                fgroup('gC', 'silu', s_g)
                fgroup('gM', 'silu', s_gM)
                fgroup('rf', 'gr', None, d=0)
                fgroup('rb', 'gr', None, d=1)
                tgroup('vC', s_v)

        def phase_kv(l):
            KV = P.buf('KV', k=([128, nM, 256], BF16), v=([128, 2, nM * 128], BF16))
            WK = P.buf('WK', b=([128, 8, 2 * nM * 128], BF16))
            wst = P.ring(2, 'kvst', f=([128, 8, 128], F32))
            for c in range(2 * nM):
                st = wst.next()
                P.dma('sp', [(st.f[:].rearrange("p k c -> p (k c)"), w_kv_in[l, c])], writes=[st])
                cp(P, 'pool', WK, WK.b[:, :, c * 128:(c + 1) * 128], st, st.f[:])
            for h in range(nM):
                ps = psb[h % 2]
                for k in range(8):
                    mm(P, ps, ps.p[:, 0:256], WK, WK.b[:, k, h * 128:(h + 1) * 128], MEMN, MEMN.b[:, k, :],
                       start=(k == 0), stop=(k == 7))
                act(P, KV, KV.k[:, h, :], ps, ps.p[:, 0:256], AF.Copy)
            for sc in range(2):
                ps = psb[2 + sc]
                for k in range(8):
                    mm(P, ps, ps.p[:, 0:nM * 128], MEMN, MEMN.b[:, k, sc * 128:(sc + 1) * 128],
                       WK, WK.b[:, k, nM * 128:2 * nM * 128], start=(k == 0), stop=(k == 7))
                act(P, KV, KV.v[:, sc, :], ps, ps.p[:, 0:nM * 128], AF.Copy)
            return KV

        def phase_mem(KV, row0):
            qr = P.ring(3, 'mq', q=([128, 512], BF16), g=([128, 512], BF16))
            pr = P.ring(3, 'mp', b=([128, 2, 512], BF16))
            rr = P.ring(2, 'mr', f=([128, 512], F32), o=([128, 512], F32))
            orr = P.ring(3, 'mo', b=([128, 512], BF16))
            scale = float(128 ** -0.5)

            def st_a(tb, h, it):
                cs = slice(tb * 512, (tb + 1) * 512)
                Q = qr.next()
                P.dma('sp', [(Q.q[:], s_qM[h * 128:(h + 1) * 128, cs]), (Q.g[:], s_gM[h * 128:(h + 1) * 128, cs])],
                      writes=[Q])
                PB = pr.next()
                for sc in range(2):
                    ps = psb[(it * 2 + sc) % 4]
                    mm(P, ps, ps.p[:], KV, KV.k[:, h, sc * 128:(sc + 1) * 128], Q, Q.q[:])
                    act(P, PB, PB.b[:, sc, :], ps, ps.p[:], AF.Exp, scale=scale)
                return (tb, h, it, Q, PB)

            def st_b(ctx):
                tb, h, it, Q, PB = ctx
                cs = slice(tb * 512, (tb + 1) * 512)
                po = psb[4 + it % 2]
                pd = psb[6]
                for sc in range(2):
                    mm(P, po, po.p[:], KV, KV.v[:, sc, h * 128:(h + 1) * 128], PB, PB.b[:, sc, :],
                       start=(sc == 0), stop=(sc == 1))
                for sc in range(2):
                    mm(P, pd, pd.p[:], ON, ON.b[:], PB, PB.b[:, sc, :], start=(sc == 0), stop=(sc == 1))
                R = rr.next()
                act(P, R, R.f[:], pd, pd.p[:], AF.Ln)
                act(P, R, R.f[:], R, R.f[:], AF.Exp, scale=-1.0)
                tt(P, 'dve', R, R.o[:], po, po.p[:], R, R.f[:], ALU.mult)
                O = orr.next()
                tt(P, 'pool', O, O.b[:], R, R.o[:], Q, Q.g[:], ALU.mult)
                P.dma('pool', [(mixT[row0 + h * 128:row0 + (h + 1) * 128, cs], O.b[:])], reads=[O])

            prev = None
            it = 0
            for tb in range(NBLK):
                for h in range(nM):
                    cur = st_a(tb, h, it)
                    it += 1
                    if prev is not None:
                        st_b(prev)
                    prev = cur
            st_b(prev)

        def phase_attn(i2, row0):
            lr = P.ring(2, 'aq', q=([64, 4, 512], BF16), g=([64, 4, 512], BF16), k=([64, 768], BF16),
                        v=([128, 6, 64], BF16))
            pr = P.ring(9, 'ap', b=([128, 512], BF16))
            dr = P.ring(2, 'ad', f=([64, 512], F32), o=([64, 512], F32))
            orr = P.ring(2, 'ao', b=([64, 4, 512], BF16))
            pss = Ring(psb[0:4])
            cnt = [0]

            def load(tb, n):
                cs = slice(tb * 512, (tb + 1) * 512)
                L = lr.next()
                k0 = max(0, tb * 512 - 128)
                k1 = min(T, tb * 512 + 640)
                ko = k0 - (tb * 512 - 128)
                t0 = max(0, tb * 4 - 1)
                t1 = min(NTILE, tb * 4 + 5)
                to = t0 - (tb * 4 - 1)
                P.dma('sp', [
                    (L.q[:], s_qA[n * 256:(n + 1) * 256, cs].rearrange("(g d) t -> d g t", d=64)),
                    (L.g[:], s_gA[n * 256:(n + 1) * 256, cs].rearrange("(g d) t -> d g t", d=64)),
                    (L.k[:, ko:ko + (k1 - k0)], s_kA[n * 64:(n + 1) * 64, k0:k1]),
                    (L.v[:, to:to + (t1 - t0), :],
                     s_vA[t0 * 128:t1 * 128, n * 64:(n + 1) * 64].rearrange("(j p) c -> p j c", p=128)),
                ], writes=[L])
                return L

            def stage_a(L, tb, n, qt):
                c = tb * 4 + qt
                js = [j for j in range(3) if 0 <= c - 1 + j < NTILE]
                PBs = []
                for j in js:
                    ps = pss.next()
                    kc = (qt + j) * 128
                    mm(P, ps, ps.p[:].rearrange("p (g t) -> p g t", g=4), L, L.k[:, kc:kc + 128], L, L.q[:, :, qt * 128:(qt + 1) * 128])
                    PB = pr.next()
                    act(P, PB, PB.b[:], ps, ps.p[:], AF.Exp)
                    e0 = (j * nA + n * 4) * 128
                    tt(P, 'pool' if j == 1 else 'dve', PB, PB.b[:], PB, PB.b[:], ET, ET.b[:, e0:e0 + 512], ALU.mult)
                    PBs.append((j, PB))
                return PBs

            def stage_b(L, O, tb, n, qt, PBs):
                it = cnt[0]
                cnt[0] += 1
                po = psb[4 + it % 2]
                pd = psb[6]
                for ii, (j, PB) in enumerate(PBs):
                    mm(P, po, po.p[0:64, :], L, L.v[:, qt + j, :], PB, PB.b[:],
                       start=(ii == 0), stop=(ii == len(PBs) - 1))
                for ii, (j, PB) in enumerate(PBs):
                    mm(P, pd, pd.p[0:64, :], ON, ON.b[:, 0:64], PB, PB.b[:],
                       start=(ii == 0), stop=(ii == len(PBs) - 1))
                Dn = dr.next()
                sk = i2 * nA + n * 4
                tt(P, 'dve', Dn, Dn.f[:].rearrange("p (g t) -> p g t", g=4),
                   pd, pd.p[0:64, :].rearrange("p (g t) -> p g t", g=4),
                   CO, CO.esink[0:64, sk:sk + 4].unsqueeze(2).to_broadcast([64, 4, 128]), ALU.add)
                act(P, Dn, Dn.f[:], Dn, Dn.f[:], AF.Ln)
                act(P, Dn, Dn.f[:], Dn, Dn.f[:], AF.Exp, scale=-1.0)
                tt(P, 'dve', Dn, Dn.o[:], po, po.p[0:64, :], Dn, Dn.f[:], ALU.mult)
                tt(P, 'pool', O, O.b[:, :, qt * 128:(qt + 1) * 128],
                   Dn, Dn.o[:].rearrange("p (g t) -> p g t", g=4),
                   L, L.g[:, :, qt * 128:(qt + 1) * 128], ALU.mult)

            items = [(tb, n, qt) for tb in range(NBLK) for n in range(nKV) for qt in range(4)]
            Ls = {}
            Os = {}
            pend = None
            for (tb, n, qt) in items:
                if qt == 0:
                    Ls[(tb, n)] = load(tb, n)
                L = Ls[(tb, n)]
                PBs = stage_a(L, tb, n, qt)
                if pend is not None:
                    ptb, pn_, pqt, pL, pPBs = pend
                    if pqt == 0:
                        Os[(ptb, pn_)] = orr.next()
                    stage_b(pL, Os[(ptb, pn_)], ptb, pn_, pqt, pPBs)
                    if pqt == 3:
                        cs = slice(ptb * 512, (ptb + 1) * 512)
                        P.dma('pool', [(mixT[row0 + pn_ * 256:row0 + (pn_ + 1) * 256, cs].rearrange("(g d) t -> d g t", d=64),
                                        Os[(ptb, pn_)].b[:])], reads=[Os[(ptb, pn_)]])
                pend = (tb, n, qt, L, PBs)
            ptb, pn_, pqt, pL, pPBs = pend
            stage_b(pL, Os[(ptb, pn_)], ptb, pn_, pqt, pPBs)
            cs = slice(ptb * 512, (ptb + 1) * 512)
            P.dma('pool', [(mixT[row0 + pn_ * 256:row0 + (pn_ + 1) * 256, cs].rearrange("(g d) t -> d g t", d=64),
                            Os[(ptb, pn_)].b[:])], reads=[Os[(ptb, pn_)]])


        def phase_scan(nh, dv, C, kdirs, gcol0, row0):
            nch = T // C
            ncb = 512 // C
            wv = 512 // ncb
            nr = dv // wv
            nvh = dv // 128
            mcol = 0 if C == 64 else 256
            rst = ON.rst64 if C == 64 else ON.rst128
            QT = [[P.buf('QT%d_%d' % (d, t_), b=([128, 512], BF16)) for t_ in range(NBLK)] for d in range(2)]
            AT = [[P.buf('AT%d_%d' % (d, t_), b=([128, 4, 128], BF16)) for t_ in range(NBLK)] for d in range(2)]
            pstA = [P.wrap('pstA', p=pst.p), P.wrap('pstB', p=pst.p)]
            SIN = [P.buf('SIN%d' % d, b=([128, nch, dv], BF16)) for d in range(2)]
            DS = P.buf('DS', f=([128, dv, nch], F32))
            DA = [P.buf('DA%d' % d, f=([128, nch], F32)) for d in range(2)]
            DR = P.buf('DR', f=([128, 32, nch], F32))
            SF = P.ring(2, 'SF', f=([128, 32, nch], F32))
            ld = P.ring(3, 'sl', lf=([128, 512], F32), k=([128, 512], BF16), q=([128, 512], BF16),
                        v=([128, 4, dv], BF16))
            w1 = P.ring(3, 'sw1', pre=([128, 512], F32), b=([128, 512], F32))
            w2 = P.ring(3, 'sw2', eb=([128, 512], F32), enb=([128, 512], F32), dl=([128, 8], F32))
            w3 = P.ring(3, 'sw3', kt=([128, 512], BF16), kh=([128, 512], BF16), khT=([128, 2, 4, 128], BF16))
            l3 = P.ring(2, 'sl3', v=([128, 4, dv], BF16), g=([128, nvh, 512], BF16))
            w4 = P.ring(2, 'sw4', sq=([128, nvh, 512], BF16), r=([128, 512], F32))
            w5 = P.ring(2, 'sw5', b=([128, nvh, 512], BF16))
            itc = [0]

            def stage_a(h, d, tb):
                hr = slice(h * 128, (h + 1) * 128)
                kd = d if kdirs else 0
                cs = slice(tb * 512, (tb + 1) * 512)
                L = ld.next()
                P.dma('sp', [
                    (L.lf[:], s_lf[d, hr, cs]), (L.k[:], s_k[kd, hr, cs]), (L.q[:], s_q[hr, cs]),
                    (L.v[:], s_v[cs, h * dv:(h + 1) * dv].rearrange("(j p) c -> p j c", p=128)),
                ], writes=[L])
                W1 = w1.next()
                pre3 = W1.pre[:].rearrange("p (c t) -> p c t", t=C)
                b3 = W1.b[:].rearrange("p (c t) -> p c t", t=C)
                if d == 0:
                    P.op('dve', lambda e, W1=W1, L=L: e.tensor_tensor_scan(
                        out=W1.pre[:], data0=rst[:], data1=L.lf[:], initial=0.0, op0=ALU.mult, op1=ALU.add),
                        reads=[L, ON], writes=[W1])
                    bsrc = W1.pre
                    edge = pre3[:, :, C - 1:C]
                else:
                    P.op('dve', lambda e, W1=W1, L=L: e.tensor_tensor_scan(
                        out=W1.b[:, ::-1], data0=rst[:], data1=L.lf[:, ::-1], initial=0.0, op0=ALU.mult, op1=ALU.add),
                        reads=[L, ON], writes=[W1])
                    bsrc = W1.b
                    edge = b3[:, :, 0:1]
                W2 = w2.next()
                act(P, W2, W2.eb[:], W1, bsrc[:], AF.Exp)
                act(P, W2, W2.dl[:, 0:ncb], W1, edge.rearrange("p c o -> p (c o)"), AF.Exp)
                btmp = W1.b if d == 0 else W1.pre
                ts(P, 'dve', W1, btmp[:], W1, bsrc[:], -80.0, None, ALU.max)
                act(P, W2, W2.enb[:], W1, btmp[:], AF.Exp, scale=-1.0)
                if d == 0:
                    j0 = tb * ncb
                    act(P, DA[d], DA[d].f[:, j0:j0 + ncb], W1, edge.rearrange("p c o -> p (c o)"), AF.Exp)
                else:
                    j0 = nch - (tb + 1) * ncb
                    act(P, DA[d], DA[d].f[:, j0:j0 + ncb][:, ::-1], W1, edge.rearrange("p c o -> p (c o)"), AF.Exp)
                tt(P, 'dve', QT[d][tb], QT[d][tb].b[:], L, L.q[:], W2, W2.eb[:], ALU.mult)
                W3 = w3.next()
                tt(P, 'dve', W3, W3.kt[:], L, L.k[:], W2, W2.enb[:], ALU.mult)
                tt(P, 'dve', W3, W3.kh[:].rearrange("p (c t) -> p c t", t=C), W3, W3.kt[:].rearrange("p (c t) -> p c t", t=C),
                   W2, W2.dl[:, 0:ncb].unsqueeze(2).to_broadcast([128, ncb, C]), ALU.mult)
                it = itc[0]
                itc[0] += 1
                return (d, tb, L, W3, it)

            def stage_b(ctx):
                d, tb, L, W3, it = ctx
                pa = psb[it % 2]
                for i4 in range(4):
                    ss = slice(i4 * 128, (i4 + 1) * 128)
                    mm(P, pa, pa.p[:, ss], W3, W3.kt[:, ss], QT[d][tb], QT[d][tb].b[:, ss])
                pq = pstA[it % 2]
                po_ = (it % 2) * 512
                for i4 in range(4):
                    P.op('pe', lambda e, W3=W3, i4=i4, po_=po_: e.transpose(pst.p[:, po_ + i4 * 128:po_ + (i4 + 1) * 128],
                                                                       W3.kh[:, i4 * 128:(i4 + 1) * 128], MK.b[:, 512:640]),
                         reads=[W3, MK], writes=[pq])
                mk = MK.b[:, mcol + d * 128:mcol + (d + 1) * 128]
                tt(P, 'dve', AT[d][tb], AT[d][tb].b[:], pa, pa.p[:].rearrange("p (i t) -> p i t", i=4),
                   MK, mk.unsqueeze(1).to_broadcast([128, 4, 128]), ALU.mult)
                if C == 64:
                    for half in range(2):
                        act(P, W3, W3.khT[:, half, :, :], pq, pst.p[:, po_:po_ + 512].rearrange("p (i t) -> p i t", i=4),
                            AF.Copy, scale=MK.f[:, 640 + half:641 + half], extra=[MK])
                else:
                    act(P, W3, W3.khT[:, 0, :, :], pq, pst.p[:, po_:po_ + 512].rearrange("p (i t) -> p i t", i=4), AF.Copy)
                for r in range(nr):
                    pd = psb[2 + (it * nr + r) % 2]
                    for cl in range(ncb):
                        i4 = (cl * C) // 128
                        p0 = (cl * C) % 128
                        slot = cl if d == 0 else ncb - 1 - cl
                        mm(P, pd, pd.p[:, slot * wv:(slot + 1) * wv], W3, W3.khT[:, p0 // 64, i4, :],
                           L, L.v[:, i4, r * wv:(r + 1) * wv])
                    j0 = tb * ncb if d == 0 else nch - (tb + 1) * ncb
                    if r % 2 == 0:
                        cp(P, 'dve', DS, DS.f[:, r * wv:(r + 1) * wv, j0:j0 + ncb],
                           pd, pd.p[:].rearrange("p (c v) -> p v c", v=wv))
                    else:
                        act(P, DS, DS.f[:, r * wv:(r + 1) * wv, j0:j0 + ncb],
                            pd, pd.p[:].rearrange("p (c v) -> p v c", v=wv), AF.Copy)

            def pass2(d):
                act(P, DR, DR.f[:], DA[d], DA[d].f[:].unsqueeze(1).to_broadcast([128, 32, nch]), AF.Copy)
                mset(P, 'pool', DR, DR.f[:, :, 0:1], 0.0)
                mset(P, 'pool', SIN[d], SIN[d].b[:, 0, :], 0.0)
                for v0 in range(0, dv, 32):
                    S = SF.next()
                    P.op('dve', lambda e, S=S, v0=v0: e.tensor_tensor_scan(
                        out=S.f[:].rearrange("p v c -> p (v c)"), data0=DR.f[:].rearrange("p v c -> p (v c)"),
                        data1=DS.f[:, v0:v0 + 32, :].rearrange("p v c -> p (v c)"), initial=0.0,
                        op0=ALU.mult, op1=ALU.add), reads=[DR, DS], writes=[S])
                    act(P, SIN[d], SIN[d].b[:, 1:nch, v0:v0 + 32],
                        S, S.f[:, :, 0:nch - 1].rearrange("p v c -> p c v"), AF.Copy)

            p3 = [0]

            def pass3_load(h, tb):
                cs = slice(tb * 512, (tb + 1) * 512)
                L3 = l3.next()
                P.dma('sp', [
                    (L3.v[:], s_v[cs, h * dv:(h + 1) * dv].rearrange("(j p) c -> p j c", p=128)),
                    (L3.g[:], s_g[h * dv:(h + 1) * dv, cs].rearrange("(a p) t -> p a t", p=128)),
                ], writes=[L3])
                return L3

            def pass3(h, tb, L3):
                cs = slice(tb * 512, (tb + 1) * 512)
                W4 = w4.next()
                pos = []
                for vh in range(nvh):
                    po = psb[4 + vh] if nvh == 2 else psb[4 + p3[0] % 2]
                    pos.append(po)
                    vs = slice(vh * 128, (vh + 1) * 128)
                    for i4 in range(4):
                        ti = tb * 4 + i4
                        ts_ = slice(i4 * 128, (i4 + 1) * 128)
                        mm(P, po, po.p[:, ts_], L3, L3.v[:, i4, vs], AT[0][tb], AT[0][tb].b[:, i4, :], start=True, stop=False)
                        mm(P, po, po.p[:, ts_], L3, L3.v[:, i4, vs], AT[1][tb], AT[1][tb].b[:, i4, :], start=False, stop=False)
                        nci = 128 // C
                        for ci in range(nci):
                            cg = ti * nci + ci
                            tcs = slice(i4 * 128 + ci * C, i4 * 128 + (ci + 1) * C)
                            mm(P, po, po.p[:, tcs], SIN[0], SIN[0].b[:, cg, vs], QT[0][tb], QT[0][tb].b[:, tcs],
                               start=False, stop=False)
                            mm(P, po, po.p[:, tcs], SIN[1], SIN[1].b[:, nch - 1 - cg, vs], QT[1][tb], QT[1][tb].b[:, tcs],
                               start=False, stop=(ci == nci - 1))
                    act(P, W4, W4.sq[:, vh, :], po, po.p[:], AF.Square)
                p3[0] += 1
                pn = psb[6]
                for vh in range(nvh):
                    mm(P, pn, pn.p[:], ON, ON.b[:], W4, W4.sq[:, vh, :], start=(vh == 0), stop=(vh == nvh - 1))
                act(P, W4, W4.r[:], pn, pn.p[:], AF.Ln, scale=1.0 / dv, bias=EPS)
                act(P, W4, W4.r[:], W4, W4.r[:], AF.Exp, scale=-0.5)
                O = w5.next()
                for vh in range(nvh):
                    gc = gcol0 + h * nvh + vh
                    stt(P, 'dve', O, O.b[:, vh, :], pos[vh], pos[vh].p[:], CO.c[:, gc:gc + 1], W4, W4.r[:],
                        ALU.mult, ALU.mult, extra=[CO])
                    tt(P, 'pool', O, O.b[:, vh, :], O, O.b[:, vh, :], L3, L3.g[:, vh, :], ALU.mult)
                P.dma('sp', [(mixT[row0 + h * dv:row0 + (h + 1) * dv, cs].rearrange("(a p) t -> p a t", p=128),
                              O.b[:])], reads=[O])

            for h in range(nh):
                iters = [(d, tb) for d in range(2) for tb in range(NBLK)]
                prev = None
                for (d, tb) in iters:
                    cur = stage_a(h, d, tb)
                    if prev is not None:
                        stage_b(prev)
                        if prev[0] == 0 and prev[1] == NBLK - 1:
                            pass2(0)
                    prev = cur
                stage_b(prev)
                pass2(1)
                nxt = pass3_load(h, 0)
                for tb in range(NBLK):
                    cur3 = nxt
                    if tb + 1 < NBLK:
                        nxt = pass3_load(h, tb + 1)
                    pass3(h, tb, cur3)


        stop = cfg.stop_after
        for l in range(cfg.n_layers):
            lastl = (l == cfg.n_layers - 1)
            HT = P.buf('HT', b=([128, 8, T], BF16))
            mark = P.sb_off
            phase_op1(l, HT)
            P.end_phase(keep=mark)
            if lastl and stop == 'op1':
                break
            phase_p2(l, HT)
            P.end_phase()
            if lastl and stop == 'p2':
                break
            i2 = l // 2
            KV = phase_kv(l)
            if l % 2 == 0:
                phase_mem(KV, nA * 64 + nB * 128)
                P.end_phase()
                if lastl and stop == 'mem':
                    break
                phase_scan(nB, 128, 64, True, cc['hg'] + i2 * nB, nA * 64)
                P.end_phase()
                if lastl and stop == 'scan0':
                    break
                phase_attn(i2, 0)
                P.end_phase()
            else:
                phase_mem(KV, nC * 256)
                P.end_phase()
                if lastl and stop == 'mem':
                    break
                phase_scan(nC, 256, 128, False, cc['gl'] + i2 * nC * 2, 0)
                P.end_phase()
        if stop is None:
            phase_op1(cfg.n_layers, None)
            P.end_phase()
        P.emit()
        print('ops recorded:', P.nops, {e: len(s) for e, s in P.streams.items()})
    return nc


def core_inputs(cfg, b, heads, inp):
    f32 = np.float32
    hA, hKV, hB, hM, hC = heads['A'], heads['KV'], heads['B'], heads['M'], heads['C']
    out = {}
    out['xT'] = np.ascontiguousarray(inp['x'][b].T)
    out['memT'] = np.ascontiguousarray(inp['mem'][b].T)

    def cols(base, hs, w):
        return np.concatenate([np.arange(base + h * w, base + (h + 1) * w) for h in hs])

    def padded(groups):
        sel = []
        for g in groups:
            sel.append(g)
            pad = (-len(g)) % 128
            if pad:
                sel.append(-np.ones(pad, dtype=np.int64))
        return np.concatenate(sel)

    def chunkify(w, sel):
        L, K = w.shape[0], w.shape[1]
        wp = np.zeros((L, K, len(sel)), f32)
        ok = sel >= 0
        wp[:, :, ok] = w[:, :, sel[ok]]
        nk, nch = K // 128, len(sel) // 128
        return np.ascontiguousarray(wp.reshape(L, nk, 128, nch, 128).transpose(0, 3, 2, 1, 4)).reshape(L, nch, 128, nk * 128)
    EV = np.cumsum([0, 512, 128, 128, 512, 512, 512, 512, 512, 512, 512, 512])
    names = ['qA', 'kA', 'vA', 'gA', 'qB', 'zf', 'zb', 'iB', 'gB', 'qM', 'gM']
    eo = dict(zip(names, EV[:-1]))
    sel_e = padded([
        cols(eo['qA'], hA, 64), cols(eo['kA'], hKV, 64), cols(eo['gA'], hA, 64),
        cols(eo['qB'], hB, 128), cols(eo['gB'], hB, 128), cols(eo['qM'], hM, 128), cols(eo['gM'], hM, 128),
        cols(eo['zf'], hB, 128), cols(eo['zb'], hB, 128), cols(eo['vA'], hKV, 64), cols(eo['iB'], hB, 128)])
    assert len(sel_e) == cfg.nce, (len(sel_e), cfg.nce)
    out['w_in_e'] = chunkify(inp['w_in_even'], sel_e)
    OD = np.cumsum([0, 512, 512, 1024, 1024, 16, 16, 512, 512])
    oo = dict(zip(['qC', 'kC', 'vC', 'gC', 'rf', 'rb', 'qM', 'gM'], OD[:-1]))
    sel_o = padded([
        cols(oo['qC'], hC, 128), cols(oo['kC'], hC, 128), cols(oo['gC'], hC, 256),
        cols(oo['qM'], hM, 128), cols(oo['gM'], hM, 128),
        np.arange(oo['rf'], oo['rf'] + 16), np.arange(oo['rb'], oo['rb'] + 16), cols(oo['vC'], hC, 256)])
    assert len(sel_o) == cfg.nco, (len(sel_o), cfg.nco)
    out['w_in_o'] = chunkify(inp['w_in_odd'], sel_o)
    rows_e = np.concatenate([cols(0, hA, 64), cols(512, hB, 128), cols(1024, hM, 128)])
    rows_o = np.concatenate([cols(0, hC, 256), cols(1024, hM, 128)])
    wo = np.zeros((4, cfg.mixr, D), f32)
    for l in range(4):
        if l % 2 == 0:
            wo[l] = inp['w_out_even'][l // 2][rows_e]
        else:
            wo[l] = inp['w_out_odd'][l // 2][rows_o]
    out['w_out'] = chunkify(wo, np.arange(D))
    kvc = np.concatenate([cols(0, hM, 128), cols(512, hM, 128)])
    out['w_kv'] = chunkify(inp['w_mem_kv'], kvc)
    wu = inp['w_gate_up'][:, :, :, cols(0, hC, 128)]
    out['w_up'] = np.ascontiguousarray(wu.transpose(2, 0, 1, 3).reshape(16, -1))
    cc = cfg.cc
    C = np.zeros((128, cfg.ncc), f32)
    for l in range(4):
        g = inp['norm_even'][l // 2] if l % 2 == 0 else inp['norm_odd'][l // 2]
        C[:, cc['norm'] + l * 8:cc['norm'] + (l + 1) * 8] = g.reshape(8, 128).T
    C[:, cc['fin']:cc['fin'] + 8] = inp['final_norm'].reshape(8, 128).T
    C[:, cc['mem']:cc['mem'] + 8] = inp['mem_norm'].reshape(8, 128).T
    nB, nC, nA = cfg.nB, cfg.nC, cfg.nA
    for i in range(2):
        for hi, h in enumerate(hB):
            C[:, cc['hg'] + i * nB + hi] = inp['hgrn_norm'][i, h * 128:(h + 1) * 128]
        for hi, h in enumerate(hC):
            for vh in range(2):
                C[:, cc['gl'] + i * nC * 2 + hi * 2 + vh] = inp['gla_norm'][i, h * 256 + vh * 128:h * 256 + (vh + 1) * 128]
        for d in range(2):
            for hi, h in enumerate(hB):
                C[:, cc['lbp'] + i * 2 * nB + d * nB + hi] = inp['lb_param'][i, d, h * 128:(h + 1) * 128]
            for hi, h in enumerate(hC):
                C[:, cc['bg'] + i * 2 * nC + d * nC + hi] = inp['b_gate'][i, d, h * 128:(h + 1) * 128]
        for hi, h in enumerate(hA):
            C[:, cc['sink'] + i * nA + hi] = inp['sink'][i, h]
    out['consts'] = C
    out['etab'] = np.ascontiguousarray(alibi_table(hA).reshape(128, -1))
    out['masks'] = scan_masks()
    return out


_CACHE = {}


def kernel(**inputs):
    inp = {k: np.asarray(v) for k, v in inputs.items()}
    cfg = Cfg(8, 4, 4, 4)
    heads = {'A': list(range(8)), 'KV': [0, 1], 'B': list(range(4)), 'M': list(range(4)), 'C': list(range(4))}
    if 'nc' not in _CACHE:
        _CACHE['nc'] = build(cfg)
    nc = _CACHE['nc']
    in_maps = [core_inputs(cfg, c % 4, heads, inp) for c in range(N_CORES)]
    res = run_bass_kernel_spmd(nc, in_maps, core_ids=list(range(N_CORES)))
    out = np.stack([np.ascontiguousarray(res.results[b]['yT'].T) for b in range(4)], axis=0)
    return out.astype(np.float32)
```
